# Optimizing a Trainium2 kernel written in Bass

```python
import jax, jax.numpy as jnp
from jax import lax
import numpy as np

D_MODEL = 1024
BATCH = 8
SEQ = 2048
DEPTH = 2
DEC_BATCH = 8
DEC_SEQ = 16
PAST_LEN = 4096

CHUNK = 64
SUB = 16
Q_BLOCK = 128
N_MEM = 256
HG_HEADS = 4
HG_DIM = 128
HG_WIDTH = HG_HEADS * HG_DIM
FX_HEADS = 4
FX_DIM = 64
FX_WIDTH = FX_HEADS * FX_DIM
MEM_HEADS = 4
MEM_DIM = 64
MEM_WIDTH = MEM_HEADS * MEM_DIM
MIX_WIDTH = HG_WIDTH + FX_WIDTH + MEM_WIDTH
IN_WIDTH = 4 * HG_WIDTH + 4 * FX_WIDTH + FX_HEADS + 2 * MEM_WIDTH
DN_ALPHA = (2 * DEPTH) ** 0.25
DN_BETA = (8 * DEPTH) ** -0.25
LN_EPS = 1e-5
RMS_EPS = 1e-6

kernel_name = 'hgrn2_fox_memory_deepnorm_stream_step'


def _layernorm(x, g, b):
    xf = x.astype(jnp.float32)
    mu = jnp.mean(xf, -1, keepdims=True)
    var = jnp.mean(jnp.square(xf - mu), -1, keepdims=True)
    return ((xf - mu) * lax.rsqrt(var + LN_EPS) * g.astype(jnp.float32) + b.astype(jnp.float32)).astype(x.dtype)


def _rmsnorm(x, g):
    return x * lax.rsqrt(jnp.mean(jnp.square(x), -1, keepdims=True) + RMS_EPS) * g.astype(jnp.float32)


def _split_in(h):
    sizes = (HG_WIDTH,) * 4 + (FX_WIDTH,) * 4 + (FX_HEADS,) + (MEM_WIDTH,) * 2
    idx = np.cumsum(sizes)[:-1].tolist()
    return jnp.split(h, idx, axis=-1)


def _hgrn2_chunk(S, inp):
    q, k, v, g = inp
    B_, H_, C, K = q.shape
    NS = C // SUB
    G = jnp.cumsum(g, axis=2)
    o = jnp.einsum('bhtk,bhkv->bhtv', q * jnp.exp(G), S)
    Gs = G.reshape(B_, H_, NS, SUB, K)
    qs = q.reshape(B_, H_, NS, SUB, K)
    ks = k.reshape(B_, H_, NS, SUB, K)
    ref = jnp.concatenate([jnp.zeros_like(Gs[:, :, :1, 0]), Gs[:, :, :-1, -1]], axis=2)
    q_ref = qs * jnp.exp(Gs - ref[:, :, :, None])
    pos = jnp.arange(C)
    before = pos[None, :] < (jnp.arange(NS) * SUB)[:, None]
    k_ref = k[:, :, None] * jnp.exp(jnp.where(before[None, None, :, :, None],
                                              ref[:, :, :, None] - G[:, :, None], -jnp.inf))
    a_off = jnp.einsum('bhitk,bhisk->bhits', q_ref, k_ref)
    tri = pos[:SUB, None] >= pos[None, :SUB]
    diff = Gs[:, :, :, :, None, :] - Gs[:, :, :, None, :, :]
    dec = jnp.exp(jnp.where(tri[:, :, None], diff, -jnp.inf))
    a_diag = jnp.einsum('bhitk,bhisk,bhitsk->bhits', qs, ks, dec)
    a_diag = jnp.einsum('bhits,ij->bhitjs', a_diag, jnp.eye(NS, dtype=a_diag.dtype)).reshape(B_, H_, NS, SUB, C)
    a = (a_off + a_diag).reshape(B_, H_, C, C)
    o = o + jnp.einsum('bhts,bhsv->bhtv', a, v)
    G_last = G[:, :, -1]
    k_end = k * jnp.exp(G_last[:, :, None] - G)
    S = jnp.exp(G_last)[..., None] * S + jnp.einsum('bhsk,bhsv->bhkv', k_end, v)
    return S, o


def _hgrn2(S0, q, k, v, g):
    B_, T, H_, _ = q.shape
    pad = (-T) % CHUNK
    padt = lambda a: jnp.pad(a, ((0, 0), (0, pad), (0, 0), (0, 0)))
    n = (T + pad) // CHUNK
    chunks = lambda a: padt(a).reshape(B_, n, CHUNK, H_, a.shape[-1]).transpose(1, 0, 3, 2, 4)
    S, o = lax.scan(_hgrn2_chunk, S0, (chunks(q), chunks(k), chunks(v), chunks(g)))
    o = o.transpose(1, 0, 3, 2, 4).reshape(B_, n * CHUNK, H_, -1)[:, :T]
    return o, S


def _fox_block(q, k, v, Dq, Dk, qpos):
    s = jnp.einsum('bthd,bshd->bhts', q, k).astype(jnp.float32) * (FX_DIM ** -0.5)
    s = s + jnp.swapaxes(Dq, 1, 2)[..., :, None] - jnp.swapaxes(Dk, 1, 2)[..., None, :]
    mask = jnp.arange(k.shape[1])[None, :] <= qpos[:, None]
    p = jax.nn.softmax(jnp.where(mask, s, -jnp.inf), axis=-1)
    return jnp.einsum('bhts,bshd->bthd', p.astype(v.dtype), v)


def _fox_prompt(q, k, v, logf):
    B_, T, H_, Dh = q.shape
    nb = T // Q_BLOCK
    D = jnp.cumsum(logf, axis=1)
    qb = jnp.swapaxes(q.reshape(B_, nb, Q_BLOCK, H_, Dh), 0, 1)
    Db = jnp.swapaxes(D.reshape(B_, nb, Q_BLOCK, H_), 0, 1)
    pos = (jnp.arange(nb) * Q_BLOCK)[:, None] + jnp.arange(Q_BLOCK)[None, :]
    out = lax.map(lambda a: _fox_block(a[0], k, v, a[1], D, a[2]), (qb, Db, pos))
    return jnp.swapaxes(out, 0, 1).reshape(B_, T, H_, Dh)


def _fox_sample(q, k, v, logf, past_k, past_v, past_logf):
    P = past_k.shape[1]
    T = q.shape[1]
    k_all = jnp.concatenate([past_k.astype(k.dtype), k], axis=1)
    v_all = jnp.concatenate([past_v.astype(v.dtype), v], axis=1)
    D = jnp.cumsum(jnp.concatenate([past_logf.astype(jnp.float32), logf], axis=1), axis=1)
    return _fox_block(q, k_all, v_all, D[:, P:], D, P + jnp.arange(T))


def _cross(q, mk, mv):
    s = jnp.einsum('bthd,bmhd->bhtm', q, mk.astype(q.dtype)).astype(jnp.float32) * (MEM_DIM ** -0.5)
    p = jax.nn.softmax(s, axis=-1)
    return jnp.einsum('bhtm,bmhd->bthd', p.astype(q.dtype), mv.astype(q.dtype))


def _layer(x, w_in, b_forget, lb, hg_g, w_out, ln_g, ln_b, hg_state, mem_k, mem_v,
           past_k=None, past_v=None, past_logf=None):
    f32 = jnp.float32
    B_, T, _ = x.shape
    hq, hf, hi, hgate, fq, fk, fv, fgate, fflog, mq, mgate = _split_in(x @ w_in)
    hd = lambda a, n: a.reshape(B_, T, n, -1)
    forget = lb + (1.0 - lb) * jax.nn.sigmoid(hf.astype(f32))
    o_hg, hg_new = _hgrn2(hg_state.astype(f32), hd(jax.nn.silu(hq.astype(f32)), HG_HEADS),
                          hd(1.0 - forget, HG_HEADS), hd(hi.astype(f32), HG_HEADS),
                          hd(jnp.log(forget), HG_HEADS))
    o_hg = _rmsnorm(o_hg, hg_g).reshape(B_, T, HG_WIDTH) * jax.nn.silu(hgate.astype(f32))
    q, k, v = hd(fq, FX_HEADS), hd(fk, FX_HEADS), hd(fv, FX_HEADS)
    logf = jax.nn.log_sigmoid((fflog + b_forget).astype(f32))
    if past_k is None:
        o_fx = _fox_prompt(q, k, v, logf)
    else:
        o_fx = _fox_sample(q, k, v, logf, past_k, past_v, past_logf)
    o_fx = o_fx.reshape(B_, T, FX_WIDTH).astype(f32) * jax.nn.silu(fgate.astype(f32))
    o_mem = _cross(hd(mq, MEM_HEADS), mem_k, mem_v).reshape(B_, T, MEM_WIDTH).astype(f32) * jax.nn.silu(mgate.astype(f32))
    mix = jnp.concatenate([o_hg, o_fx, o_mem], axis=-1).astype(x.dtype)
    y = _layernorm(DN_ALPHA * x + mix @ w_out, ln_g, ln_b)
    return y, hg_new.astype(x.dtype), k, v, logf.astype(x.dtype)


def setup_inputs(seed: int = 0) -> dict:
    key = jax.random.key(seed)
    ks = jax.random.split(key, 16)
    nrm = lambda i, shape: jax.random.normal(ks[i], shape, jnp.float32)
    return {
        'x_prompt': nrm(0, (BATCH, SEQ, D_MODEL)),
        'x_sample': nrm(1, (DEC_BATCH, DEC_SEQ, D_MODEL)),
        'mem_prompt': nrm(2, (BATCH, N_MEM, D_MODEL)),
        'state_hgrn': 0.5 * nrm(3, (DEPTH, DEC_BATCH, HG_HEADS, HG_DIM, HG_DIM)),
        'cache_fox_k': nrm(4, (DEPTH, DEC_BATCH, PAST_LEN, FX_HEADS, FX_DIM)),
        'cache_fox_v': nrm(5, (DEPTH, DEC_BATCH, PAST_LEN, FX_HEADS, FX_DIM)),
        'cache_fox_logf': jax.nn.log_sigmoid(2.0 + nrm(6, (DEPTH, DEC_BATCH, PAST_LEN, FX_HEADS))),
        'cache_mem_k': nrm(7, (DEPTH, DEC_BATCH, N_MEM, MEM_HEADS, MEM_DIM)),
        'cache_mem_v': nrm(8, (DEPTH, DEC_BATCH, N_MEM, MEM_HEADS, MEM_DIM)),
        'w_in': nrm(9, (DEPTH, D_MODEL, IN_WIDTH)) * D_MODEL ** -0.5,
        'b_fox_forget': 1.0 + 0.5 * nrm(10, (DEPTH, FX_HEADS)),
        'hgrn_lower_bounds': 0.1 * nrm(11, (DEPTH, HG_WIDTH)),
        'hgrn_norm_g': 1.0 + 0.02 * nrm(12, (DEPTH, HG_DIM)),
        'w_mem_kv': nrm(13, (DEPTH, D_MODEL, 2 * MEM_WIDTH)) * D_MODEL ** -0.5,
        'w_out': nrm(14, (DEPTH, MIX_WIDTH, D_MODEL)) * (MIX_WIDTH ** -0.5 * DN_BETA),
        'ln_g': 1.0 + 0.02 * nrm(15, (DEPTH, D_MODEL)),
        'ln_b': 0.02 * jax.random.normal(jax.random.fold_in(key, 99), (DEPTH, D_MODEL), jnp.float32),
    }


def reference(x_prompt, x_sample, mem_prompt, state_hgrn, cache_fox_k, cache_fox_v, cache_fox_logf,
              cache_mem_k, cache_mem_v, w_in, b_fox_forget, hgrn_lower_bounds, hgrn_norm_g,
              w_mem_kv, w_out, ln_g, ln_b):
    sm = jax.nn.softmax(hgrn_lower_bounds.astype(jnp.float32), axis=0)
    lower = jnp.cumsum(sm, axis=0) - sm[0]
    Bp = x_prompt.shape[0]
    yp, ys = x_prompt, x_sample
    p_hg, p_k, p_v, p_lf, p_mk, p_mv = [], [], [], [], [], []
    s_hg, s_k, s_v, s_lf = [], [], [], []
    for l in range(DEPTH):
        w = (w_in[l], b_fox_forget[l], lower[l], hgrn_norm_g[l], w_out[l], ln_g[l], ln_b[l])
        mk, mv = jnp.split(mem_prompt @ w_mem_kv[l], 2, axis=-1)
        mk = mk.reshape(Bp, N_MEM, MEM_HEADS, MEM_DIM)
        mv = mv.reshape(Bp, N_MEM, MEM_HEADS, MEM_DIM)
        hg0 = jnp.zeros((Bp, HG_HEADS, HG_DIM, HG_DIM), jnp.float32)
        yp, hg, k, v, lf = _layer(yp, *w, hg0, mk, mv)
        p_hg.append(hg); p_k.append(k); p_v.append(v); p_lf.append(lf); p_mk.append(mk); p_mv.append(mv)
        ys, hg, k, v, lf = _layer(ys, *w, state_hgrn[l], cache_mem_k[l], cache_mem_v[l],
                                  cache_fox_k[l], cache_fox_v[l], cache_fox_logf[l])
        s_hg.append(hg); s_k.append(k); s_v.append(v); s_lf.append(lf)
    return (yp, ys, jnp.stack(p_hg), jnp.stack(p_k), jnp.stack(p_v), jnp.stack(p_lf),
            jnp.stack(p_mk), jnp.stack(p_mv), jnp.stack(s_hg), jnp.stack(s_k), jnp.stack(s_v),
            jnp.stack(s_lf))
```

```python
import os
import numpy as np
from contextlib import ExitStack
import concourse.bass as bass
import concourse.mybir as mybir
from concourse.bass_utils import run_bass_kernel_spmd

F32 = mybir.dt.float32
BF16 = mybir.dt.bfloat16
AF = mybir.ActivationFunctionType
ALU = mybir.AluOpType
AX = mybir.AxisListType

D = 1024
TP = 2048
TS = 16
TT = TP + TS
NB = 17
PAST = 4096
NPB = PAST // 128
DEPTH = 2
INW = 3588
HQ, HF, HI, HG, FQ, FK, FV, FG, FL, MQ, MG = 0, 512, 1024, 1536, 2048, 2304, 2560, 2816, 3072, 3076, 3332
TTILES = [(0, 512), (512, 512), (1024, 512), (1536, 512), (2048, 16)]
DN_ALPHA = (2 * DEPTH) ** 0.25
LN_EPS = 1e-5
RMS_EPS = 1e-6
N_DMA_SEMS = 56


def blk_rows(tb):
    return 128 if tb < 16 else 16


class Buf:
    __slots__ = ("name", "last_w", "rd", "rd_dma")

    def __init__(self, name):
        self.name = name
        self.last_w = None
        self.rd = {}
        self.rd_dma = []


class Op:
    __slots__ = ("eng", "meth", "args", "kwargs", "dma", "deps", "needed", "sem", "val", "idx", "prev_dma")


def flat(x):
    out = []
    for e in x:
        if isinstance(e, (list, tuple)):
            out.extend(flat(e))
        elif e is not None:
            out.append(e)
    return out


class Prog:
    def __init__(self):
        self.ops = []

    def add(self, eng, meth, args=(), kwargs=None, reads=(), writes=(), dma=False):
        op = Op()
        op.eng, op.meth, op.args, op.kwargs, op.dma = eng, meth, args, (kwargs or {}), dma
        op.idx = len(self.ops)
        op.needed = dma
        op.sem = None
        op.val = None
        op.prev_dma = None
        deps = set()
        reads = flat(reads)
        writes = flat(writes)
        ops = self.ops

        def consider(d, raw):
            if d is None:
                return
            o = ops[d]
            if (not raw) and (not dma) and (not o.dma) and o.eng == eng:
                return
            deps.add(d)

        for b in reads:
            consider(b.last_w, True)
        for b in writes:
            consider(b.last_w, False)
            for r in b.rd.values():
                consider(r, False)
            for r in b.rd_dma:
                consider(r, False)
        op.deps = deps
        for b in reads:
            if dma:
                b.rd_dma.append(op.idx)
            else:
                b.rd[eng] = op.idx
        for b in writes:
            b.last_w = op.idx
            b.rd = {}
            b.rd_dma = []
        ops.append(op)
        return op

    def emit(self, nc, es):
        ops = self.ops
        for op in ops:
            for d in op.deps:
                ops[d].needed = True
        engs = {"pe": nc.tensor, "act": nc.scalar, "dve": nc.vector, "pool": nc.gpsimd, "sp": nc.sync}
        esem = {k: es.enter_context(nc.semaphore("sem_" + k)) for k in ("pe", "act", "dve", "pool")}
        dsem = [es.enter_context(nc.semaphore("dsem%d" % i)) for i in range(N_DMA_SEMS)]
        cnt = {k: 0 for k in esem}
        dcount = [0] * N_DMA_SEMS
        dlast = [None] * N_DMA_SEMS
        NSP = 24
        ndma = {"sp": 0, "pool": 0}
        for op in ops:
            if op.dma:
                if op.eng == "sp":
                    s = ndma["sp"] % NSP
                else:
                    s = NSP + ndma["pool"] % (N_DMA_SEMS - NSP)
                ndma[op.eng] += 1
                dcount[s] += 16
                op.sem, op.val = dsem[s], dcount[s]
                op.prev_dma = dlast[s]
                dlast[s] = op.idx
            elif op.needed:
                cnt[op.eng] += 1
                op.sem, op.val = esem[op.eng], cnt[op.eng]
        per_eng = {k: [] for k in engs}
        for op in ops:
            per_eng[op.eng].append(op)
        block = es.enter_context(nc.Block())

        def run(ename):
            def body(e):
                waited = {}
                nwait = 0
                for op in per_eng[ename]:
                    deps = set(op.deps)
                    if op.prev_dma is not None:
                        deps.add(op.prev_dma)
                    need = {}
                    for d in deps:
                        o = ops[d]
                        key = id(o.sem)
                        if waited.get(key, 0) >= o.val:
                            continue
                        if key not in need or need[key][1] < o.val:
                            need[key] = (o.sem, o.val)
                    for key, (s, v) in need.items():
                        e.wait_ge(s, v)
                        waited[key] = v
                        nwait += 1
                    ins = getattr(e, op.meth)(*op.args, **op.kwargs)
                    if op.sem is not None:
                        ins.then_inc(op.sem, 16 if op.dma else 1)
                if ename == "sp":
                    for i in range(N_DMA_SEMS):
                        if dcount[i] > 0:
                            e.wait_ge(dsem[i], dcount[i])
                print("engine %s: %d ops, %d waits" % (ename, len(per_eng[ename]), nwait))
            return body

        block.tensor(run("pe"))
        block.scalar(run("act"))
        block.vector(run("dve"))
        block.gpsimd(run("pool"))
        block.sync(run("sp"))


def build_nc():
    nc = bass.Bass("TRN2", target_bir_lowering=False)
    es = ExitStack()
    P = Prog()

    def din(name, shape):
        return nc.dram_tensor(name, list(shape), F32, kind="ExternalInput").ap()

    def dout(name, shape):
        return nc.dram_tensor(name, list(shape), F32, kind="ExternalOutput").ap()

    x_all = din("x_all", [TT, D])
    mem_in = din("mem", [256, D])
    st_in = din("st", [DEPTH, 4, 128, 128])
    ck_in = din("ck", [DEPTH, PAST, 256])
    cv_in = din("cv", [DEPTH, PAST, 256])
    clf_in = din("clf", [DEPTH, PAST, 4])
    cmk_in = din("cmk", [DEPTH, 256, 256])
    cmv_in = din("cmv", [DEPTH, 256, 256])
    w_in = din("w_in", [DEPTH, D, INW])
    bff_in = din("bff", [DEPTH, 4])
    hlb_in = din("hlb", [DEPTH, 512])
    hng_in = din("hng", [DEPTH, 128])
    wkv_in = din("wkv", [DEPTH, D, 512])
    wout_in = din("wout", [DEPTH, D, D])
    lng_in = din("lng", [DEPTH, D])
    lnb_in = din("lnb", [DEPTH, D])

    y_out = dout("y_all", [TT, D])
    hst_out = dout("hst", [DEPTH, 2, 4, 128, 128])
    kv_out = dout("kv_all", [DEPTH, TT, 512])
    lf_out = dout("lf_all", [DEPTH, TT, 4])
    mkv_out = dout("mkv", [DEPTH, 256, 512])
    xres = nc.dram_tensor("xres_scratch", [TT, D], F32).ap()

    def sb(name, shape, dt=F32):
        return es.enter_context(nc.sbuf_tensor(name, list(shape), dt))

    fence_t = sb("fence_t", [128, 1])
    fence_s = sb("fence_s", [128, 1])
    c_fence = Buf("fence_src")
    P.add("pool", "memset", (fence_s[:], 0.0), {}, [], [c_fence])
    closed_bufs = []
    open_scopes = []

    class Scope:
        def __init__(self, tag):
            self.tag = tag
            self.es = ExitStack()
            self.bufs = []
            self.fence = None
            open_scopes.append(self)
            if closed_bufs:
                old = list(closed_bufs)
                del closed_bufs[:]
                self.fence = P.add("sp", "dma_start", (), dict(out=fence_t[0:1, 0:1], in_=fence_s[0:1, 0:1]), old + [c_fence], old, dma=True)

        def sb(self, name, shape, dt=F32):
            return self.es.enter_context(nc.sbuf_tensor(self.tag + name, list(shape), dt))

        def buf(self, name):
            b = Buf(self.tag + name)
            if self.fence is not None:
                b.last_w = self.fence.idx
            self.bufs.append(b)
            return b

        def close(self):
            closed_bufs.extend(self.bufs)
            self.es.close()
            open_scopes.remove(self)

    def psum(name, shape, dt=F32):
        return es.enter_context(nc.psum_tensor(name, list(shape), dt))

    def mm(out, lhsT, rhs, start, stop, reads, writes, skip=False):
        kw = dict(lhsT=lhsT, rhs=rhs, start=start, stop=stop)
        if skip:
            kw["skip_group_check"] = True
        P.add("pe", "matmul", (out,), kw, reads, writes)

    def tr(out, in_, ident, reads, writes):
        P.add("pe", "transpose", (out, in_, ident), {}, reads, writes)

    def act(out, in_, func, reads, writes, bias=None, scale=None):
        kw = dict(out=out, in_=in_, func=func)
        if bias is not None:
            kw["bias"] = bias
        if scale is not None:
            kw["scale"] = scale
        P.add("act", "activation", (), kw, reads, writes)

    def vop(eng, meth, reads, writes, *args, **kwargs):
        P.add(eng, meth, args, kwargs, reads, writes)

    def dma(eng, out, in_, reads, writes, **kw):
        P.add(eng, "dma_start", (), dict(out=out, in_=in_, **kw), reads, writes, dma=True)

    NBANK = 7
    psb = [psum("psb%d" % i, [128, 512]) for i in range(NBANK)]
    psb_b = [[Buf("psb%d" % i)] * 8 for i in range(NBANK)]
    pst = psum("pst", [128, 1024], BF16)
    pst_b = Buf("pst")

    identb = sb("identb", [128, 128], BF16)
    identf = sb("identf", [128, 128])
    trif = sb("trif", [128, 128])
    trib = sb("trib", [128, 128], BF16)
    onesf = sb("onesf", [128, 128])
    scanm = sb("scanm", [128, 512])
    c_ident, c_tri, c_ones, c_scanm = Buf("ident"), Buf("tri"), Buf("ones"), Buf("scanm")

    vop("pool", "memset", [], [c_ident], identf[:], 0.0)
    vop("pool", "affine_select", [c_ident], [c_ident], out=identf[:], in_=identf[:], pattern=[[-1, 128]],
        compare_op=ALU.not_equal, fill=1.0, base=0, channel_multiplier=1)
    vop("pool", "tensor_copy", [c_ident], [c_ident], out=identb[:], in_=identf[:])
    vop("pool", "memset", [], [c_tri], trif[:], 1.0)
    vop("pool", "affine_select", [c_tri], [c_tri], out=trif[:], in_=trif[:], pattern=[[1, 128]],
        compare_op=ALU.is_ge, fill=0.0, base=0, channel_multiplier=-1)
    vop("pool", "tensor_copy", [c_tri], [c_tri], out=trib[:], in_=trif[:])
    vop("pool", "memset", [], [c_ones], onesf[:], 1.0)
    mneg = sb("mneg", [128, 128])
    c_mneg = Buf("mneg")
    vop("pool", "memset", [], [c_mneg], mneg[:], 0.0)
    vop("pool", "affine_select", [c_mneg], [c_mneg], out=mneg[:], in_=mneg[:], pattern=[[1, 128]],
        compare_op=ALU.is_ge, fill=-1.0e30, base=0, channel_multiplier=-1)
    onesb = sb("onesb", [128, 128], BF16)
    vop("pool", "tensor_copy", [c_ones], [c_ones], out=onesb[:], in_=onesf[:])
    sutf = sb("sutf", [128, 128])
    vop("pool", "tensor_tensor", [c_tri, c_ident], [c_tri], out=sutf[:], in0=trif[:], in1=identf[:], op=ALU.subtract)
    vop("pool", "memset", [], [c_scanm], scanm[:], 1.0)
    vop("pool", "memset", [c_scanm], [c_scanm], scanm[:].rearrange("p (c k) -> p c k", k=64)[:, :, 0:1], 0.0)

    lbt = sb("lbt", [128, 4, 2])
    lbe = sb("lbe", [128, 4, 2])
    lbs = sb("lbs", [128, 4, 1])
    low = sb("low", [128, 4, 2])
    lnoml = sb("lnoml", [128, 4, 2])
    c_lb = Buf("lb")
    with nc.allow_non_contiguous_dma(reason="tiny parameter vectors"):
        pass
    for l_ in range(DEPTH):
        for h_ in range(4):
            dma("sp", lbt[:, h_, l_:l_ + 1], hlb_in[l_:l_ + 1, h_ * 128:(h_ + 1) * 128].rearrange("o p -> p o"), [], [c_lb],
                allow_slow_non_contiguous=True)
    act(lbe[:], lbt[:], AF.Exp, [c_lb], [c_lb])
    vop("dve", "tensor_tensor", [c_lb], [c_lb], out=lbs[:], in0=lbe[:, :, 0:1], in1=lbe[:, :, 1:2], op=ALU.add)
    vop("dve", "reciprocal", [c_lb], [c_lb], out=lbs[:], in_=lbs[:])
    vop("dve", "tensor_tensor", [c_lb], [c_lb], out=lbe[:], in0=lbe[:], in1=lbs[:].to_broadcast([128, 4, 2]), op=ALU.mult)
    vop("dve", "tensor_tensor", [c_lb], [c_lb], out=low[:, :, 0:1], in0=lbe[:, :, 0:1], in1=lbe[:, :, 0:1], op=ALU.subtract)
    vop("dve", "tensor_tensor", [c_lb], [c_lb], out=low[:, :, 1:2], in0=lbe[:, :, 0:1], in1=lbe[:, :, 1:2], op=ALU.add)
    vop("dve", "tensor_tensor", [c_lb], [c_lb], out=low[:, :, 1:2], in0=low[:, :, 1:2], in1=lbe[:, :, 0:1], op=ALU.subtract)
    act(lnoml[:], low[:], AF.Ln, [c_lb], [c_lb], bias=1.0, scale=-1.0)

    negb = sb("negb", [4, 2])
    c_negb = Buf("negb")
    dma("sp", negb[:], bff_in.rearrange("l h -> h l"), [], [c_negb], allow_slow_non_contiguous=True)
    vop("dve", "tensor_scalar", [c_negb], [c_negb], out=negb[:], in0=negb[:], scalar1=-1.0, scalar2=None, op0=ALU.mult)
    gcol = sb("gcol", [128, 2])
    c_gcol = Buf("gcol")
    dma("sp", gcol[:], hng_in.rearrange("l p -> p l"), [], [c_gcol], allow_slow_non_contiguous=True)
    vop("dve", "tensor_scalar", [c_gcol], [c_gcol], out=gcol[:], in0=gcol[:], scalar1=float(np.sqrt(128.0)), scalar2=None, op0=ALU.mult)

    xT = sb("xT", [128, 8, TT], BF16)
    xT_b = [Buf("xT%d" % i) for i in range(NB)]
    mix = sb("mix", [128, 8, TT], BF16)
    mix_b = [[Buf("mix%d_%d" % (c, j)) for j in range(5)] for c in range(8)]

    def xT_bufs(t0, n):
        return [xT_b[i] for i in range(t0 // 128, (t0 + n - 1) // 128 + 1)]

    NSLAB = 2
    slab_big = sb("slab", [128, 8, 1024], BF16)
    slab = [slab_big[:, :, i * 512:(i + 1) * 512] for i in range(NSLAB)]
    slab_b = [Buf("slab%d" % i) for i in range(NSLAB)]
    slab_ctr = [0]

    def load_slab(src2d, c0, n):
        i = slab_ctr[0] % NSLAB
        slab_ctr[0] += 1
        dma("pool", slab[i][:, :, 0:n], src2d.rearrange("(k p) c -> p k c", p=128)[:, :, c0:c0 + n], [], [slab_b[i]])
        return slab[i], slab_b[i]

    acc_ctr = [0]

    def next_acc():
        i = acc_ctr[0] % 3
        acc_ctr[0] += 1
        return psb[i], psb_b[i]

    xbt = [sb("xbt%d" % i, [128, D], BF16) for i in range(2)]
    xbt_b = [Buf("xbt%d" % i) for i in range(2)]

    tr_ctr = [0]

    def transpose_block_to(dst, dst_bufs, src_tile, src_buf, rows, col0):
        w = tr_ctr[0] % 2
        tr_ctr[0] += 1
        if w == 0:
            pt, pt_b = pst[:], [pst_b]
        else:
            pt, pt_b = psb[6][:].bitcast(BF16), psb_b[6]
        for k in range(8):
            tr(pt[:, k * 128:k * 128 + rows], src_tile[0:rows, k * 128:(k + 1) * 128], identb[0:rows, 0:rows],
               [src_buf, c_ident], pt_b)
        src3 = pt.rearrange("p (k t) -> p k t", k=8)[:, :, 0:rows]
        if w == 0:
            vop("dve", "tensor_copy", pt_b, dst_bufs, out=dst[:, :, col0:col0 + rows], in_=src3)
        else:
            act(dst[:, :, col0:col0 + rows], src3, AF.Copy, pt_b, dst_bufs)

    def build_xT_block(tb):
        r = blk_rows(tb)
        i = (tb // 2) % 2
        dma("pool", xbt[i][0:r, :], x_all[tb * 128:tb * 128 + r, :], [], [xbt_b[i]])
        transpose_block_to(xT, [xT_b[tb]], xbt[i], xbt_b[i], r, tb * 128)

    memT = sb("memT", [128, 8, 256], BF16)
    c_memT = Buf("memT")

    def build_memT():
        for mb in range(2):
            i = mb % 2
            dma("pool", xbt[i][:, :], mem_in[mb * 128:(mb + 1) * 128, :], [], [xbt_b[i]])
            transpose_block_to(memT, [c_memT], xbt[i], xbt_b[i], 128, mb * 128)

    one_c = sb("one_c", [128, 1])
    c_onec = Buf("onec")
    vop("pool", "memset", [], [c_onec], one_c[:], 1.0)

    ATT = {}

    def alloc_attn_tiles(sc):
        ATT["PT"] = [sc.sb("PT%d" % i, [128, 512], BF16) for i in range(5)]
        ATT["PT_b"] = [sc.buf("PT%d" % i) for i in range(5)]
        ATT["stg"] = [sc.sb("stg%d" % i, [128, 512]) for i in range(3)]
        ATT["stg_b"] = [sc.buf("stg%d" % i) for i in range(3)]
        ATT["rs"] = [sc.sb("rs%d" % i, [128, 512]) for i in range(2)]
        ATT["rs_b"] = [sc.buf("rs%d" % i) for i in range(2)]

    pt_ctr = [0]
    stg_ctr = [0]

    def next_stg():
        i = stg_ctr[0] % 3
        stg_ctr[0] += 1
        return ATT["stg"][i], ATT["stg_b"][i]

    def next_pt():
        i = pt_ctr[0] % 5
        pt_ctr[0] += 1
        return ATT["PT"][i], ATT["PT_b"][i]

    sc_ctr = [0]

    score_banks = [[3, 4]]

    def next_score():
        bl = score_banks[0]
        i = bl[sc_ctr[0] % len(bl)]
        sc_ctr[0] += 1
        return psb[i], psb_b[i]

    oacc_ctr = [0]

    def next_oacc():
        i = (5, 6)[oacc_ctr[0] % 2]
        oacc_ctr[0] += 1
        return psb[i], psb_b[i]

    rs_ctr = [0]

    def attn_finish(po, po_b, h, ci, t0, n, c0=0):
        pb = 64 * (h % 2)
        ob = 64 - pb
        i = rs_ctr[0] % 2
        rs_ctr[0] += 1
        j = t0 // 512
        rs_t, rs_b = ATT["rs"], ATT["rs_b"]
        act(rs_t[i][pb:pb + 64, 0:n], po[ob:ob + 64, c0:c0 + n], AF.Ln, po_b, [rs_b[i]])
        act(rs_t[i][pb:pb + 64, 0:n], rs_t[i][pb:pb + 64, 0:n], AF.Exp, [rs_b[i]], [rs_b[i]], scale=-1.0)
        vop("dve", "tensor_tensor", po_b + [rs_b[i]], [rs_b[i]], out=rs_t[i][pb:pb + 64, 0:n], in0=po[pb:pb + 64, c0:c0 + n],
            in1=rs_t[i][pb:pb + 64, 0:n], op=ALU.mult)
        vop("dve", "tensor_tensor", [rs_b[i], mix_b[ci][j]], [mix_b[ci][j]], out=mix[pb:pb + 64, ci, t0:t0 + n],
            in0=rs_t[i][pb:pb + 64, 0:n], in1=mix[pb:pb + 64, ci, t0:t0 + n], op=ALU.mult)

    def keep_warm(n, bank=1):
        for _ in range(n):
            P.add("pe", "matmul", (psb[bank][:, 0:128],), dict(lhsT=identb[:], rhs=identb[:], start=True, stop=True, skip_group_check=True),
                  [c_ident], psb_b[bank])

    NWARM = int(os.environ.get("KWARM", "0"))

    def run_pipeline(steps, lag=1):
        lag = int(os.environ.get("KLAG", lag))
        n = len(steps)
        for k in range(n + lag):
            if k < n and steps[k][0] is not None:
                steps[k][0]()
            if k >= lag and steps[k - lag][1] is not None:
                steps[k - lag][1]()

    xres_b = [Buf("xres%d" % tb) for tb in range(NB)]

    dbg_stop = os.environ.get("KDBG", "")

    class _Stop(Exception):
        pass

    def stage(name):
        if dbg_stop == name:
            raise _Stop()

    def layer_body(l):
        w_l = w_in[l]

        def proj_fm(sl, sl_b, c0, m, evac, tiles=None):
            for j, (t0, n) in enumerate(TTILES):
                if tiles is not None and j not in tiles:
                    continue
                ps, ps_bufs = next_acc()
                for k in range(8):
                    mm(ps[0:m, 0:n], sl[:, k, c0:c0 + m], xT[:, k, t0:t0 + n], k == 0, k == 7,
                       [sl_b] + xT_bufs(t0, n), ps_bufs)
                evac(ps, ps_bufs, j, t0, n)

        def gate_evac(ci):
            def f(ps, ps_bufs, j, t0, n):
                act(mix[:, ci, t0:t0 + n], ps[:, 0:n], AF.Silu, ps_bufs, [mix_b[ci][j]])
            return f

        sl, slb = load_slab(w_l, HG, 512)
        if l == 0:
            SX = Scope("X0_")
            xs = [SX.sb("xs%d" % i, [128, D]) for i in range(2)]
            xs_b = [SX.buf("xs%d" % i) for i in range(2)]
            xb2 = [SX.sb("xb2%d" % i, [128, D], BF16) for i in range(2)]
            xb2_b = [SX.buf("xb2%d" % i) for i in range(2)]
            for j in range(5):
                for tb in range(4 * j, min(4 * j + 4, NB)):
                    if tb % 2 == 0:
                        build_xT_block(tb)
                    else:
                        r = blk_rows(tb)
                        i = (tb // 2) % 2
                        dma("sp", xs[i][0:r, :], x_all[tb * 128:tb * 128 + r, :], [], [xs_b[i]])
                        vop("dve", "tensor_copy", [xs_b[i]], [xb2_b[i]], out=xb2[i][0:r, :], in_=xs[i][0:r, :])
                        transpose_block_to(xT, [xT_b[tb]], xb2[i], xb2_b[i], r, tb * 128)
                for c in range(4):
                    proj_fm(sl, slb, c * 128, 128, gate_evac(c), tiles=(j,))
            build_memT()
            SX.close()
        else:
            for c in range(4):
                proj_fm(sl, slb, c * 128, 128, gate_evac(c))
        sl, slb = load_slab(w_l, FG, 256)
        for c in range(2):
            proj_fm(sl, slb, c * 128, 128, gate_evac(4 + c))
        sl, slb = load_slab(w_l, MG, 256)
        for c in range(2):
            proj_fm(sl, slb, c * 128, 128, gate_evac(6 + c))

        sl_hq, slb_hq = load_slab(w_l, HQ, 512)
        SH = Scope("H%d_" % l)
        qt = [SH.sb("qt%d" % h, [128, TT], BF16) for h in range(4)]
        kt = [SH.sb("kt%d" % h, [128, TT], BF16) for h in range(4)]
        ktm = [SH.sb("ktm%d" % h, [128, NB, 128], BF16) for h in range(4)]
        qt_b = [SH.buf("qt%d" % h) for h in range(4)]
        kt_b = [SH.buf("kt%d" % h) for h in range(4)]
        ktm_b = [SH.buf("ktm%d" % h) for h in range(4)]
        eGl = SH.sb("eGl", [128, 4, 33])
        eGl_b = [SH.buf("eGl%d" % h) for h in range(4)]
        vhg = SH.sb("vhg", [128, NB, 512], BF16)
        vhg_b = [SH.buf("vhg%d" % tb) for tb in range(NB)]
        S32 = [SH.sb("S32_%d" % h, [128, 128]) for h in range(4)]
        Sbf = [SH.sb("Sbf_%d" % h, [128, 128], BF16) for h in range(4)]
        S32s = [SH.sb("S32s_%d" % h, [128, 128]) for h in range(4)]
        Sbfs = [SH.sb("Sbfs_%d" % h, [128, 128], BF16) for h in range(4)]
        S_b = [SH.buf("S%d" % h) for h in range(4)]
        Sbf_b = [SH.buf("Sbf%d" % h) for h in range(4)]
        Ss_b = [SH.buf("Ss%d" % h) for h in range(4)]
        Sbfs_b = [SH.buf("Sbfs%d" % h) for h in range(4)]
        AT = [SH.sb("AT%d" % h, [128, NB, 64], BF16) for h in range(4)]
        AT_b = [[SH.buf("AT%d_%d" % (h, j)) for j in range(5)] for h in range(4)]
        Tst = [[SH.sb("T%d_%d" % (h, i), [128, 128]) for i in range(2)] for h in range(4)]
        Tst_b = [[SH.buf("T%d_%d" % (h, i)) for i in range(2)] for h in range(4)]
        osq_ctr = [0]
        osb_ctr = [0]
        rms_pending = []
        rms_b_pending = []
        SHp = Scope("Hp%d_" % l)
        NHT = 3
        hu = [SHp.sb("hu%d" % i, [128, 512]) for i in range(NHT)]
        hl1 = [SHp.sb("hl1%d" % i, [128, 512]) for i in range(NHT)]
        hl2 = [SHp.sb("hl2%d" % i, [128, 512]) for i in range(NHT)]
        hG = [SHp.sb("hG%d" % i, [128, 512]) for i in range(NHT)]
        hu_b = [SHp.buf("hu%d" % i) for i in range(NHT)]
        hl1_b = [SHp.buf("hl1%d" % i) for i in range(NHT)]
        hl2_b = [SHp.buf("hl2%d" % i) for i in range(NHT)]
        hG_b = [SHp.buf("hG%d" % i) for i in range(NHT)]
        ht_ctr = [0]
        hf_pending = []

        sl, slb = sl_hq, slb_hq
        for h in range(4):
            def f(ps, ps_bufs, j, t0, n, h=h):
                act(qt[h][:, t0:t0 + n], ps[:, 0:n], AF.Silu, ps_bufs, [qt_b[h]])
            proj_fm(sl, slb, h * 128, 128, f)
        sl, slb = load_slab(w_l, HI, 512)
        for tb in range(NB):
            r = blk_rows(tb)
            ps, ps_bufs = next_acc()
            for k in range(8):
                mm(ps[0:r, :], xT[:, k, tb * 128:tb * 128 + r], sl[:, k, 0:512], k == 0, k == 7, [slb, xT_b[tb]], ps_bufs)
            vop("dve", "tensor_copy", ps_bufs, [vhg_b[tb]], out=vhg[0:r, tb, :], in_=ps[0:r, :])
        stage("Ha%d" % l)
        def ktm_transposes(h):
            for g0 in range(0, NB, 8):
                g1 = min(NB, g0 + 8)
                for tb in range(g0, g1):
                    r = blk_rows(tb)
                    tr(pst[0:r, (tb - g0) * 128:(tb - g0 + 1) * 128], kt[h][:, tb * 128:tb * 128 + r], identb[:], [kt_b[h], c_ident], [pst_b])
                if g1 - g0 == 8:
                    vop("dve", "tensor_copy", [pst_b], [ktm_b[h]], out=ktm[h][:, g0:g1, :], in_=pst[:].rearrange("p (b k) -> p b k", b=8))
                else:
                    vop("dve", "tensor_copy", [pst_b], [ktm_b[h]], out=ktm[h][0:16, g0, :], in_=pst[0:16, 0:128])

        sl, slb = load_slab(w_l, HF, 512)
        for h in range(4):
            def f(ps, ps_bufs, j, t0, n, h=h):
                i = ht_ctr[0] % NHT
                ht_ctr[0] += 1
                u_, l1_, l2_, G_ = hu[i][:, 0:n], hl1[i][:, 0:n], hl2[i][:, 0:n], hG[i][:, 0:n]

                def stage_b(i=i, n=n, l1_=l1_, l2_=l2_, G_=G_):
                    vop("pool", "tensor_tensor", [hl1_b[i], hl2_b[i]], [hl2_b[i]], out=l2_, in0=l2_, in1=l1_, op=ALU.subtract)
                    vop("dve", "tensor_tensor_scan", [hl2_b[i], c_scanm], [hG_b[i]], out=G_, data0=scanm[:, 0:n], data1=l2_,
                        initial=0.0, op0=ALU.mult, op1=ALU.add)
                    vop("pool", "tensor_tensor", [hl1_b[i], hG_b[i]], [hl1_b[i]], out=l1_, in0=l1_, in1=G_, op=ALU.add)

                def stage_c(i=i, h=h, j=j, t0=t0, n=n, u_=u_, l1_=l1_, l2_=l2_, G_=G_):
                    act(l2_, G_, AF.Exp, [hG_b[i]], [hl2_b[i]])
                    vop("dve", "tensor_tensor", [hl2_b[i], qt_b[h]], [qt_b[h]], out=qt[h][:, t0:t0 + n], in0=l2_, in1=qt[h][:, t0:t0 + n],
                        op=ALU.mult)
                    if n == 512:
                        vop("dve", "tensor_copy", [hl2_b[i]], [eGl_b[h]], out=eGl[:, h, j * 8:(j + 1) * 8],
                            in_=l2_.rearrange("p (c k) -> p c k", k=64)[:, :, 63])
                    else:
                        vop("dve", "tensor_copy", [hl2_b[i]], [eGl_b[h]], out=eGl[:, h, 32:33], in_=hl2[i][:, n - 1:n])
                    act(l1_, l1_, AF.Exp, [hl1_b[i], c_lb], [hl1_b[i]], bias=lnoml[:, h, l:l + 1], scale=-1.0)
                    vop("dve", "tensor_tensor", [hl1_b[i], hu_b[i]], [kt_b[h]], out=kt[h][:, t0:t0 + n], in0=l1_, in1=u_, op=ALU.mult)

                if len(hf_pending) >= 2:
                    hf_pending.pop(0)[1]()
                if hf_pending:
                    hf_pending[-1][0]()
                act(u_, ps[:, 0:n], AF.Exp, ps_bufs, [hu_b[i]], scale=-1.0)
                act(l1_, u_, AF.Ln, [hu_b[i]], [hl1_b[i]], bias=1.0, scale=1.0)
                act(l2_, u_, AF.Ln, [hu_b[i], c_lb], [hl2_b[i]], bias=1.0, scale=low[:, h, l:l + 1])
                hf_pending.append((stage_b, stage_c))
            proj_fm(sl, slb, h * 128, 128, f)
            if h >= 1:
                ktm_transposes(h - 1)
        if hf_pending:
            if len(hf_pending) >= 2:
                hf_pending[0][1]()
            hf_pending[-1][0]()
            hf_pending[-1][1]()
            del hf_pending[:]
        ktm_transposes(3)
        for h in range(4):
            dma("sp", S32s[h][:], st_in[l, h], [], [Ss_b[h]])
            dma("pool", Sbfs[h][:], st_in[l, h], [], [Sbfs_b[h]])
        SHp.close()
        SHl = Scope("Hl%d_" % l)
        osq = [SHl.sb("osq%d" % i, [128, 512]) for i in range(2)]
        osq_b = [SHl.buf("osq%d" % i) for i in range(2)]
        osqb = [SHl.sb("osqb%d" % i, [128, 512], BF16) for i in range(2)]
        osqb_b = [SHl.buf("osqb%d" % i) for i in range(2)]
        osb = [SHl.sb("osb%d" % i, [128, 512]) for i in range(4)]
        osb_b = [SHl.buf("osb%d" % i) for i in range(4)]
        stage("Hb%d" % l)
        for j, (t0, n) in enumerate(TTILES):
            C = 64 if j < 4 else 16
            nch = n // C
            for h in range(4):
                ps, ps_bufs = next_score()
                for cc in range(nch):
                    ta = t0 + cc * C
                    mm(ps[0:C, cc * 64:cc * 64 + C], kt[h][:, ta:ta + C], qt[h][:, ta:ta + C], True, True, [kt_b[h], qt_b[h]], ps_bufs)
                if j < 4:
                    for half in range(2):
                        vop("dve", "tensor_tensor", ps_bufs + [c_tri], [AT_b[h][j]], out=AT[h][half * 64:(half + 1) * 64, 4 * j:4 * j + 4, :],
                            in0=ps[0:64, :].rearrange("p (c k) -> p c k", k=64)[:, half:8:2, :],
                            in1=trif[0:64, 0:64].unsqueeze(1).to_broadcast([64, 4, 64]), op=ALU.mult)
                else:
                    vop("dve", "tensor_tensor", ps_bufs + [c_tri], [AT_b[h][j]], out=AT[h][0:16, 16, 0:16], in0=ps[0:16, 0:16],
                        in1=trif[0:16, 0:16], op=ALU.mult)
        stage("Hc%d" % l)
        obanks = [0, 1, 2, 5]
        lbanks = [6, 4, 3]

        def emit_L(c):
            if c > 32:
                return
            C_ = 64 if c < 32 else 16
            tb_ = c // 2 if c < 32 else 16
            pb_ = 64 * (c % 2) if c < 32 else 0
            lb_ = lbanks[c % 3]
            for h in range(4):
                mm(psb[lb_][:, h * 128:(h + 1) * 128], ktm[h][pb_:pb_ + C_, tb_, :], vhg[pb_:pb_ + C_, tb_, h * 128:(h + 1) * 128], True, True,
                   [ktm_b[h], vhg_b[tb_]], psb_b[lb_])

        emit_L(0)
        for j, (t0, n) in enumerate(TTILES):
            C = 64 if j < 4 else 16
            nch = n // C
            samp = (j == 4)
            for cc in range(nch):
                c = (t0 // 64) + cc if j < 4 else 32
                tb = c // 2 if j < 4 else 16
                pbk = 64 * (c % 2) if j < 4 else 0
                ta = t0 + cc * C
                first = (c == 0)
                for h in range(4):
                    po = psb[obanks[h]][:, cc * C:(cc + 1) * C]
                    mm(po, vhg[pbk:pbk + C, tb, h * 128:(h + 1) * 128], AT[h][pbk:pbk + C, tb, 0:C], True, first,
                       [vhg_b[tb], AT_b[h][j]], psb_b[obanks[h]])
                lbank = lbanks[c % 3]
                emit_L(c + 1)
                if not first:
                    for h in range(4):
                        po = psb[obanks[h]][:, cc * C:(cc + 1) * C]
                        if samp:
                            mm(po, Sbfs[h][:], qt[h][:, ta:ta + C], False, True, [Sbfs_b[h], qt_b[h]], psb_b[obanks[h]])
                        else:
                            mm(po, Sbf[h][:], qt[h][:, ta:ta + C], False, True, [Sbf_b[h], qt_b[h]], psb_b[obanks[h]])
                for h in range(4):
                    psS = psb[lbank][:, h * 128:(h + 1) * 128]
                    psS_b = psb_b[lbank]
                    eg = eGl[:, h, c:c + 1]
                    if samp:
                        vop("dve", "tensor_tensor", psS_b + [Ss_b[h]], [Tst_b[h][0]], out=Tst[h][0][:], in0=psS, in1=S32s[h][:], op=ALU.add)
                        vop("dve", "tensor_scalar", [Tst_b[h][0], eGl_b[h]], [Ss_b[h]], out=S32s[h][:], in0=Tst[h][0][:], scalar1=eg, scalar2=None,
                            op0=ALU.mult)
                        dma("sp", hst_out[l, 1, h], S32s[h][:], [Ss_b[h]], [])
                        continue
                    cur, prv = Tst[h][c % 2], Tst[h][(c + 1) % 2]
                    cur_b, prv_b = Tst_b[h][c % 2], Tst_b[h][(c + 1) % 2]
                    if first:
                        vop("dve", "tensor_copy", psS_b, [cur_b], out=cur[:], in_=psS)
                    else:
                        vop("dve", "scalar_tensor_tensor", [prv_b, eGl_b[h]] + psS_b, [cur_b], out=cur[:], in0=prv[:], scalar=eGl[:, h, c - 1:c],
                            in1=psS, op0=ALU.mult, op1=ALU.add)
                    if c == 31:
                        vop("dve", "tensor_scalar", [cur_b, eGl_b[h]], [S_b[h]], out=S32[h][:], in0=cur[:], scalar1=eg, scalar2=None, op0=ALU.mult)
                        dma("sp", hst_out[l, 0, h], S32[h][:], [S_b[h]], [])
                    else:
                        act(Sbf[h][:], cur[:], AF.Identity, [cur_b, eGl_b[h]], [Sbf_b[h]], scale=eg)
                if rms_b_pending:
                    rms_b_pending.pop(0)()
                if rms_pending and (cc % 2 == 1 or samp):
                    rms_b_pending.append(rms_pending.pop(0)())
            while rms_pending or rms_b_pending:
                if rms_b_pending:
                    rms_b_pending.pop(0)()
                if rms_pending:
                    rms_b_pending.append(rms_pending.pop(0)())
            for h in range(4):
                ob = obanks[h]
                oi = osb_ctr[0] % 4
                osb_ctr[0] += 1
                act(osb[oi][:, 0:n], psb[ob][:, 0:n], AF.Copy, psb_b[ob], [osb_b[oi]])

                def rms_a(h=h, oi=oi, j=j, t0=t0, n=n, stt={}):
                    qi_ = osq_ctr[0] % 2
                    osq_ctr[0] += 1
                    stt["qi"] = qi_
                    act(osqb[qi_][:, 0:n], osb[oi][:, 0:n], AF.Square, [osb_b[oi]], [osqb_b[qi_]])

                    def rms_b():
                        pss, pss_b = pst[:].bitcast(F32), [pst_b]
                        mm(pss[:, 0:n], onesb[:], osqb[qi_][:, 0:n], True, True, [c_ones, osqb_b[qi_]], pss_b)
                        act(osq[qi_][:, 0:n], pss[:, 0:n], AF.Ln, pss_b, [osq_b[qi_]], bias=float(128.0 * RMS_EPS), scale=1.0)
                        act(osq[qi_][:, 0:n], osq[qi_][:, 0:n], AF.Exp, [osq_b[qi_]], [osq_b[qi_]], scale=-0.5)
                        vop("dve", "tensor_tensor", [osb_b[oi], osq_b[qi_]], [osq_b[qi_]], out=osq[qi_][:, 0:n], in0=osb[oi][:, 0:n],
                            in1=osq[qi_][:, 0:n], op=ALU.mult)
                        vop("dve", "scalar_tensor_tensor", [osq_b[qi_], c_gcol, mix_b[h][j]], [mix_b[h][j]], out=mix[:, h, t0:t0 + n],
                            in0=osq[qi_][:, 0:n], scalar=gcol[:, l:l + 1], in1=mix[:, h, t0:t0 + n], op0=ALU.mult, op1=ALU.mult)
                    return rms_b
                rms = rms_a
                if os.environ.get("KRMS", "1") == "1":
                    rms_pending.append(rms)
                else:
                    rms()
        while rms_pending or rms_b_pending:
            if rms_b_pending:
                rms_b_pending.pop(0)()
            if rms_pending:
                rms_b_pending.append(rms_pending.pop(0)())
        SHl.close()
        SH.close()
        stage("H%d" % l)

        sl_mq, slb_mq = load_slab(w_l, MQ, 256)
        for mb in range(2):
            dma("pool", xbt[mb][:, 0:256], cmk_in[l, mb * 128:(mb + 1) * 128, :], [], [xbt_b[mb]])
        SM = Scope("M%d_" % l)
        alloc_attn_tiles(SM)
        MKT = SM.sb("MKT", [128, 2, 256], BF16)
        MKTs = SM.sb("MKTs", [128, 2, 256], BF16)
        MVa = SM.sb("MVa", [128, 2, 4, 128], BF16)
        MVas = SM.sb("MVas", [128, 2, 4, 128], BF16)
        c_MKT, c_MKTs, c_MVa, c_MVas = SM.buf("MKT"), SM.buf("MKTs"), SM.buf("MVa"), SM.buf("MVas")
        vop("dve", "memset", [], [c_MVa], MVa[:], 1.0)
        vop("dve", "memset", [], [c_MVas], MVas[:], 1.0)
        for mb in range(2):
            cmv4 = cmv_in[l, mb * 128:(mb + 1) * 128, :].rearrange("p (h d) -> p h d", h=4)
            dma("pool", MVas[:, mb, 0:4:2, 0:64], cmv4[:, 0:4:2, :], [], [c_MVas])
            dma("pool", MVas[:, mb, 1:4:2, 64:128], cmv4[:, 1:4:2, :], [], [c_MVas])
        mqT = SM.sb("mqT", [128, 2, TT], BF16)
        mqT_b = [SM.buf("mqT%d" % c) for c in range(2)]

        sl, slb = sl_mq, slb_mq
        for c in range(2):
            def f(ps, ps_bufs, j, t0, n, c=c):
                act(mqT[:, c, t0:t0 + n], ps[:, 0:n], AF.Identity, ps_bufs, [mqT_b[c]], scale=0.125)
            proj_fm(sl, slb, c * 128, 128, f)
        sl, slb = load_slab(wkv_in[l], 0, 512)
        for mb in range(2):
            ps, ps_bufs = next_acc()
            for k in range(8):
                mm(ps[:, :], memT[:, k, mb * 128:(mb + 1) * 128], sl[:, k, 0:512], k == 0, k == 7, [slb, c_memT], ps_bufs)
            st_, st_bf = next_stg()
            vop("dve", "tensor_copy", ps_bufs, [st_bf], out=st_[:], in_=ps[:, :])
            dma("sp", mkv_out[l, mb * 128:(mb + 1) * 128, :], st_[:], [st_bf], [])
            v4 = st_[:, 256:512].rearrange("p (h d) -> p h d", h=4)
            vop("pool", "tensor_copy", [st_bf], [c_MVa], out=MVa[:, mb, 0:4:2, 0:64], in_=v4[:, 0:4:2, :])
            vop("pool", "tensor_copy", [st_bf], [c_MVa], out=MVa[:, mb, 1:4:2, 64:128], in_=v4[:, 1:4:2, :])
        for c in range(2):
            ps, ps_bufs = next_acc()
            for k in range(8):
                mm(ps[:, 0:256], sl[:, k, c * 128:(c + 1) * 128], memT[:, k, :], k == 0, k == 7, [slb, c_memT], ps_bufs)
            vop("dve", "tensor_copy", ps_bufs, [c_MKT], out=MKT[:, c, :], in_=ps[:, 0:256])
        for mb in range(2):
            i = mb % 2
            for c in range(2):
                tr(pst[:, c * 128:(c + 1) * 128], xbt[i][:, c * 128:(c + 1) * 128], identb[:], [xbt_b[i], c_ident], [pst_b])
            vop("dve", "tensor_copy", [pst_b], [c_MKTs], out=MKTs[:, :, mb * 128:(mb + 1) * 128],
                in_=pst[:, 0:256].rearrange("p (c t) -> p c t", c=2))
        steps = []
        for h in range(4):
            c = h // 2
            pb = 64 * (h % 2)
            for j, (t0, n) in enumerate(TTILES):
                samp = (j == 4)
                kt_, kt_bf = (MKTs, c_MKTs) if samp else (MKT, c_MKT)
                mv_, mv_bf = (MVas, c_MVas) if samp else (MVa, c_MVa)
                st = {}
                for mb in range(2):
                    def front(st=st, mb=mb, kt_=kt_, kt_bf=kt_bf, c=c, pb=pb, t0=t0, n=n):
                        ps, ps_bufs = next_score()
                        mm(ps[:, 0:n], kt_[pb:pb + 64, c, mb * 128:(mb + 1) * 128], mqT[pb:pb + 64, c, t0:t0 + n], True, True,
                           [kt_bf, mqT_b[c]], ps_bufs)
                        pt_, pt_bf = next_pt()
                        act(pt_[:, 0:n], ps[:, 0:n], AF.Exp, ps_bufs, [pt_bf])
                        st[mb] = (pt_, pt_bf)

                    def back(st=st, mb=mb, mv_=mv_, mv_bf=mv_bf, h=h, c=c, t0=t0, n=n):
                        if mb == 0:
                            st["po"] = next_oacc()
                        po, po_b = st["po"]
                        pt_, pt_bf = st[mb]
                        mm(po[:, 0:n], mv_[:, mb, h, :], pt_[:, 0:n], mb == 0, mb == 1, [mv_bf, pt_bf], po_b)
                        if mb == 1:
                            attn_finish(po, po_b, h, 6 + c, t0, n)
                    steps.append((front, back))
        score_banks[0] = [3, 4, 1]
        run_pipeline(steps, lag=2)
        score_banks[0] = [3, 4]
        SM.close()
        stage("M%d" % l)

        sl_fq, slb_fq = load_slab(w_l, FQ, 512)
        SF = Scope("F%d_" % l)
        alloc_attn_tiles(SF)
        Qa = [SF.sb("Qa%d" % h, [128, TT], BF16) for h in range(4)]
        Ka = [SF.sb("Ka%d" % h, [128, TT], BF16) for h in range(4)]
        Qa_b = [SF.buf("Qa%d" % h) for h in range(4)]
        Qaug_b = [[SF.buf("Qaug%d_%d" % (h, r_)) for r_ in range(3)] for h in range(4)]
        Ka_b = [SF.buf("Ka%d" % h) for h in range(4)]
        Kaug_b = [SF.buf("Kaug%d" % h) for h in range(4)]
        Va = SF.sb("Va", [128, NB, 4, 128], BF16)
        Va_b, Va1_b = SF.buf("Va"), SF.buf("Va1")
        negD = SF.sb("negD", [128, 4, NB])
        negDp = SF.sb("negDp", [128, 4, NPB])
        c_negD, c_negDp = SF.buf("negD"), SF.buf("negDp")
        lf = SF.sb("lf", [4, TT])
        Dsp = SF.sb("Dsp", [4, 3, TT], BF16)
        clf_t = SF.sb("clf_t", [128, 32, 4])
        clf_c = SF.sb("clf_c", [128, 32, 4])
        Dtot = SF.sb("Dtot", [4, 1])
        c_lf, c_Dsp, c_Dpast, c_Dtot = SF.buf("lf"), SF.buf("Dsp"), SF.buf("Dpast"), SF.buf("Dtot")
        KTq = SF.sb("KTq", [128, 4, 1024], BF16)
        KTq_b, KTq1_b = SF.buf("KTq"), SF.buf("KTq1")
        Ve = SF.sb("Ve", [128, 16, 128], BF16)
        Vo = SF.sb("Vo", [128, 16, 128], BF16)
        Vq_b, Vq1_b = SF.buf("Vq"), SF.buf("Vq1")
        Kc = SF.sb("Kc", [128, 8, 256], BF16)
        Kc_b = SF.buf("Kc")
        for h in range(4):
            vop("dve", "memset", [], [Kaug_b[h]], Ka[h][64:67, :], 1.0)
        vop("dve", "memset", [], [KTq1_b], KTq[64:67, :, :], 1.0)
        vop("dve", "memset", [], [Vq1_b], Ve[:, :, 64:128], 1.0)
        vop("dve", "memset", [], [Vq1_b], Vo[:, :, 0:64], 1.0)
        vop("dve", "memset", [], [Va1_b], Va[:, :, 0:4:2, 64:128], 1.0)
        vop("dve", "memset", [], [Va1_b], Va[:, :, 1:4:2, 0:64], 1.0)
        ck3 = ck_in[l].rearrange("(p b) c -> p b c", b=32)
        cvv = cv_in[l].rearrange("(p b) (m two d) -> p (b m) two d", b=32, two=2, d=64)
        pos, pos_b = psb[0], psb_b[0]

        def load_quarter_dma(qi):
            dma("pool", Kc[:], ck3[:, qi * 8:(qi + 1) * 8, :], [], [Kc_b])
            dma("pool", Ve[:, :, 0:64], cvv[:, qi * 16:(qi + 1) * 16, 0, :], [], [Vq_b])
            dma("pool", Vo[:, :, 64:128], cvv[:, qi * 16:(qi + 1) * 16, 1, :], [], [Vq_b])

        def load_quarter_tr(qi):
            for c in range(2):
                for b8 in range(8):
                    tr(pst[:, b8 * 128:(b8 + 1) * 128], Kc[:, b8, c * 128:(c + 1) * 128], identb[:], [Kc_b, c_ident], [pst_b])
                vop("dve", "tensor_copy", [pst_b], [KTq_b], out=KTq[0:64, 2 * c, :], in_=pst[0:64, :])
                act(KTq[0:64, 2 * c + 1, :], pst[64:128, :], AF.Copy, [pst_b], [KTq_b])

        sq_state = {}

        def sample_front(qi):
            ps, ps_bufs = next_score()
            for h in range(4):
                for b8 in range(8):
                    cc = (h * 8 + b8) * 16
                    mm(ps[:, cc:cc + 16], KTq[0:67, h, b8 * 128:(b8 + 1) * 128], Qa[h][0:67, TP:TT], True, True,
                       [KTq_b, KTq1_b, Qa_b[h]] + Qaug_b[h], ps_bufs)
            st_, st_bf = next_stg()
            vop("dve", "tensor_tensor", ps_bufs + [c_negDp], [st_bf], out=st_[:].rearrange("p (h b q) -> p h b q", h=4, b=8),
                in0=ps[:, :].rearrange("p (h b q) -> p h b q", h=4, b=8),
                in1=negDp[:, :, qi * 8:(qi + 1) * 8].unsqueeze(3).to_broadcast([128, 4, 8, 16]), op=ALU.add)
            pt_, pt_bf = next_pt()
            act(pt_[:], st_[:], AF.Exp, [st_bf], [pt_bf])
            sq_state[qi] = (pt_, pt_bf)

        def sample_back(qi):
            pt_, pt_bf = sq_state[qi]
            for h in range(4):
                vt = Ve if h % 2 == 0 else Vo
                for b8 in range(8):
                    cc = (h * 8 + b8) * 16
                    mm(pos[:, h * 16:(h + 1) * 16], vt[:, b8 * 2 + h // 2, :], pt_[:, cc:cc + 16], qi == 0 and b8 == 0 and h == 0, False,
                       [Vq_b, Vq1_b, pt_bf], pos_b, skip=True)
            if qi + 1 < 4:
                load_quarter_dma(qi + 1)

        load_quarter_dma(0)
        dma("sp", clf_t[:], clf_in[l].rearrange("(p b) h -> p b h", b=32), [], [c_Dpast])

        sl, slb = sl_fq, slb_fq
        for c in range(2):
            def f(ps, ps_bufs, j, t0, n, c=c):
                act(Qa[2 * c][0:64, t0:t0 + n], ps[0:64, 0:n], AF.Identity, ps_bufs, [Qa_b[2 * c]], scale=0.125)
                act(Qa[2 * c + 1][0:64, t0:t0 + n], ps[64:128, 0:n], AF.Identity, ps_bufs, [Qa_b[2 * c + 1]], scale=0.125)
            proj_fm(sl, slb, c * 128, 128, f)
        for c in range(2):
            def f(ps, ps_bufs, j, t0, n, c=c):
                vop("dve", "tensor_copy", ps_bufs, [Ka_b[2 * c]], out=Ka[2 * c][0:64, t0:t0 + n], in_=ps[0:64, 0:n])
                vop("dve", "tensor_copy", ps_bufs, [Ka_b[2 * c + 1]], out=Ka[2 * c + 1][0:64, t0:t0 + n], in_=ps[64:128, 0:n])
            proj_fm(sl, slb, 256 + c * 128, 128, f)
        sl, slb = load_slab(w_l, FL, 4)

        def f(ps, ps_bufs, j, t0, n):
            act(lf[:, t0:t0 + n], ps[0:4, 0:n], AF.Exp, ps_bufs + [c_negb], [c_lf], bias=negb[:, l:l + 1], scale=-1.0)
        proj_fm(sl, slb, 0, 4, f)

        onebc = one_c[0:4, 0:1]
        stat_stages = []

        def s1():
            act(lf[:], lf[:], AF.Ln, [c_lf], [c_lf], bias=1.0, scale=1.0)
            vop("dve", "tensor_scalar", [c_lf], [c_lf], out=lf[:], in0=lf[:], scalar1=-1.0, scalar2=None, op0=ALU.mult)
            for h in range(4):
                vop("dve", "tensor_tensor_scan", [c_Dpast, c_onec], [c_Dpast], out=clf_c[:, :, h], data0=one_c[:, 0:1].to_broadcast([128, 32]),
                    data1=clf_t[:, :, h], initial=0.0, op0=ALU.mult, op1=ALU.add)
        stat_stages.append(s1)

        def s2():
            ps, ps_bufs = next_score()
            for tb in range(NB):
                r = blk_rows(tb)
                tr(ps[0:r, tb * 4:tb * 4 + 4], lf[0:4, tb * 128:tb * 128 + r], identf[0:4, 0:4], [c_lf, c_ident], ps_bufs)
            st_, st_bf = next_stg()
            vop("dve", "tensor_copy", ps_bufs, [st_bf], out=st_[:, 0:NB * 4], in_=ps[:, 0:NB * 4])
            dma("sp", lf_out[l, 0:TP, :].rearrange("(b p) h -> p b h", p=128), st_[:, 0:64].rearrange("p (b h) -> p b h", h=4), [st_bf], [])
            dma("sp", lf_out[l, TP:TT, :], st_[0:16, 64:68], [st_bf], [])
            ps, ps_bufs = next_score()
            mm(ps[:, 0:4], sutf[:], clf_c[:, 31, :], True, True, [c_tri, c_Dpast], ps_bufs)
            mm(ps[0:4, 8:9], clf_c[:, 31, :], onesf[:, 0:1], True, True, [c_ones, c_Dpast], ps_bufs)
            vop("dve", "tensor_copy", ps_bufs, [c_Dtot], out=Dtot[:], in_=ps[0:4, 8:9])
            vop("dve", "scalar_tensor_tensor", ps_bufs + [c_Dpast], [c_negDp], out=negDp[:].rearrange("p h b -> p b h"), in0=clf_c[:], scalar=-1.0,
                in1=ps[:, 0:4].unsqueeze(1).to_broadcast([128, 32, 4]), op0=ALU.mult, op1=ALU.subtract)
        stat_stages.append(s2)

        def s3():
            vop("dve", "tensor_tensor_scan", [c_lf, c_onec], [c_lf], out=lf[:, 0:TP], data0=onebc.to_broadcast([4, TP]),
                data1=lf[:, 0:TP], initial=0.0, op0=ALU.mult, op1=ALU.add)
            vop("dve", "tensor_tensor_scan", [c_lf, c_onec, c_Dtot], [c_lf], out=lf[:, TP:TT], data0=onebc.to_broadcast([4, TS]),
                data1=lf[:, TP:TT], initial=Dtot[:, 0:1], op0=ALU.mult, op1=ALU.add)
        stat_stages.append(s3)

        def s4():
            ps, ps_bufs = next_score()
            for tb in range(NB):
                r = blk_rows(tb)
                tr(ps[0:r, tb * 4:tb * 4 + 4], lf[0:4, tb * 128:tb * 128 + r], identf[0:4, 0:4], [c_lf, c_ident], ps_bufs)
            vop("dve", "tensor_scalar", ps_bufs, [c_negD], out=negD[:].rearrange("p h b -> p b h"),
                in0=ps[:, 0:NB * 4].rearrange("p (b h) -> p b h", h=4), scalar1=-1.0, scalar2=None, op0=ALU.mult)
        stat_stages.append(s4)

        def s5():
            vop("dve", "tensor_copy", [c_lf], [c_Dsp], out=Dsp[:, 0, :], in_=lf[:])
            vop("dve", "tensor_tensor", [c_lf, c_Dsp], [c_lf], out=lf[:], in0=lf[:], in1=Dsp[:, 0, :], op=ALU.subtract)
            vop("dve", "tensor_copy", [c_lf], [c_Dsp], out=Dsp[:, 1, :], in_=lf[:])
            vop("dve", "tensor_tensor", [c_lf, c_Dsp], [c_lf], out=lf[:], in0=lf[:], in1=Dsp[:, 1, :], op=ALU.subtract)
            vop("dve", "tensor_copy", [c_lf], [c_Dsp], out=Dsp[:, 2, :], in_=lf[:])
            for h in range(4):
                for r_ in range(3):
                    dma("sp", Qa[h][64 + r_:65 + r_, :], Dsp[h:h + 1, r_, :], [c_Dsp], [Qaug_b[h][r_]])
        stat_stages.append(s5)

        sl, slb = load_slab(w_l, FK, 512)
        for tb in range(NB):
            r = blk_rows(tb)
            ps, ps_bufs = next_acc()
            for k in range(8):
                mm(ps[0:r, :], xT[:, k, tb * 128:tb * 128 + r], sl[:, k, 0:512], k == 0, k == 7, [slb, xT_b[tb]], ps_bufs)
            st_, st_bf = next_stg()
            act(st_[0:r, :], ps[0:r, :], AF.Copy, ps_bufs, [st_bf])
            dma("sp", kv_out[l, tb * 128:tb * 128 + r, :], st_[0:r, :], [st_bf], [])
            v4 = st_[0:r, 256:512].rearrange("p (h d) -> p h d", h=4)
            vop("pool", "tensor_copy", [st_bf], [Va_b], out=Va[0:r, tb, 0:4:2, 0:64], in_=v4[:, 0:4:2, :])
            vop("pool", "tensor_copy", [st_bf], [Va_b], out=Va[0:r, tb, 1:4:2, 64:128], in_=v4[:, 1:4:2, :])
            if tb % 3 == 1 and stat_stages:
                stat_stages.pop(0)()
        while stat_stages:
            stat_stages.pop(0)()
        load_quarter_tr(0)
        dma("pool", slab_big[:], wout_in[l].rearrange("(k p) c -> p k c", p=128), [], [slab_b[0], slab_b[1]])
        stage("F1%d" % l)

        nk_state = {}

        def newkeys_front():
            ps, ps_bufs = next_score()
            for h in range(4):
                mm(ps[0:16, h * 16:(h + 1) * 16], Ka[h][0:67, TP:TT], Qa[h][0:67, TP:TT], True, True,
                   [Ka_b[h], Kaug_b[h], Qa_b[h]] + Qaug_b[h], ps_bufs)
            vop("dve", "tensor_tensor", ps_bufs + [c_mneg], ps_bufs, out=ps[0:16, 0:64].rearrange("p (h q) -> p h q", h=4),
                in0=ps[0:16, 0:64].rearrange("p (h q) -> p h q", h=4), in1=mneg[0:16, 0:16].unsqueeze(1).to_broadcast([16, 4, 16]), op=ALU.add)
            pt_, pt_bf = next_pt()
            for h in range(4):
                act(pt_[0:16, h * 16:(h + 1) * 16], ps[0:16, h * 16:(h + 1) * 16], AF.Exp, ps_bufs + [c_negD], [pt_bf],
                    bias=negD[0:16, h, 16:17], scale=1.0)
            nk_state["pt"] = (pt_, pt_bf)

        def newkeys_back():
            pt_, pt_bf = nk_state["pt"]
            for h in range(4):
                mm(pos[:, h * 16:(h + 1) * 16], Va[0:16, 16, h, :], pt_[0:16, h * 16:(h + 1) * 16], False, True, [Va_b, Va1_b, pt_bf], pos_b, skip=True)
            for h in range(4):
                attn_finish(pos, pos_b, h, 4 + h // 2, TP, TS, c0=h * 16)

        steps = []
        for h in range(4):
            for j in range(4):
                nkb = 4 * j + 4
                st = {}
                if j == 0 and h >= 1:
                    steps.append((lambda h=h: load_quarter_tr(h), None))
                if j == 1:
                    steps.append((lambda h=h: sample_front(h), lambda h=h: sample_back(h)))
                if j == 3 and h == 3:
                    steps.append((newkeys_front, newkeys_back))
                for kb in range(nkb):
                    q0 = max(512 * j, 128 * kb)
                    n = 512 * (j + 1) - q0

                    def front(st=st, h=h, j=j, kb=kb, q0=q0, n=n):
                        ps, ps_bufs = next_score()
                        mm(ps[:, 0:n], Ka[h][0:67, kb * 128:(kb + 1) * 128], Qa[h][0:67, q0:q0 + n], True, True,
                           [Ka_b[h], Kaug_b[h], Qa_b[h]] + Qaug_b[h], ps_bufs)
                        if kb >= 4 * j:
                            vop("dve", "tensor_tensor", ps_bufs + [c_mneg], ps_bufs, out=ps[:, 0:128], in0=ps[:, 0:128], in1=mneg[:], op=ALU.add)
                        pt_, pt_bf = next_pt()
                        act(pt_[:, 0:n], ps[:, 0:n], AF.Exp, ps_bufs + [c_negD], [pt_bf], bias=negD[:, h, kb:kb + 1], scale=1.0)
                        st[kb] = (pt_, pt_bf)

                    def back(st=st, h=h, j=j, kb=kb, q0=q0, n=n, nkb=nkb):
                        if kb == 0:
                            st["po"] = next_oacc()
                        po, po_b = st["po"]
                        pt_, pt_bf = st[kb]
                        mm(po[:, q0 - 512 * j:512], Va[:, kb, h, :], pt_[:, 0:n], kb == 0, kb == nkb - 1, [Va_b, Va1_b, pt_bf], po_b)
                        if kb == nkb - 1:
                            attn_finish(po, po_b, h, 4 + h // 2, 512 * j, 512)
                        keep_warm(NWARM)
                    steps.append((front, back))

        score_banks[0] = [3, 4, 1, 2]
        run_pipeline(steps, lag=int(os.environ.get("KLAGF", "3")))
        score_banks[0] = [3, 4]
        SF.close()
        stage("F%d" % l)

        SL = Scope("L%d_" % l)
        lng_t = SL.sb("lng_t", [128, D])
        lnb_t = SL.sb("lnb_t", [128, D])
        c_lng, c_lnb = SL.buf("lng"), SL.buf("lnb")
        NR = 3
        xr_t = [SL.sb("xr%d" % i, [128, D]) for i in range(NR)]
        xr_b = [SL.buf("xr%d" % i) for i in range(NR)]
        r_t = [SL.sb("r%d" % i, [128, D]) for i in range(NR)]
        r_b = [SL.buf("r%d" % i) for i in range(NR)]
        y_t = [SL.sb("y%d" % i, [128, D]) for i in range(NR)]
        y_b = [SL.buf("y%d" % i) for i in range(NR)]
        yb_t = [SL.sb("yb%d" % i, [128, D], BF16) for i in range(NB)]
        yb_b = [SL.buf("yb%d" % i) for i in range(NB)]
        bst = [SL.sb("bst%d" % i, [128, 2, 6]) for i in range(NR)]
        mv_t = [SL.sb("mv%d" % i, [128, 4]) for i in range(NR)]
        st_b = [SL.buf("st%d" % i) for i in range(NR)]
        wo = slab_big
        dma("sp", lng_t[:], lng_in[l:l + 1, :].partition_broadcast(128), [], [c_lng])
        dma("sp", lnb_t[:], lnb_in[l:l + 1, :].partition_broadcast(128), [], [c_lnb])
        last = (l == DEPTH - 1)
        bank_pairs = [(0, 1), (2, 3), (4, 5)]

        def stage_a(tb):
            r = blk_rows(tb)
            i = tb % NR
            j = tb // 4
            banks = bank_pairs[i]
            for hf in range(2):
                ps, ps_bufs = psb[banks[hf]], psb_b[banks[hf]]
                for k in range(8):
                    mm(ps[0:r, :], mix[:, k, tb * 128:tb * 128 + r], wo[:, k, hf * 512:(hf + 1) * 512], k == 0, k == 7,
                       [mix_b[k][j], slab_b[0], slab_b[1]], ps_bufs)

        def load_xr(tb):
            r = blk_rows(tb)
            i = tb % NR
            if l == 0:
                dma("pool", xr_t[i][0:r, :], x_all[tb * 128:tb * 128 + r, :], [], [xr_b[i]])
            else:
                dma("pool", xr_t[i][0:r, :], xres[tb * 128:tb * 128 + r, :], [xres_b[tb]], [xr_b[i]])

        def stage_a2(tb):
            r = blk_rows(tb)
            i = tb % NR
            banks = bank_pairs[i]
            for hf in range(2):
                ps, ps_bufs = psb[banks[hf]], psb_b[banks[hf]]
                vop("dve", "scalar_tensor_tensor", ps_bufs + [xr_b[i]], [r_b[i]], out=r_t[i][0:r, hf * 512:(hf + 1) * 512],
                    in0=xr_t[i][0:r, hf * 512:(hf + 1) * 512], scalar=float(DN_ALPHA), in1=ps[0:r, :], op0=ALU.mult, op1=ALU.add)
                vop("dve", "bn_stats", [r_b[i]], [st_b[i]], out=bst[i][0:r, hf, :], in_=r_t[i][0:r, hf * 512:(hf + 1) * 512])
            vop("dve", "bn_aggr", [st_b[i]], [st_b[i]], out=mv_t[i][0:r, 0:2], in_=bst[i][0:r, :, :])
            act(mv_t[i][0:r, 2:3], mv_t[i][0:r, 1:2], AF.Ln, [st_b[i]], [st_b[i]], bias=float(LN_EPS), scale=1.0)
            act(mv_t[i][0:r, 2:3], mv_t[i][0:r, 2:3], AF.Exp, [st_b[i]], [st_b[i]], scale=-0.5)
            vop("dve", "scalar_tensor_tensor", [st_b[i]], [st_b[i]], out=mv_t[i][0:r, 3:4], in0=mv_t[i][0:r, 0:1], scalar=-1.0,
                in1=mv_t[i][0:r, 2:3], op0=ALU.mult, op1=ALU.mult)
            act(y_t[i][0:r, :], r_t[i][0:r, :], AF.Identity, [r_b[i], st_b[i]], [y_b[i]], bias=mv_t[i][0:r, 3:4], scale=mv_t[i][0:r, 2:3])

        def stage_b(tb):
            r = blk_rows(tb)
            i = tb % NR
            vop("pool", "tensor_tensor", [y_b[i], c_lng], [y_b[i]], out=y_t[i][0:r, :], in0=y_t[i][0:r, :], in1=lng_t[0:r, :], op=ALU.mult)
            vop("dve", "tensor_tensor", [y_b[i], c_lnb], [y_b[i]], out=y_t[i][0:r, 0:512], in0=y_t[i][0:r, 0:512], in1=lnb_t[0:r, 0:512], op=ALU.add)
            vop("pool", "tensor_tensor", [y_b[i], c_lnb], [y_b[i]], out=y_t[i][0:r, 512:D], in0=y_t[i][0:r, 512:D], in1=lnb_t[0:r, 512:D], op=ALU.add)
            if last:
                dma("sp", y_out[tb * 128:tb * 128 + r, :], y_t[i][0:r, :], [y_b[i]], [])
            else:
                dma("sp", xres[tb * 128:tb * 128 + r, :], y_t[i][0:r, :], [y_b[i]], [xres_b[tb]])

        def stage_c1(tb):
            if last:
                return
            r = blk_rows(tb)
            i = tb % NR
            act(yb_t[tb][0:r, :], y_t[i][0:r, :], AF.Copy, [y_b[i]], [yb_b[tb]])

        def stage_c2(tb):
            if last:
                return
            transpose_block_to(xT, [xT_b[tb]], yb_t[tb], yb_b[tb], blk_rows(tb), tb * 128)

        for t in range(NB + 2):
            if t == 0:
                load_xr(0)
                load_xr(1)
            if t + 2 < NB:
                load_xr(t + 2)
            if t < NB:
                stage_a(t)
                stage_a2(t)
            if 0 <= t - 1 < NB:
                stage_b(t - 1)
            if 0 <= t - 2 < NB:
                stage_c1(t - 2)
        for t in range(NB):
            stage_c2(t)
        SL.close()
        stage("L%d" % l)

    try:
        for l in range(DEPTH):
            layer_body(l)
    except _Stop:
        for sc in reversed(list(open_scopes)):
            sc.close()
    P.emit(nc, es)
    es.close()
    return nc


_NC_CACHE = {}


def kernel(x_prompt, x_sample, mem_prompt, state_hgrn, cache_fox_k, cache_fox_v, cache_fox_logf,
           cache_mem_k, cache_mem_v, w_in, b_fox_forget, hgrn_lower_bounds, hgrn_norm_g,
           w_mem_kv, w_out, ln_g, ln_b):
    f = lambda a: np.ascontiguousarray(np.asarray(a, dtype=np.float32))
    x_prompt, x_sample, mem_prompt = f(x_prompt), f(x_sample), f(mem_prompt)
    state_hgrn, cache_fox_k, cache_fox_v, cache_fox_logf = f(state_hgrn), f(cache_fox_k), f(cache_fox_v), f(cache_fox_logf)
    cache_mem_k, cache_mem_v = f(cache_mem_k), f(cache_mem_v)
    shared = dict(w_in=f(w_in), bff=f(b_fox_forget), hlb=f(hgrn_lower_bounds), hng=f(hgrn_norm_g), wkv=f(w_mem_kv),
                  wout=f(w_out), lng=f(ln_g), lnb=f(ln_b))
    if "nc" not in _NC_CACHE:
        _NC_CACHE["nc"] = build_nc()
    nc = _NC_CACHE["nc"]
    in_maps = []
    for b in range(8):
        m = dict(shared)
        m["x_all"] = np.ascontiguousarray(np.concatenate([x_prompt[b], x_sample[b]], axis=0))
        m["mem"] = mem_prompt[b]
        m["st"] = np.ascontiguousarray(state_hgrn[:, b])
        m["ck"] = np.ascontiguousarray(cache_fox_k[:, b].reshape(DEPTH, PAST, 256))
        m["cv"] = np.ascontiguousarray(cache_fox_v[:, b].reshape(DEPTH, PAST, 256))
        m["clf"] = np.ascontiguousarray(cache_fox_logf[:, b])
        m["cmk"] = np.ascontiguousarray(cache_mem_k[:, b].reshape(DEPTH, 256, 256))
        m["cmv"] = np.ascontiguousarray(cache_mem_v[:, b].reshape(DEPTH, 256, 256))
        in_maps.append(m)
    res = run_bass_kernel_spmd(nc, in_maps, core_ids=list(range(8)))
    R = res.results
    y_all = np.stack([R[b]["y_all"] for b in range(8)])
    hst = np.stack([R[b]["hst"] for b in range(8)], axis=2)
    kv = np.stack([R[b]["kv_all"] for b in range(8)], axis=1)
    lf = np.stack([R[b]["lf_all"] for b in range(8)], axis=1)
    mkv = np.stack([R[b]["mkv"] for b in range(8)], axis=1)
    c = np.ascontiguousarray
    y_prompt = c(y_all[:, :TP])
    y_sample = c(y_all[:, TP:])
    p_state = c(hst[:, 0])
    s_state = c(hst[:, 1])
    p_k = c(kv[:, :, :TP, 0:256]).reshape(DEPTH, 8, TP, 4, 64)
    p_v = c(kv[:, :, :TP, 256:512]).reshape(DEPTH, 8, TP, 4, 64)
    s_k = c(kv[:, :, TP:, 0:256]).reshape(DEPTH, 8, TS, 4, 64)
    s_v = c(kv[:, :, TP:, 256:512]).reshape(DEPTH, 8, TS, 4, 64)
    p_lf = c(lf[:, :, :TP])
    s_lf = c(lf[:, :, TP:])
    p_mk = c(mkv[:, :, :, 0:256]).reshape(DEPTH, 8, 256, 4, 64)
    p_mv = c(mkv[:, :, :, 256:512]).reshape(DEPTH, 8, 256, 4, 64)
    return (y_prompt, y_sample, p_state, p_k, p_v, p_lf, p_mk, p_mv, s_state, s_k, s_v, s_lf)
```

```python
import os
import numpy as np
from contextlib import ExitStack
import concourse.bass as bass
import concourse.mybir as mybir
from concourse.bass_utils import run_bass_kernel_spmd

F32 = mybir.dt.float32
BF16 = mybir.dt.bfloat16
AF = mybir.ActivationFunctionType
ALU = mybir.AluOpType
AX = mybir.AxisListType

D = 1024
TP = 2048
TS = 16
TT = TP + TS
NB = 17
PAST = 4096
NPB = PAST // 128
DEPTH = 2
INW = 3588
HQ, HF, HI, HG, FQ, FK, FV, FG, FL, MQ, MG = 0, 512, 1024, 1536, 2048, 2304, 2560, 2816, 3072, 3076, 3332
TTILES = [(0, 512), (512, 512), (1024, 512), (1536, 512), (2048, 16)]
DN_ALPHA = (2 * DEPTH) ** 0.25
LN_EPS = 1e-5
RMS_EPS = 1e-6
N_DMA_SEMS = 56


def blk_rows(tb):
    return 128 if tb < 16 else 16


class Buf:
    __slots__ = ("name", "last_w", "rd", "rd_dma")

    def __init__(self, name):
        self.name = name
        self.last_w = None
        self.rd = {}
        self.rd_dma = []


class Op:
    __slots__ = ("eng", "meth", "args", "kwargs", "dma", "deps", "needed", "sem", "val", "idx", "prev_dma")


def flat(x):
    out = []
    for e in x:
        if isinstance(e, (list, tuple)):
            out.extend(flat(e))
        elif e is not None:
            out.append(e)
    return out


class Prog:
    def __init__(self):
        self.ops = []

    def add(self, eng, meth, args=(), kwargs=None, reads=(), writes=(), dma=False):
        op = Op()
        op.eng, op.meth, op.args, op.kwargs, op.dma = eng, meth, args, (kwargs or {}), dma
        op.idx = len(self.ops)
        op.needed = dma
        op.sem = None
        op.val = None
        op.prev_dma = None
        deps = set()
        reads = flat(reads)
        writes = flat(writes)
        ops = self.ops

        def consider(d, raw):
            if d is None:
                return
            o = ops[d]
            if (not raw) and (not dma) and (not o.dma) and o.eng == eng:
                return
            deps.add(d)

        for b in reads:
            consider(b.last_w, True)
        for b in writes:
            consider(b.last_w, False)
            for r in b.rd.values():
                consider(r, False)
            for r in b.rd_dma:
                consider(r, False)
        op.deps = deps
        for b in reads:
            if dma:
                b.rd_dma.append(op.idx)
            else:
                b.rd[eng] = op.idx
        for b in writes:
            b.last_w = op.idx
            b.rd = {}
            b.rd_dma = []
        ops.append(op)
        return op

    def emit(self, nc, es):
        ops = self.ops
        for op in ops:
            for d in op.deps:
                ops[d].needed = True
        engs = {"pe": nc.tensor, "act": nc.scalar, "dve": nc.vector, "pool": nc.gpsimd, "sp": nc.sync}
        esem = {k: es.enter_context(nc.semaphore("sem_" + k)) for k in ("pe", "act", "dve", "pool")}
        dsem = [es.enter_context(nc.semaphore("dsem%d" % i)) for i in range(N_DMA_SEMS)]
        cnt = {k: 0 for k in esem}
        dcount = [0] * N_DMA_SEMS
        dlast = [None] * N_DMA_SEMS
        NSP = 24
        ndma = {"sp": 0, "pool": 0}
        for op in ops:
            if op.dma:
                if op.eng == "sp":
                    s = ndma["sp"] % NSP
                else:
                    s = NSP + ndma["pool"] % (N_DMA_SEMS - NSP)
                ndma[op.eng] += 1
                dcount[s] += 16
                op.sem, op.val = dsem[s], dcount[s]
                op.prev_dma = dlast[s]
                dlast[s] = op.idx
            elif op.needed:
                cnt[op.eng] += 1
                op.sem, op.val = esem[op.eng], cnt[op.eng]
        per_eng = {k: [] for k in engs}
        for op in ops:
            per_eng[op.eng].append(op)
        block = es.enter_context(nc.Block())

        def run(ename):
            def body(e):
                waited = {}
                nwait = 0
                for op in per_eng[ename]:
                    deps = set(op.deps)
                    if op.prev_dma is not None:
                        deps.add(op.prev_dma)
                    need = {}
                    for d in deps:
                        o = ops[d]
                        key = id(o.sem)
                        if waited.get(key, 0) >= o.val:
                            continue
                        if key not in need or need[key][1] < o.val:
                            need[key] = (o.sem, o.val)
                    for key, (s, v) in need.items():
                        e.wait_ge(s, v)
                        waited[key] = v
                        nwait += 1
                    ins = getattr(e, op.meth)(*op.args, **op.kwargs)
                    if op.sem is not None:
                        ins.then_inc(op.sem, 16 if op.dma else 1)
                if ename == "sp":
                    for i in range(N_DMA_SEMS):
                        if dcount[i] > 0:
                            e.wait_ge(dsem[i], dcount[i])
                print("engine %s: %d ops, %d waits" % (ename, len(per_eng[ename]), nwait))
            return body

        block.tensor(run("pe"))
        block.scalar(run("act"))
        block.vector(run("dve"))
        block.gpsimd(run("pool"))
        block.sync(run("sp"))


def build_nc():
    nc = bass.Bass("TRN2", target_bir_lowering=False)
    es = ExitStack()
    P = Prog()

    def din(name, shape):
        return nc.dram_tensor(name, list(shape), F32, kind="ExternalInput").ap()

    def dout(name, shape):
        return nc.dram_tensor(name, list(shape), F32, kind="ExternalOutput").ap()

    x_all = din("x_all", [TT, D])
    mem_in = din("mem", [256, D])
    st_in = din("st", [DEPTH, 4, 128, 128])
    ck_in = din("ck", [DEPTH, PAST, 256])
    cv_in = din("cv", [DEPTH, PAST, 256])
    clf_in = din("clf", [DEPTH, PAST, 4])
    cmk_in = din("cmk", [DEPTH, 256, 256])
    cmv_in = din("cmv", [DEPTH, 256, 256])
    w_in = din("w_in", [DEPTH, D, INW])
    bff_in = din("bff", [DEPTH, 4])
    hlb_in = din("hlb", [DEPTH, 512])
    hng_in = din("hng", [DEPTH, 128])
    wkv_in = din("wkv", [DEPTH, D, 512])
    wout_in = din("wout", [DEPTH, D, D])
    lng_in = din("lng", [DEPTH, D])
    lnb_in = din("lnb", [DEPTH, D])

    y_out = dout("y_all", [TT, D])
    hst_out = dout("hst", [DEPTH, 2, 4, 128, 128])
    kv_out = dout("kv_all", [DEPTH, TT, 512])
    lf_out = dout("lf_all", [DEPTH, TT, 4])
    mkv_out = dout("mkv", [DEPTH, 256, 512])
    xres = nc.dram_tensor("xres_scratch", [TT, D], F32).ap()

    def sb(name, shape, dt=F32):
        return es.enter_context(nc.sbuf_tensor(name, list(shape), dt))

    fence_t = sb("fence_t", [128, 1])
    fence_s = sb("fence_s", [128, 1])
    c_fence = Buf("fence_src")
    P.add("pool", "memset", (fence_s[:], 0.0), {}, [], [c_fence])
    closed_bufs = []
    open_scopes = []

    class Scope:
        def __init__(self, tag):
            self.tag = tag
            self.es = ExitStack()
            self.bufs = []
            self.fence = None
            open_scopes.append(self)
            if closed_bufs:
                old = list(closed_bufs)
                del closed_bufs[:]
                self.fence = P.add("sp", "dma_start", (), dict(out=fence_t[0:1, 0:1], in_=fence_s[0:1, 0:1]), old + [c_fence], old, dma=True)

        def sb(self, name, shape, dt=F32):
            return self.es.enter_context(nc.sbuf_tensor(self.tag + name, list(shape), dt))

        def buf(self, name):
            b = Buf(self.tag + name)
            if self.fence is not None:
                b.last_w = self.fence.idx
            self.bufs.append(b)
            return b

        def close(self):
            closed_bufs.extend(self.bufs)
            self.es.close()
            open_scopes.remove(self)

    def psum(name, shape, dt=F32):
        return es.enter_context(nc.psum_tensor(name, list(shape), dt))

    def mm(out, lhsT, rhs, start, stop, reads, writes, skip=False):
        kw = dict(lhsT=lhsT, rhs=rhs, start=start, stop=stop)
        if skip:
            kw["skip_group_check"] = True
        P.add("pe", "matmul", (out,), kw, reads, writes)

    def tr(out, in_, ident, reads, writes):
        P.add("pe", "transpose", (out, in_, ident), {}, reads, writes)

    def act(out, in_, func, reads, writes, bias=None, scale=None):
        kw = dict(out=out, in_=in_, func=func)
        if bias is not None:
            kw["bias"] = bias
        if scale is not None:
            kw["scale"] = scale
        P.add("act", "activation", (), kw, reads, writes)

    def vop(eng, meth, reads, writes, *args, **kwargs):
        P.add(eng, meth, args, kwargs, reads, writes)

    def dma(eng, out, in_, reads, writes, **kw):
        P.add(eng, "dma_start", (), dict(out=out, in_=in_, **kw), reads, writes, dma=True)

    NBANK = 7
    psb = [psum("psb%d" % i, [128, 512]) for i in range(NBANK)]
    psb_b = [[Buf("psb%d" % i)] * 8 for i in range(NBANK)]
    pst = psum("pst", [128, 1024], BF16)
    pst_b = Buf("pst")

    identb = sb("identb", [128, 128], BF16)
    identf = sb("identf", [128, 128])
    trif = sb("trif", [128, 128])
    trib = sb("trib", [128, 128], BF16)
    onesf = sb("onesf", [128, 128])
    scanm = sb("scanm", [128, 512])
    c_ident, c_tri, c_ones, c_scanm = Buf("ident"), Buf("tri"), Buf("ones"), Buf("scanm")

    vop("pool", "memset", [], [c_ident], identf[:], 0.0)
    vop("pool", "affine_select", [c_ident], [c_ident], out=identf[:], in_=identf[:], pattern=[[-1, 128]],
        compare_op=ALU.not_equal, fill=1.0, base=0, channel_multiplier=1)
    vop("pool", "tensor_copy", [c_ident], [c_ident], out=identb[:], in_=identf[:])
    vop("pool", "memset", [], [c_tri], trif[:], 1.0)
    vop("pool", "affine_select", [c_tri], [c_tri], out=trif[:], in_=trif[:], pattern=[[1, 128]],
        compare_op=ALU.is_ge, fill=0.0, base=0, channel_multiplier=-1)
    vop("pool", "tensor_copy", [c_tri], [c_tri], out=trib[:], in_=trif[:])
    vop("pool", "memset", [], [c_ones], onesf[:], 1.0)
    mneg = sb("mneg", [128, 128])
    c_mneg = Buf("mneg")
    vop("pool", "memset", [], [c_mneg], mneg[:], 0.0)
    vop("pool", "affine_select", [c_mneg], [c_mneg], out=mneg[:], in_=mneg[:], pattern=[[1, 128]],
        compare_op=ALU.is_ge, fill=-1.0e30, base=0, channel_multiplier=-1)
    onesb = sb("onesb", [128, 128], BF16)
    vop("pool", "tensor_copy", [c_ones], [c_ones], out=onesb[:], in_=onesf[:])
    sutf = sb("sutf", [128, 128])
    vop("pool", "tensor_tensor", [c_tri, c_ident], [c_tri], out=sutf[:], in0=trif[:], in1=identf[:], op=ALU.subtract)
    vop("pool", "memset", [], [c_scanm], scanm[:], 1.0)
    vop("pool", "memset", [c_scanm], [c_scanm], scanm[:].rearrange("p (c k) -> p c k", k=64)[:, :, 0:1], 0.0)

    lbt = sb("lbt", [128, 4, 2])
    lbe = sb("lbe", [128, 4, 2])
    lbs = sb("lbs", [128, 4, 1])
    low = sb("low", [128, 4, 2])
    lnoml = sb("lnoml", [128, 4, 2])
    c_lb = Buf("lb")
    with nc.allow_non_contiguous_dma(reason="tiny parameter vectors"):
        pass
    for l_ in range(DEPTH):
        for h_ in range(4):
            dma("sp", lbt[:, h_, l_:l_ + 1], hlb_in[l_:l_ + 1, h_ * 128:(h_ + 1) * 128].rearrange("o p -> p o"), [], [c_lb],
                allow_slow_non_contiguous=True)
    act(lbe[:], lbt[:], AF.Exp, [c_lb], [c_lb])
    vop("dve", "tensor_tensor", [c_lb], [c_lb], out=lbs[:], in0=lbe[:, :, 0:1], in1=lbe[:, :, 1:2], op=ALU.add)
    vop("dve", "reciprocal", [c_lb], [c_lb], out=lbs[:], in_=lbs[:])
    vop("dve", "tensor_tensor", [c_lb], [c_lb], out=lbe[:], in0=lbe[:], in1=lbs[:].to_broadcast([128, 4, 2]), op=ALU.mult)
    vop("dve", "tensor_tensor", [c_lb], [c_lb], out=low[:, :, 0:1], in0=lbe[:, :, 0:1], in1=lbe[:, :, 0:1], op=ALU.subtract)
    vop("dve", "tensor_tensor", [c_lb], [c_lb], out=low[:, :, 1:2], in0=lbe[:, :, 0:1], in1=lbe[:, :, 1:2], op=ALU.add)
    vop("dve", "tensor_tensor", [c_lb], [c_lb], out=low[:, :, 1:2], in0=low[:, :, 1:2], in1=lbe[:, :, 0:1], op=ALU.subtract)
    act(lnoml[:], low[:], AF.Ln, [c_lb], [c_lb], bias=1.0, scale=-1.0)

    negb = sb("negb", [4, 2])
    c_negb = Buf("negb")
    dma("sp", negb[:], bff_in.rearrange("l h -> h l"), [], [c_negb], allow_slow_non_contiguous=True)
    vop("dve", "tensor_scalar", [c_negb], [c_negb], out=negb[:], in0=negb[:], scalar1=-1.0, scalar2=None, op0=ALU.mult)
    gcol = sb("gcol", [128, 2])
    c_gcol = Buf("gcol")
    dma("sp", gcol[:], hng_in.rearrange("l p -> p l"), [], [c_gcol], allow_slow_non_contiguous=True)
    vop("dve", "tensor_scalar", [c_gcol], [c_gcol], out=gcol[:], in0=gcol[:], scalar1=float(np.sqrt(128.0)), scalar2=None, op0=ALU.mult)

    xT = sb("xT", [128, 8, TT], BF16)
    xT_b = [Buf("xT%d" % i) for i in range(NB)]
    mix = sb("mix", [128, 8, TT], BF16)
    mix_b = [[Buf("mix%d_%d" % (c, j)) for j in range(5)] for c in range(8)]

    def xT_bufs(t0, n):
        return [xT_b[i] for i in range(t0 // 128, (t0 + n - 1) // 128 + 1)]

    NSLAB = 2
    slab_big = sb("slab", [128, 8, 1024], BF16)
    slab = [slab_big[:, :, i * 512:(i + 1) * 512] for i in range(NSLAB)]
    slab_b = [Buf("slab%d" % i) for i in range(NSLAB)]
    slab_ctr = [0]

    def load_slab(src2d, c0, n):
        i = slab_ctr[0] % NSLAB
        slab_ctr[0] += 1
        dma("pool", slab[i][:, :, 0:n], src2d.rearrange("(k p) c -> p k c", p=128)[:, :, c0:c0 + n], [], [slab_b[i]])
        return slab[i], slab_b[i]

    acc_ctr = [0]

    def next_acc():
        i = acc_ctr[0] % 3
        acc_ctr[0] += 1
        return psb[i], psb_b[i]

    xbt = [sb("xbt%d" % i, [128, D], BF16) for i in range(2)]
    xbt_b = [Buf("xbt%d" % i) for i in range(2)]

    tr_ctr = [0]

    def transpose_block_to(dst, dst_bufs, src_tile, src_buf, rows, col0):
        w = tr_ctr[0] % 2
        tr_ctr[0] += 1
        if w == 0:
            pt, pt_b = pst[:], [pst_b]
        else:
            pt, pt_b = psb[6][:].bitcast(BF16), psb_b[6]
        for k in range(8):
            tr(pt[:, k * 128:k * 128 + rows], src_tile[0:rows, k * 128:(k + 1) * 128], identb[0:rows, 0:rows],
               [src_buf, c_ident], pt_b)
        src3 = pt.rearrange("p (k t) -> p k t", k=8)[:, :, 0:rows]
        if w == 0:
            vop("dve", "tensor_copy", pt_b, dst_bufs, out=dst[:, :, col0:col0 + rows], in_=src3)
        else:
            act(dst[:, :, col0:col0 + rows], src3, AF.Copy, pt_b, dst_bufs)

    def build_xT_block(tb):
        r = blk_rows(tb)
        i = (tb // 2) % 2
        dma("pool", xbt[i][0:r, :], x_all[tb * 128:tb * 128 + r, :], [], [xbt_b[i]])
        transpose_block_to(xT, [xT_b[tb]], xbt[i], xbt_b[i], r, tb * 128)

    memT = sb("memT", [128, 8, 256], BF16)
    c_memT = Buf("memT")

    def build_memT():
        for mb in range(2):
            i = mb % 2
            dma("pool", xbt[i][:, :], mem_in[mb * 128:(mb + 1) * 128, :], [], [xbt_b[i]])
            transpose_block_to(memT, [c_memT], xbt[i], xbt_b[i], 128, mb * 128)

    one_c = sb("one_c", [128, 1])
    c_onec = Buf("onec")
    vop("pool", "memset", [], [c_onec], one_c[:], 1.0)

    ATT = {}

    def alloc_attn_tiles(sc):
        ATT["PT"] = [sc.sb("PT%d" % i, [128, 512], BF16) for i in range(5)]
        ATT["PT_b"] = [sc.buf("PT%d" % i) for i in range(5)]
        ATT["stg"] = [sc.sb("stg%d" % i, [128, 512]) for i in range(3)]
        ATT["stg_b"] = [sc.buf("stg%d" % i) for i in range(3)]
        ATT["rs"] = [sc.sb("rs%d" % i, [128, 512]) for i in range(2)]
        ATT["rs_b"] = [sc.buf("rs%d" % i) for i in range(2)]

    pt_ctr = [0]
    stg_ctr = [0]

    def next_stg():
        i = stg_ctr[0] % 3
        stg_ctr[0] += 1
        return ATT["stg"][i], ATT["stg_b"][i]

    def next_pt():
        i = pt_ctr[0] % 5
        pt_ctr[0] += 1
        return ATT["PT"][i], ATT["PT_b"][i]

    sc_ctr = [0]

    score_banks = [[3, 4]]

    def next_score():
        bl = score_banks[0]
        i = bl[sc_ctr[0] % len(bl)]
        sc_ctr[0] += 1
        return psb[i], psb_b[i]

    oacc_ctr = [0]

    def next_oacc():
        i = (5, 6)[oacc_ctr[0] % 2]
        oacc_ctr[0] += 1
        return psb[i], psb_b[i]

    rs_ctr = [0]

    def attn_finish(po, po_b, h, ci, t0, n, c0=0):
        pb = 64 * (h % 2)
        ob = 64 - pb
        i = rs_ctr[0] % 2
        rs_ctr[0] += 1
        j = t0 // 512
        rs_t, rs_b = ATT["rs"], ATT["rs_b"]
        act(rs_t[i][pb:pb + 64, 0:n], po[ob:ob + 64, c0:c0 + n], AF.Ln, po_b, [rs_b[i]])
        act(rs_t[i][pb:pb + 64, 0:n], rs_t[i][pb:pb + 64, 0:n], AF.Exp, [rs_b[i]], [rs_b[i]], scale=-1.0)
        vop("dve", "tensor_tensor", po_b + [rs_b[i]], [rs_b[i]], out=rs_t[i][pb:pb + 64, 0:n], in0=po[pb:pb + 64, c0:c0 + n],
            in1=rs_t[i][pb:pb + 64, 0:n], op=ALU.mult)
        vop("dve", "tensor_tensor", [rs_b[i], mix_b[ci][j]], [mix_b[ci][j]], out=mix[pb:pb + 64, ci, t0:t0 + n],
            in0=rs_t[i][pb:pb + 64, 0:n], in1=mix[pb:pb + 64, ci, t0:t0 + n], op=ALU.mult)

    def keep_warm(n, bank=1):
        for _ in range(n):
            P.add("pe", "matmul", (psb[bank][:, 0:128],), dict(lhsT=identb[:], rhs=identb[:], start=True, stop=True, skip_group_check=True),
                  [c_ident], psb_b[bank])

    NWARM = int(os.environ.get("KWARM", "0"))

    def run_pipeline(steps, lag=1):
        lag = int(os.environ.get("KLAG", lag))
        n = len(steps)
        for k in range(n + lag):
            if k < n and steps[k][0] is not None:
                steps[k][0]()
            if k >= lag and steps[k - lag][1] is not None:
                steps[k - lag][1]()

    xres_b = [Buf("xres%d" % tb) for tb in range(NB)]

    dbg_stop = os.environ.get("KDBG", "")

    class _Stop(Exception):
        pass

    def stage(name):
        if dbg_stop == name:
            raise _Stop()

    def layer_body(l):
        w_l = w_in[l]

        def proj_fm(sl, sl_b, c0, m, evac, tiles=None):
            for j, (t0, n) in enumerate(TTILES):
                if tiles is not None and j not in tiles:
                    continue
                ps, ps_bufs = next_acc()
                for k in range(8):
                    mm(ps[0:m, 0:n], sl[:, k, c0:c0 + m], xT[:, k, t0:t0 + n], k == 0, k == 7,
                       [sl_b] + xT_bufs(t0, n), ps_bufs)
                evac(ps, ps_bufs, j, t0, n)

        def gate_evac(ci):
            def f(ps, ps_bufs, j, t0, n):
                act(mix[:, ci, t0:t0 + n], ps[:, 0:n], AF.Silu, ps_bufs, [mix_b[ci][j]])
            return f

        sl, slb = load_slab(w_l, HG, 512)
        if l == 0:
            SX = Scope("X0_")
            xs = [SX.sb("xs%d" % i, [128, D]) for i in range(2)]
            xs_b = [SX.buf("xs%d" % i) for i in range(2)]
            xb2 = [SX.sb("xb2%d" % i, [128, D], BF16) for i in range(2)]
            xb2_b = [SX.buf("xb2%d" % i) for i in range(2)]
            for j in range(5):
                for tb in range(4 * j, min(4 * j + 4, NB)):
                    if tb % 2 == 0:
                        build_xT_block(tb)
                    else:
                        r = blk_rows(tb)
                        i = (tb // 2) % 2
                        dma("sp", xs[i][0:r, :], x_all[tb * 128:tb * 128 + r, :], [], [xs_b[i]])
                        vop("dve", "tensor_copy", [xs_b[i]], [xb2_b[i]], out=xb2[i][0:r, :], in_=xs[i][0:r, :])
                        transpose_block_to(xT, [xT_b[tb]], xb2[i], xb2_b[i], r, tb * 128)
                for c in range(4):
                    proj_fm(sl, slb, c * 128, 128, gate_evac(c), tiles=(j,))
            build_memT()
            SX.close()
        else:
            for c in range(4):
                proj_fm(sl, slb, c * 128, 128, gate_evac(c))
        sl, slb = load_slab(w_l, FG, 256)
        for c in range(2):
            proj_fm(sl, slb, c * 128, 128, gate_evac(4 + c))
        sl, slb = load_slab(w_l, MG, 256)
        for c in range(2):
            proj_fm(sl, slb, c * 128, 128, gate_evac(6 + c))

        sl_hq, slb_hq = load_slab(w_l, HQ, 512)
        SH = Scope("H%d_" % l)
        qt = [SH.sb("qt%d" % h, [128, TT], BF16) for h in range(4)]
        kt = [SH.sb("kt%d" % h, [128, TT], BF16) for h in range(4)]
        ktm = [SH.sb("ktm%d" % h, [128, NB, 128], BF16) for h in range(4)]
        qt_b = [SH.buf("qt%d" % h) for h in range(4)]
        kt_b = [SH.buf("kt%d" % h) for h in range(4)]
        ktm_b = [SH.buf("ktm%d" % h) for h in range(4)]
        eGl = SH.sb("eGl", [128, 4, 33])
        eGl_b = [SH.buf("eGl%d" % h) for h in range(4)]
        vhg = SH.sb("vhg", [128, NB, 512], BF16)
        vhg_b = [SH.buf("vhg%d" % tb) for tb in range(NB)]
        S32 = [SH.sb("S32_%d" % h, [128, 128]) for h in range(4)]
        Sbf = [SH.sb("Sbf_%d" % h, [128, 128], BF16) for h in range(4)]
        S32s = [SH.sb("S32s_%d" % h, [128, 128]) for h in range(4)]
        Sbfs = [SH.sb("Sbfs_%d" % h, [128, 128], BF16) for h in range(4)]
        S_b = [SH.buf("S%d" % h) for h in range(4)]
        Sbf_b = [SH.buf("Sbf%d" % h) for h in range(4)]
        Ss_b = [SH.buf("Ss%d" % h) for h in range(4)]
        Sbfs_b = [SH.buf("Sbfs%d" % h) for h in range(4)]
        AT = [SH.sb("AT%d" % h, [128, NB, 64], BF16) for h in range(4)]
        AT_b = [[SH.buf("AT%d_%d" % (h, j)) for j in range(5)] for h in range(4)]
        Tst = [[SH.sb("T%d_%d" % (h, i), [128, 128]) for i in range(2)] for h in range(4)]
        Tst_b = [[SH.buf("T%d_%d" % (h, i)) for i in range(2)] for h in range(4)]
        osq_ctr = [0]
        osb_ctr = [0]
        rms_pending = []
        rms_b_pending = []
        SHp = Scope("Hp%d_" % l)
        NHT = 3
        hu = [SHp.sb("hu%d" % i, [128, 512]) for i in range(NHT)]
        hl1 = [SHp.sb("hl1%d" % i, [128, 512]) for i in range(NHT)]
        hl2 = [SHp.sb("hl2%d" % i, [128, 512]) for i in range(NHT)]
        hG = [SHp.sb("hG%d" % i, [128, 512]) for i in range(NHT)]
        hu_b = [SHp.buf("hu%d" % i) for i in range(NHT)]
        hl1_b = [SHp.buf("hl1%d" % i) for i in range(NHT)]
        hl2_b = [SHp.buf("hl2%d" % i) for i in range(NHT)]
        hG_b = [SHp.buf("hG%d" % i) for i in range(NHT)]
        ht_ctr = [0]
        hf_pending = []

        sl, slb = sl_hq, slb_hq
        for h in range(4):
            def f(ps, ps_bufs, j, t0, n, h=h):
                act(qt[h][:, t0:t0 + n], ps[:, 0:n], AF.Silu, ps_bufs, [qt_b[h]])
            proj_fm(sl, slb, h * 128, 128, f)
        sl, slb = load_slab(w_l, HI, 512)
        for tb in range(NB):
            r = blk_rows(tb)
            ps, ps_bufs = next_acc()
            for k in range(8):
                mm(ps[0:r, :], xT[:, k, tb * 128:tb * 128 + r], sl[:, k, 0:512], k == 0, k == 7, [slb, xT_b[tb]], ps_bufs)
            vop("dve", "tensor_copy", ps_bufs, [vhg_b[tb]], out=vhg[0:r, tb, :], in_=ps[0:r, :])
        stage("Ha%d" % l)
        def ktm_transposes(h):
            for g0 in range(0, NB, 8):
                g1 = min(NB, g0 + 8)
                for tb in range(g0, g1):
                    r = blk_rows(tb)
                    tr(pst[0:r, (tb - g0) * 128:(tb - g0 + 1) * 128], kt[h][:, tb * 128:tb * 128 + r], identb[:], [kt_b[h], c_ident], [pst_b])
                if g1 - g0 == 8:
                    vop("dve", "tensor_copy", [pst_b], [ktm_b[h]], out=ktm[h][:, g0:g1, :], in_=pst[:].rearrange("p (b k) -> p b k", b=8))
                else:
                    vop("dve", "tensor_copy", [pst_b], [ktm_b[h]], out=ktm[h][0:16, g0, :], in_=pst[0:16, 0:128])

        sl, slb = load_slab(w_l, HF, 512)
        for h in range(4):
            def f(ps, ps_bufs, j, t0, n, h=h):
                i = ht_ctr[0] % NHT
                ht_ctr[0] += 1
                u_, l1_, l2_, G_ = hu[i][:, 0:n], hl1[i][:, 0:n], hl2[i][:, 0:n], hG[i][:, 0:n]

                def stage_b(i=i, n=n, l1_=l1_, l2_=l2_, G_=G_):
                    vop("pool", "tensor_tensor", [hl1_b[i], hl2_b[i]], [hl2_b[i]], out=l2_, in0=l2_, in1=l1_, op=ALU.subtract)
                    vop("dve", "tensor_tensor_scan", [hl2_b[i], c_scanm], [hG_b[i]], out=G_, data0=scanm[:, 0:n], data1=l2_,
                        initial=0.0, op0=ALU.mult, op1=ALU.add)
                    vop("pool", "tensor_tensor", [hl1_b[i], hG_b[i]], [hl1_b[i]], out=l1_, in0=l1_, in1=G_, op=ALU.add)

                def stage_c(i=i, h=h, j=j, t0=t0, n=n, u_=u_, l1_=l1_, l2_=l2_, G_=G_):
                    act(l2_, G_, AF.Exp, [hG_b[i]], [hl2_b[i]])
                    vop("dve", "tensor_tensor", [hl2_b[i], qt_b[h]], [qt_b[h]], out=qt[h][:, t0:t0 + n], in0=l2_, in1=qt[h][:, t0:t0 + n],
                        op=ALU.mult)
                    if n == 512:
                        vop("dve", "tensor_copy", [hl2_b[i]], [eGl_b[h]], out=eGl[:, h, j * 8:(j + 1) * 8],
                            in_=l2_.rearrange("p (c k) -> p c k", k=64)[:, :, 63])
                    else:
                        vop("dve", "tensor_copy", [hl2_b[i]], [eGl_b[h]], out=eGl[:, h, 32:33], in_=hl2[i][:, n - 1:n])
                    act(l1_, l1_, AF.Exp, [hl1_b[i], c_lb], [hl1_b[i]], bias=lnoml[:, h, l:l + 1], scale=-1.0)
                    vop("dve", "tensor_tensor", [hl1_b[i], hu_b[i]], [kt_b[h]], out=kt[h][:, t0:t0 + n], in0=l1_, in1=u_, op=ALU.mult)

                if len(hf_pending) >= 2:
                    hf_pending.pop(0)[1]()
                if hf_pending:
                    hf_pending[-1][0]()
                act(u_, ps[:, 0:n], AF.Exp, ps_bufs, [hu_b[i]], scale=-1.0)
                act(l1_, u_, AF.Ln, [hu_b[i]], [hl1_b[i]], bias=1.0, scale=1.0)
                act(l2_, u_, AF.Ln, [hu_b[i], c_lb], [hl2_b[i]], bias=1.0, scale=low[:, h, l:l + 1])
                hf_pending.append((stage_b, stage_c))
            proj_fm(sl, slb, h * 128, 128, f)
            if h >= 1:
                ktm_transposes(h - 1)
        if hf_pending:
            if len(hf_pending) >= 2:
                hf_pending[0][1]()
            hf_pending[-1][0]()
            hf_pending[-1][1]()
            del hf_pending[:]
        ktm_transposes(3)
        for h in range(4):
            dma("sp", S32s[h][:], st_in[l, h], [], [Ss_b[h]])
            dma("pool", Sbfs[h][:], st_in[l, h], [], [Sbfs_b[h]])
        SHp.close()
        SHl = Scope("Hl%d_" % l)
        osq = [SHl.sb("osq%d" % i, [128, 512]) for i in range(2)]
        osq_b = [SHl.buf("osq%d" % i) for i in range(2)]
        osqb = [SHl.sb("osqb%d" % i, [128, 512], BF16) for i in range(2)]
        osqb_b = [SHl.buf("osqb%d" % i) for i in range(2)]
        osb = [SHl.sb("osb%d" % i, [128, 512]) for i in range(4)]
        osb_b = [SHl.buf("osb%d" % i) for i in range(4)]
        stage("Hb%d" % l)
        for j, (t0, n) in enumerate(TTILES):
            C = 64 if j < 4 else 16
            nch = n // C
            for h in range(4):
                ps, ps_bufs = next_score()
                for cc in range(nch):
                    ta = t0 + cc * C
                    mm(ps[0:C, cc * 64:cc * 64 + C], kt[h][:, ta:ta + C], qt[h][:, ta:ta + C], True, True, [kt_b[h], qt_b[h]], ps_bufs)
                if j < 4:
                    for half in range(2):
                        vop("dve", "tensor_tensor", ps_bufs + [c_tri], [AT_b[h][j]], out=AT[h][half * 64:(half + 1) * 64, 4 * j:4 * j + 4, :],
                            in0=ps[0:64, :].rearrange("p (c k) -> p c k", k=64)[:, half:8:2, :],
                            in1=trif[0:64, 0:64].unsqueeze(1).to_broadcast([64, 4, 64]), op=ALU.mult)
                else:
                    vop("dve", "tensor_tensor", ps_bufs + [c_tri], [AT_b[h][j]], out=AT[h][0:16, 16, 0:16], in0=ps[0:16, 0:16],
                        in1=trif[0:16, 0:16], op=ALU.mult)
        stage("Hc%d" % l)
        obanks = [0, 1, 2, 5]
        lbanks = [6, 4, 3]

        def emit_L(c):
            if c > 32:
                return
            C_ = 64 if c < 32 else 16
            tb_ = c // 2 if c < 32 else 16
            pb_ = 64 * (c % 2) if c < 32 else 0
            lb_ = lbanks[c % 3]
            for h in range(4):
                mm(psb[lb_][:, h * 128:(h + 1) * 128], ktm[h][pb_:pb_ + C_, tb_, :], vhg[pb_:pb_ + C_, tb_, h * 128:(h + 1) * 128], True, True,
                   [ktm_b[h], vhg_b[tb_]], psb_b[lb_])

        emit_L(0)
        for j, (t0, n) in enumerate(TTILES):
            C = 64 if j < 4 else 16
            nch = n // C
            samp = (j == 4)
            for cc in range(nch):
                c = (t0 // 64) + cc if j < 4 else 32
                tb = c // 2 if j < 4 else 16
                pbk = 64 * (c % 2) if j < 4 else 0
                ta = t0 + cc * C
                first = (c == 0)
                for h in range(4):
                    po = psb[obanks[h]][:, cc * C:(cc + 1) * C]
                    mm(po, vhg[pbk:pbk + C, tb, h * 128:(h + 1) * 128], AT[h][pbk:pbk + C, tb, 0:C], True, first,
                       [vhg_b[tb], AT_b[h][j]], psb_b[obanks[h]])
                lbank = lbanks[c % 3]
                emit_L(c + 1)
                if not first:
                    for h in range(4):
                        po = psb[obanks[h]][:, cc * C:(cc + 1) * C]
                        if samp:
                            mm(po, Sbfs[h][:], qt[h][:, ta:ta + C], False, True, [Sbfs_b[h], qt_b[h]], psb_b[obanks[h]])
                        else:
                            mm(po, Sbf[h][:], qt[h][:, ta:ta + C], False, True, [Sbf_b[h], qt_b[h]], psb_b[obanks[h]])
                for h in range(4):
                    psS = psb[lbank][:, h * 128:(h + 1) * 128]
                    psS_b = psb_b[lbank]
                    eg = eGl[:, h, c:c + 1]
                    if samp:
                        vop("dve", "tensor_tensor", psS_b + [Ss_b[h]], [Tst_b[h][0]], out=Tst[h][0][:], in0=psS, in1=S32s[h][:], op=ALU.add)
                        vop("dve", "tensor_scalar", [Tst_b[h][0], eGl_b[h]], [Ss_b[h]], out=S32s[h][:], in0=Tst[h][0][:], scalar1=eg, scalar2=None,
                            op0=ALU.mult)
                        dma("sp", hst_out[l, 1, h], S32s[h][:], [Ss_b[h]], [])
                        continue
                    cur, prv = Tst[h][c % 2], Tst[h][(c + 1) % 2]
                    cur_b, prv_b = Tst_b[h][c % 2], Tst_b[h][(c + 1) % 2]
                    if first:
                        vop("dve", "tensor_copy", psS_b, [cur_b], out=cur[:], in_=psS)
                    else:
                        vop("dve", "scalar_tensor_tensor", [prv_b, eGl_b[h]] + psS_b, [cur_b], out=cur[:], in0=prv[:], scalar=eGl[:, h, c - 1:c],
                            in1=psS, op0=ALU.mult, op1=ALU.add)
                    if c == 31:
                        vop("dve", "tensor_scalar", [cur_b, eGl_b[h]], [S_b[h]], out=S32[h][:], in0=cur[:], scalar1=eg, scalar2=None, op0=ALU.mult)
                        dma("sp", hst_out[l, 0, h], S32[h][:], [S_b[h]], [])
                    else:
                        act(Sbf[h][:], cur[:], AF.Identity, [cur_b, eGl_b[h]], [Sbf_b[h]], scale=eg)
                if rms_b_pending:
                    rms_b_pending.pop(0)()
                if rms_pending and (cc % 2 == 1 or samp):
                    rms_b_pending.append(rms_pending.pop(0)())
            while rms_pending or rms_b_pending:
                if rms_b_pending:
                    rms_b_pending.pop(0)()
                if rms_pending:
                    rms_b_pending.append(rms_pending.pop(0)())
            for h in range(4):
                ob = obanks[h]
                oi = osb_ctr[0] % 4
                osb_ctr[0] += 1
                act(osb[oi][:, 0:n], psb[ob][:, 0:n], AF.Copy, psb_b[ob], [osb_b[oi]])

                def rms_a(h=h, oi=oi, j=j, t0=t0, n=n, stt={}):
                    qi_ = osq_ctr[0] % 2
                    osq_ctr[0] += 1
                    stt["qi"] = qi_
                    act(osqb[qi_][:, 0:n], osb[oi][:, 0:n], AF.Square, [osb_b[oi]], [osqb_b[qi_]])

                    def rms_b():
                        pss, pss_b = pst[:].bitcast(F32), [pst_b]
                        mm(pss[:, 0:n], onesb[:], osqb[qi_][:, 0:n], True, True, [c_ones, osqb_b[qi_]], pss_b)
                        act(osq[qi_][:, 0:n], pss[:, 0:n], AF.Ln, pss_b, [osq_b[qi_]], bias=float(128.0 * RMS_EPS), scale=1.0)
                        act(osq[qi_][:, 0:n], osq[qi_][:, 0:n], AF.Exp, [osq_b[qi_]], [osq_b[qi_]], scale=-0.5)
                        vop("dve", "tensor_tensor", [osb_b[oi], osq_b[qi_]], [osq_b[qi_]], out=osq[qi_][:, 0:n], in0=osb[oi][:, 0:n],
                            in1=osq[qi_][:, 0:n], op=ALU.mult)
                        vop("dve", "scalar_tensor_tensor", [osq_b[qi_], c_gcol, mix_b[h][j]], [mix_b[h][j]], out=mix[:, h, t0:t0 + n],
                            in0=osq[qi_][:, 0:n], scalar=gcol[:, l:l + 1], in1=mix[:, h, t0:t0 + n], op0=ALU.mult, op1=ALU.mult)
                    return rms_b
                rms = rms_a
                if os.environ.get("KRMS", "1") == "1":
                    rms_pending.append(rms)
                else:
                    rms()
        while rms_pending or rms_b_pending:
            if rms_b_pending:
                rms_b_pending.pop(0)()
            if rms_pending:
                rms_b_pending.append(rms_pending.pop(0)())
        SHl.close()
        SH.close()
        stage("H%d" % l)

        sl_mq, slb_mq = load_slab(w_l, MQ, 256)
        SM = Scope("M%d_" % l)
        alloc_attn_tiles(SM)
        MKT = SM.sb("MKT", [128, 2, 256], BF16)
        MKTs = SM.sb("MKTs", [128, 2, 256], BF16)
        MVa = SM.sb("MVa", [128, 2, 4, 128], BF16)
        MVas = SM.sb("MVas", [128, 2, 4, 128], BF16)
        c_MKT, c_MKTs, c_MVa, c_MVas = SM.buf("MKT"), SM.buf("MKTs"), SM.buf("MVa"), SM.buf("MVas")
        vop("dve", "memset", [], [c_MVa], MVa[:], 1.0)
        vop("dve", "memset", [], [c_MVas], MVas[:], 1.0)
        mqT = SM.sb("mqT", [128, 2, TT], BF16)
        mqT_b = [SM.buf("mqT%d" % c) for c in range(2)]

        sl, slb = sl_mq, slb_mq
        for c in range(2):
            def f(ps, ps_bufs, j, t0, n, c=c):
                act(mqT[:, c, t0:t0 + n], ps[:, 0:n], AF.Identity, ps_bufs, [mqT_b[c]], scale=0.125)
            proj_fm(sl, slb, c * 128, 128, f)
        sl, slb = load_slab(wkv_in[l], 0, 512)
        for mb in range(2):
            ps, ps_bufs = next_acc()
            for k in range(8):
                mm(ps[:, :], memT[:, k, mb * 128:(mb + 1) * 128], sl[:, k, 0:512], k == 0, k == 7, [slb, c_memT], ps_bufs)
            st_, st_bf = next_stg()
            vop("dve", "tensor_copy", ps_bufs, [st_bf], out=st_[:], in_=ps[:, :])
            dma("sp", mkv_out[l, mb * 128:(mb + 1) * 128, :], st_[:], [st_bf], [])
            v4 = st_[:, 256:512].rearrange("p (h d) -> p h d", h=4)
            vop("pool", "tensor_copy", [st_bf], [c_MVa], out=MVa[:, mb, 0:4:2, 0:64], in_=v4[:, 0:4:2, :])
            vop("pool", "tensor_copy", [st_bf], [c_MVa], out=MVa[:, mb, 1:4:2, 64:128], in_=v4[:, 1:4:2, :])
        for c in range(2):
            ps, ps_bufs = next_acc()
            for k in range(8):
                mm(ps[:, 0:256], sl[:, k, c * 128:(c + 1) * 128], memT[:, k, :], k == 0, k == 7, [slb, c_memT], ps_bufs)
            vop("dve", "tensor_copy", ps_bufs, [c_MKT], out=MKT[:, c, :], in_=ps[:, 0:256])
        for mb in range(2):
            i = mb % 2
            dma("pool", xbt[i][:, 0:256], cmk_in[l, mb * 128:(mb + 1) * 128, :], [], [xbt_b[i]])
            for c in range(2):
                tr(pst[:, c * 128:(c + 1) * 128], xbt[i][:, c * 128:(c + 1) * 128], identb[:], [xbt_b[i], c_ident], [pst_b])
            vop("dve", "tensor_copy", [pst_b], [c_MKTs], out=MKTs[:, :, mb * 128:(mb + 1) * 128],
                in_=pst[:, 0:256].rearrange("p (c t) -> p c t", c=2))
            cmv4 = cmv_in[l, mb * 128:(mb + 1) * 128, :].rearrange("p (h d) -> p h d", h=4)
            dma("pool", MVas[:, mb, 0:4:2, 0:64], cmv4[:, 0:4:2, :], [], [c_MVas])
            dma("pool", MVas[:, mb, 1:4:2, 64:128], cmv4[:, 1:4:2, :], [], [c_MVas])
        steps = []
        for h in range(4):
            c = h // 2
            pb = 64 * (h % 2)
            for j, (t0, n) in enumerate(TTILES):
                samp = (j == 4)
                kt_, kt_bf = (MKTs, c_MKTs) if samp else (MKT, c_MKT)
                mv_, mv_bf = (MVas, c_MVas) if samp else (MVa, c_MVa)
                st = {}
                for mb in range(2):
                    def front(st=st, mb=mb, kt_=kt_, kt_bf=kt_bf, c=c, pb=pb, t0=t0, n=n):
                        ps, ps_bufs = next_score()
                        mm(ps[:, 0:n], kt_[pb:pb + 64, c, mb * 128:(mb + 1) * 128], mqT[pb:pb + 64, c, t0:t0 + n], True, True,
                           [kt_bf, mqT_b[c]], ps_bufs)
                        pt_, pt_bf = next_pt()
                        act(pt_[:, 0:n], ps[:, 0:n], AF.Exp, ps_bufs, [pt_bf])
                        st[mb] = (pt_, pt_bf)

                    def back(st=st, mb=mb, mv_=mv_, mv_bf=mv_bf, h=h, c=c, t0=t0, n=n):
                        if mb == 0:
                            st["po"] = next_oacc()
                        po, po_b = st["po"]
                        pt_, pt_bf = st[mb]
                        mm(po[:, 0:n], mv_[:, mb, h, :], pt_[:, 0:n], mb == 0, mb == 1, [mv_bf, pt_bf], po_b)
                        if mb == 1:
                            attn_finish(po, po_b, h, 6 + c, t0, n)
                    steps.append((front, back))
        score_banks[0] = [3, 4, 1]
        run_pipeline(steps, lag=2)
        score_banks[0] = [3, 4]
        SM.close()
        stage("M%d" % l)

        sl_fq, slb_fq = load_slab(w_l, FQ, 512)
        SF = Scope("F%d_" % l)
        alloc_attn_tiles(SF)
        Qa = [SF.sb("Qa%d" % h, [128, TT], BF16) for h in range(4)]
        Ka = [SF.sb("Ka%d" % h, [128, TT], BF16) for h in range(4)]
        Qa_b = [SF.buf("Qa%d" % h) for h in range(4)]
        Qaug_b = [[SF.buf("Qaug%d_%d" % (h, r_)) for r_ in range(3)] for h in range(4)]
        Ka_b = [SF.buf("Ka%d" % h) for h in range(4)]
        Kaug_b = [SF.buf("Kaug%d" % h) for h in range(4)]
        Va = SF.sb("Va", [128, NB, 4, 128], BF16)
        Va_b, Va1_b = SF.buf("Va"), SF.buf("Va1")
        negD = SF.sb("negD", [128, 4, NB])
        negDp = SF.sb("negDp", [128, 4, NPB])
        c_negD, c_negDp = SF.buf("negD"), SF.buf("negDp")
        lf = SF.sb("lf", [4, TT])
        Dsp = SF.sb("Dsp", [4, 3, TT], BF16)
        clf_t = SF.sb("clf_t", [128, 32, 4])
        clf_c = SF.sb("clf_c", [128, 32, 4])
        Dtot = SF.sb("Dtot", [4, 1])
        c_lf, c_Dsp, c_Dpast, c_Dtot = SF.buf("lf"), SF.buf("Dsp"), SF.buf("Dpast"), SF.buf("Dtot")
        KTq = SF.sb("KTq", [128, 4, 1024], BF16)
        KTq_b, KTq1_b = SF.buf("KTq"), SF.buf("KTq1")
        Ve = SF.sb("Ve", [128, 16, 128], BF16)
        Vo = SF.sb("Vo", [128, 16, 128], BF16)
        Vq_b, Vq1_b = SF.buf("Vq"), SF.buf("Vq1")
        Kc = SF.sb("Kc", [128, 8, 256], BF16)
        Kc_b = SF.buf("Kc")
        for h in range(4):
            vop("dve", "memset", [], [Kaug_b[h]], Ka[h][64:67, :], 1.0)
        vop("dve", "memset", [], [KTq1_b], KTq[64:67, :, :], 1.0)
        vop("dve", "memset", [], [Vq1_b], Ve[:, :, 64:128], 1.0)
        vop("dve", "memset", [], [Vq1_b], Vo[:, :, 0:64], 1.0)
        vop("dve", "memset", [], [Va1_b], Va[:, :, 0:4:2, 64:128], 1.0)
        vop("dve", "memset", [], [Va1_b], Va[:, :, 1:4:2, 0:64], 1.0)
        ck3 = ck_in[l].rearrange("(p b) c -> p b c", b=32)
        cvv = cv_in[l].rearrange("(p b) (m two d) -> p (b m) two d", b=32, two=2, d=64)
        pos, pos_b = psb[0], psb_b[0]

        def load_quarter_dma(qi):
            dma("pool", Kc[:], ck3[:, qi * 8:(qi + 1) * 8, :], [], [Kc_b])
            dma("pool", Ve[:, :, 0:64], cvv[:, qi * 16:(qi + 1) * 16, 0, :], [], [Vq_b])
            dma("pool", Vo[:, :, 64:128], cvv[:, qi * 16:(qi + 1) * 16, 1, :], [], [Vq_b])

        def load_quarter_tr(qi):
            for c in range(2):
                for b8 in range(8):
                    tr(pst[:, b8 * 128:(b8 + 1) * 128], Kc[:, b8, c * 128:(c + 1) * 128], identb[:], [Kc_b, c_ident], [pst_b])
                vop("dve", "tensor_copy", [pst_b], [KTq_b], out=KTq[0:64, 2 * c, :], in_=pst[0:64, :])
                act(KTq[0:64, 2 * c + 1, :], pst[64:128, :], AF.Copy, [pst_b], [KTq_b])

        sq_state = {}

        def sample_front(qi):
            ps, ps_bufs = next_score()
            for h in range(4):
                for b8 in range(8):
                    cc = (h * 8 + b8) * 16
                    mm(ps[:, cc:cc + 16], KTq[0:67, h, b8 * 128:(b8 + 1) * 128], Qa[h][0:67, TP:TT], True, True,
                       [KTq_b, KTq1_b, Qa_b[h]] + Qaug_b[h], ps_bufs)
            st_, st_bf = next_stg()
            vop("dve", "tensor_tensor", ps_bufs + [c_negDp], [st_bf], out=st_[:].rearrange("p (h b q) -> p h b q", h=4, b=8),
                in0=ps[:, :].rearrange("p (h b q) -> p h b q", h=4, b=8),
                in1=negDp[:, :, qi * 8:(qi + 1) * 8].unsqueeze(3).to_broadcast([128, 4, 8, 16]), op=ALU.add)
            pt_, pt_bf = next_pt()
            act(pt_[:], st_[:], AF.Exp, [st_bf], [pt_bf])
            sq_state[qi] = (pt_, pt_bf)

        def sample_back(qi):
            pt_, pt_bf = sq_state[qi]
            for h in range(4):
                vt = Ve if h % 2 == 0 else Vo
                for b8 in range(8):
                    cc = (h * 8 + b8) * 16
                    mm(pos[:, h * 16:(h + 1) * 16], vt[:, b8 * 2 + h // 2, :], pt_[:, cc:cc + 16], qi == 0 and b8 == 0 and h == 0, False,
                       [Vq_b, Vq1_b, pt_bf], pos_b, skip=True)
            if qi + 1 < 4:
                load_quarter_dma(qi + 1)

        load_quarter_dma(0)
        dma("sp", clf_t[:], clf_in[l].rearrange("(p b) h -> p b h", b=32), [], [c_Dpast])

        sl, slb = sl_fq, slb_fq
        for c in range(2):
            def f(ps, ps_bufs, j, t0, n, c=c):
                act(Qa[2 * c][0:64, t0:t0 + n], ps[0:64, 0:n], AF.Identity, ps_bufs, [Qa_b[2 * c]], scale=0.125)
                act(Qa[2 * c + 1][0:64, t0:t0 + n], ps[64:128, 0:n], AF.Identity, ps_bufs, [Qa_b[2 * c + 1]], scale=0.125)
            proj_fm(sl, slb, c * 128, 128, f)
        for c in range(2):
            def f(ps, ps_bufs, j, t0, n, c=c):
                vop("dve", "tensor_copy", ps_bufs, [Ka_b[2 * c]], out=Ka[2 * c][0:64, t0:t0 + n], in_=ps[0:64, 0:n])
                vop("dve", "tensor_copy", ps_bufs, [Ka_b[2 * c + 1]], out=Ka[2 * c + 1][0:64, t0:t0 + n], in_=ps[64:128, 0:n])
            proj_fm(sl, slb, 256 + c * 128, 128, f)
        sl, slb = load_slab(w_l, FL, 4)

        def f(ps, ps_bufs, j, t0, n):
            act(lf[:, t0:t0 + n], ps[0:4, 0:n], AF.Exp, ps_bufs + [c_negb], [c_lf], bias=negb[:, l:l + 1], scale=-1.0)
        proj_fm(sl, slb, 0, 4, f)

        onebc = one_c[0:4, 0:1]
        stat_stages = []

        def s1():
            act(lf[:], lf[:], AF.Ln, [c_lf], [c_lf], bias=1.0, scale=1.0)
            vop("dve", "tensor_scalar", [c_lf], [c_lf], out=lf[:], in0=lf[:], scalar1=-1.0, scalar2=None, op0=ALU.mult)
            for h in range(4):
                vop("dve", "tensor_tensor_scan", [c_Dpast, c_onec], [c_Dpast], out=clf_c[:, :, h], data0=one_c[:, 0:1].to_broadcast([128, 32]),
                    data1=clf_t[:, :, h], initial=0.0, op0=ALU.mult, op1=ALU.add)
        stat_stages.append(s1)

        def s2():
            ps, ps_bufs = next_score()
            for tb in range(NB):
                r = blk_rows(tb)
                tr(ps[0:r, tb * 4:tb * 4 + 4], lf[0:4, tb * 128:tb * 128 + r], identf[0:4, 0:4], [c_lf, c_ident], ps_bufs)
            st_, st_bf = next_stg()
            vop("dve", "tensor_copy", ps_bufs, [st_bf], out=st_[:, 0:NB * 4], in_=ps[:, 0:NB * 4])
            dma("sp", lf_out[l, 0:TP, :].rearrange("(b p) h -> p b h", p=128), st_[:, 0:64].rearrange("p (b h) -> p b h", h=4), [st_bf], [])
            dma("sp", lf_out[l, TP:TT, :], st_[0:16, 64:68], [st_bf], [])
            ps, ps_bufs = next_score()
            mm(ps[:, 0:4], sutf[:], clf_c[:, 31, :], True, True, [c_tri, c_Dpast], ps_bufs)
            mm(ps[0:4, 8:9], clf_c[:, 31, :], onesf[:, 0:1], True, True, [c_ones, c_Dpast], ps_bufs)
            vop("dve", "tensor_copy", ps_bufs, [c_Dtot], out=Dtot[:], in_=ps[0:4, 8:9])
            vop("dve", "scalar_tensor_tensor", ps_bufs + [c_Dpast], [c_negDp], out=negDp[:].rearrange("p h b -> p b h"), in0=clf_c[:], scalar=-1.0,
                in1=ps[:, 0:4].unsqueeze(1).to_broadcast([128, 32, 4]), op0=ALU.mult, op1=ALU.subtract)
        stat_stages.append(s2)

        def s3():
            vop("dve", "tensor_tensor_scan", [c_lf, c_onec], [c_lf], out=lf[:, 0:TP], data0=onebc.to_broadcast([4, TP]),
                data1=lf[:, 0:TP], initial=0.0, op0=ALU.mult, op1=ALU.add)
            vop("dve", "tensor_tensor_scan", [c_lf, c_onec, c_Dtot], [c_lf], out=lf[:, TP:TT], data0=onebc.to_broadcast([4, TS]),
                data1=lf[:, TP:TT], initial=Dtot[:, 0:1], op0=ALU.mult, op1=ALU.add)
        stat_stages.append(s3)

        def s4():
            ps, ps_bufs = next_score()
            for tb in range(NB):
                r = blk_rows(tb)
                tr(ps[0:r, tb * 4:tb * 4 + 4], lf[0:4, tb * 128:tb * 128 + r], identf[0:4, 0:4], [c_lf, c_ident], ps_bufs)
            vop("dve", "tensor_scalar", ps_bufs, [c_negD], out=negD[:].rearrange("p h b -> p b h"),
                in0=ps[:, 0:NB * 4].rearrange("p (b h) -> p b h", h=4), scalar1=-1.0, scalar2=None, op0=ALU.mult)
        stat_stages.append(s4)

        def s5():
            vop("dve", "tensor_copy", [c_lf], [c_Dsp], out=Dsp[:, 0, :], in_=lf[:])
            vop("dve", "tensor_tensor", [c_lf, c_Dsp], [c_lf], out=lf[:], in0=lf[:], in1=Dsp[:, 0, :], op=ALU.subtract)
            vop("dve", "tensor_copy", [c_lf], [c_Dsp], out=Dsp[:, 1, :], in_=lf[:])
            vop("dve", "tensor_tensor", [c_lf, c_Dsp], [c_lf], out=lf[:], in0=lf[:], in1=Dsp[:, 1, :], op=ALU.subtract)
            vop("dve", "tensor_copy", [c_lf], [c_Dsp], out=Dsp[:, 2, :], in_=lf[:])
            for h in range(4):
                for r_ in range(3):
                    dma("sp", Qa[h][64 + r_:65 + r_, :], Dsp[h:h + 1, r_, :], [c_Dsp], [Qaug_b[h][r_]])
        stat_stages.append(s5)

        sl, slb = load_slab(w_l, FK, 512)
        for tb in range(NB):
            r = blk_rows(tb)
            ps, ps_bufs = next_acc()
            for k in range(8):
                mm(ps[0:r, :], xT[:, k, tb * 128:tb * 128 + r], sl[:, k, 0:512], k == 0, k == 7, [slb, xT_b[tb]], ps_bufs)
            st_, st_bf = next_stg()
            act(st_[0:r, :], ps[0:r, :], AF.Copy, ps_bufs, [st_bf])
            dma("sp", kv_out[l, tb * 128:tb * 128 + r, :], st_[0:r, :], [st_bf], [])
            v4 = st_[0:r, 256:512].rearrange("p (h d) -> p h d", h=4)
            vop("pool", "tensor_copy", [st_bf], [Va_b], out=Va[0:r, tb, 0:4:2, 0:64], in_=v4[:, 0:4:2, :])
            vop("pool", "tensor_copy", [st_bf], [Va_b], out=Va[0:r, tb, 1:4:2, 64:128], in_=v4[:, 1:4:2, :])
            if tb % 3 == 1 and stat_stages:
                stat_stages.pop(0)()
        while stat_stages:
            stat_stages.pop(0)()
        load_quarter_tr(0)
        dma("pool", slab_big[:], wout_in[l].rearrange("(k p) c -> p k c", p=128), [], [slab_b[0], slab_b[1]])
        stage("F1%d" % l)

        nk_state = {}

        def newkeys_front():
            ps, ps_bufs = next_score()
            for h in range(4):
                mm(ps[0:16, h * 16:(h + 1) * 16], Ka[h][0:67, TP:TT], Qa[h][0:67, TP:TT], True, True,
                   [Ka_b[h], Kaug_b[h], Qa_b[h]] + Qaug_b[h], ps_bufs)
            vop("dve", "tensor_tensor", ps_bufs + [c_mneg], ps_bufs, out=ps[0:16, 0:64].rearrange("p (h q) -> p h q", h=4),
                in0=ps[0:16, 0:64].rearrange("p (h q) -> p h q", h=4), in1=mneg[0:16, 0:16].unsqueeze(1).to_broadcast([16, 4, 16]), op=ALU.add)
            pt_, pt_bf = next_pt()
            for h in range(4):
                act(pt_[0:16, h * 16:(h + 1) * 16], ps[0:16, h * 16:(h + 1) * 16], AF.Exp, ps_bufs + [c_negD], [pt_bf],
                    bias=negD[0:16, h, 16:17], scale=1.0)
            nk_state["pt"] = (pt_, pt_bf)

        def newkeys_back():
            pt_, pt_bf = nk_state["pt"]
            for h in range(4):
                mm(pos[:, h * 16:(h + 1) * 16], Va[0:16, 16, h, :], pt_[0:16, h * 16:(h + 1) * 16], False, True, [Va_b, Va1_b, pt_bf], pos_b, skip=True)
            for h in range(4):
                attn_finish(pos, pos_b, h, 4 + h // 2, TP, TS, c0=h * 16)

        steps = []
        for h in range(4):
            for j in range(4):
                nkb = 4 * j + 4
                st = {}
                if j == 0 and h >= 1:
                    steps.append((lambda h=h: load_quarter_tr(h), None))
                if j == 1:
                    steps.append((lambda h=h: sample_front(h), lambda h=h: sample_back(h)))
                if j == 3 and h == 3:
                    steps.append((newkeys_front, newkeys_back))
                for kb in range(nkb):
                    q0 = max(512 * j, 128 * kb)
                    n = 512 * (j + 1) - q0

                    def front(st=st, h=h, j=j, kb=kb, q0=q0, n=n):
                        ps, ps_bufs = next_score()
                        mm(ps[:, 0:n], Ka[h][0:67, kb * 128:(kb + 1) * 128], Qa[h][0:67, q0:q0 + n], True, True,
                           [Ka_b[h], Kaug_b[h], Qa_b[h]] + Qaug_b[h], ps_bufs)
                        if kb >= 4 * j:
                            vop("dve", "tensor_tensor", ps_bufs + [c_mneg], ps_bufs, out=ps[:, 0:128], in0=ps[:, 0:128], in1=mneg[:], op=ALU.add)
                        pt_, pt_bf = next_pt()
                        act(pt_[:, 0:n], ps[:, 0:n], AF.Exp, ps_bufs + [c_negD], [pt_bf], bias=negD[:, h, kb:kb + 1], scale=1.0)
                        st[kb] = (pt_, pt_bf)

                    def back(st=st, h=h, j=j, kb=kb, q0=q0, n=n, nkb=nkb):
                        if kb == 0:
                            st["po"] = next_oacc()
                        po, po_b = st["po"]
                        pt_, pt_bf = st[kb]
                        mm(po[:, q0 - 512 * j:512], Va[:, kb, h, :], pt_[:, 0:n], kb == 0, kb == nkb - 1, [Va_b, Va1_b, pt_bf], po_b)
                        if kb == nkb - 1:
                            attn_finish(po, po_b, h, 4 + h // 2, 512 * j, 512)
                        keep_warm(NWARM)
                    steps.append((front, back))

        score_banks[0] = [3, 4, 1, 2]
        run_pipeline(steps, lag=int(os.environ.get("KLAGF", "4")))
        score_banks[0] = [3, 4]
        SF.close()
        stage("F%d" % l)

        SL = Scope("L%d_" % l)
        lng_t = SL.sb("lng_t", [128, D])
        lnb_t = SL.sb("lnb_t", [128, D])
        c_lng, c_lnb = SL.buf("lng"), SL.buf("lnb")
        NR = 3
        xr_t = [SL.sb("xr%d" % i, [128, D]) for i in range(NR)]
        xr_b = [SL.buf("xr%d" % i) for i in range(NR)]
        r_t = [SL.sb("r%d" % i, [128, D]) for i in range(NR)]
        r_b = [SL.buf("r%d" % i) for i in range(NR)]
        y_t = [SL.sb("y%d" % i, [128, D]) for i in range(NR)]
        y_b = [SL.buf("y%d" % i) for i in range(NR)]
        yb_t = [SL.sb("yb%d" % i, [128, D], BF16) for i in range(NB)]
        yb_b = [SL.buf("yb%d" % i) for i in range(NB)]
        bst = [SL.sb("bst%d" % i, [128, 2, 6]) for i in range(NR)]
        mv_t = [SL.sb("mv%d" % i, [128, 4]) for i in range(NR)]
        st_b = [SL.buf("st%d" % i) for i in range(NR)]
        wo = slab_big
        dma("sp", lng_t[:], lng_in[l:l + 1, :].partition_broadcast(128), [], [c_lng])
        dma("sp", lnb_t[:], lnb_in[l:l + 1, :].partition_broadcast(128), [], [c_lnb])
        last = (l == DEPTH - 1)
        bank_pairs = [(0, 1), (2, 3), (4, 5)]

        def stage_a(tb):
            r = blk_rows(tb)
            i = tb % NR
            j = tb // 4
            banks = bank_pairs[i]
            for hf in range(2):
                ps, ps_bufs = psb[banks[hf]], psb_b[banks[hf]]
                for k in range(8):
                    mm(ps[0:r, :], mix[:, k, tb * 128:tb * 128 + r], wo[:, k, hf * 512:(hf + 1) * 512], k == 0, k == 7,
                       [mix_b[k][j], slab_b[0], slab_b[1]], ps_bufs)

        def load_xr(tb):
            r = blk_rows(tb)
            i = tb % NR
            if l == 0:
                dma("pool", xr_t[i][0:r, :], x_all[tb * 128:tb * 128 + r, :], [], [xr_b[i]])
            else:
                dma("pool", xr_t[i][0:r, :], xres[tb * 128:tb * 128 + r, :], [xres_b[tb]], [xr_b[i]])

        def stage_a2(tb):
            r = blk_rows(tb)
            i = tb % NR
            banks = bank_pairs[i]
            for hf in range(2):
                ps, ps_bufs = psb[banks[hf]], psb_b[banks[hf]]
                vop("dve", "scalar_tensor_tensor", ps_bufs + [xr_b[i]], [r_b[i]], out=r_t[i][0:r, hf * 512:(hf + 1) * 512],
                    in0=xr_t[i][0:r, hf * 512:(hf + 1) * 512], scalar=float(DN_ALPHA), in1=ps[0:r, :], op0=ALU.mult, op1=ALU.add)
                vop("dve", "bn_stats", [r_b[i]], [st_b[i]], out=bst[i][0:r, hf, :], in_=r_t[i][0:r, hf * 512:(hf + 1) * 512])
            vop("dve", "bn_aggr", [st_b[i]], [st_b[i]], out=mv_t[i][0:r, 0:2], in_=bst[i][0:r, :, :])
            act(mv_t[i][0:r, 2:3], mv_t[i][0:r, 1:2], AF.Ln, [st_b[i]], [st_b[i]], bias=float(LN_EPS), scale=1.0)
            act(mv_t[i][0:r, 2:3], mv_t[i][0:r, 2:3], AF.Exp, [st_b[i]], [st_b[i]], scale=-0.5)
            vop("dve", "scalar_tensor_tensor", [st_b[i]], [st_b[i]], out=mv_t[i][0:r, 3:4], in0=mv_t[i][0:r, 0:1], scalar=-1.0,
                in1=mv_t[i][0:r, 2:3], op0=ALU.mult, op1=ALU.mult)
            act(y_t[i][0:r, :], r_t[i][0:r, :], AF.Identity, [r_b[i], st_b[i]], [y_b[i]], bias=mv_t[i][0:r, 3:4], scale=mv_t[i][0:r, 2:3])

        def stage_b(tb):
            r = blk_rows(tb)
            i = tb % NR
            vop("pool", "tensor_tensor", [y_b[i], c_lng], [y_b[i]], out=y_t[i][0:r, :], in0=y_t[i][0:r, :], in1=lng_t[0:r, :], op=ALU.mult)
            vop("dve", "tensor_tensor", [y_b[i], c_lnb], [y_b[i]], out=y_t[i][0:r, 0:512], in0=y_t[i][0:r, 0:512], in1=lnb_t[0:r, 0:512], op=ALU.add)
            vop("pool", "tensor_tensor", [y_b[i], c_lnb], [y_b[i]], out=y_t[i][0:r, 512:D], in0=y_t[i][0:r, 512:D], in1=lnb_t[0:r, 512:D], op=ALU.add)
            if last:
                dma("sp", y_out[tb * 128:tb * 128 + r, :], y_t[i][0:r, :], [y_b[i]], [])
            else:
                dma("sp", xres[tb * 128:tb * 128 + r, :], y_t[i][0:r, :], [y_b[i]], [xres_b[tb]])

        def stage_c1(tb):
            if last:
                return
            r = blk_rows(tb)
            i = tb % NR
            act(yb_t[tb][0:r, :], y_t[i][0:r, :], AF.Copy, [y_b[i]], [yb_b[tb]])

        def stage_c2(tb):
            if last:
                return
            transpose_block_to(xT, [xT_b[tb]], yb_t[tb], yb_b[tb], blk_rows(tb), tb * 128)

        for t in range(NB + 2):
            if t == 0:
                load_xr(0)
                load_xr(1)
            if t + 2 < NB:
                load_xr(t + 2)
            if t < NB:
                stage_a(t)
                stage_a2(t)
            if 0 <= t - 1 < NB:
                stage_b(t - 1)
            if 0 <= t - 2 < NB:
                stage_c1(t - 2)
        for t in range(NB):
            stage_c2(t)
        SL.close()
        stage("L%d" % l)

    try:
        for l in range(DEPTH):
            layer_body(l)
    except _Stop:
        for sc in reversed(list(open_scopes)):
            sc.close()
    P.emit(nc, es)
    es.close()
    return nc


_NC_CACHE = {}


def kernel(x_prompt, x_sample, mem_prompt, state_hgrn, cache_fox_k, cache_fox_v, cache_fox_logf,
           cache_mem_k, cache_mem_v, w_in, b_fox_forget, hgrn_lower_bounds, hgrn_norm_g,
           w_mem_kv, w_out, ln_g, ln_b):
    f = lambda a: np.ascontiguousarray(np.asarray(a, dtype=np.float32))
    x_prompt, x_sample, mem_prompt = f(x_prompt), f(x_sample), f(mem_prompt)
    state_hgrn, cache_fox_k, cache_fox_v, cache_fox_logf = f(state_hgrn), f(cache_fox_k), f(cache_fox_v), f(cache_fox_logf)
    cache_mem_k, cache_mem_v = f(cache_mem_k), f(cache_mem_v)
    shared = dict(w_in=f(w_in), bff=f(b_fox_forget), hlb=f(hgrn_lower_bounds), hng=f(hgrn_norm_g), wkv=f(w_mem_kv),
                  wout=f(w_out), lng=f(ln_g), lnb=f(ln_b))
    if "nc" not in _NC_CACHE:
        _NC_CACHE["nc"] = build_nc()
    nc = _NC_CACHE["nc"]
    in_maps = []
    for b in range(8):
        m = dict(shared)
        m["x_all"] = np.ascontiguousarray(np.concatenate([x_prompt[b], x_sample[b]], axis=0))
        m["mem"] = mem_prompt[b]
        m["st"] = np.ascontiguousarray(state_hgrn[:, b])
        m["ck"] = np.ascontiguousarray(cache_fox_k[:, b].reshape(DEPTH, PAST, 256))
        m["cv"] = np.ascontiguousarray(cache_fox_v[:, b].reshape(DEPTH, PAST, 256))
        m["clf"] = np.ascontiguousarray(cache_fox_logf[:, b])
        m["cmk"] = np.ascontiguousarray(cache_mem_k[:, b].reshape(DEPTH, 256, 256))
        m["cmv"] = np.ascontiguousarray(cache_mem_v[:, b].reshape(DEPTH, 256, 256))
        in_maps.append(m)
    res = run_bass_kernel_spmd(nc, in_maps, core_ids=list(range(8)))
    R = res.results
    y_all = np.stack([R[b]["y_all"] for b in range(8)])
    hst = np.stack([R[b]["hst"] for b in range(8)], axis=2)
    kv = np.stack([R[b]["kv_all"] for b in range(8)], axis=1)
    lf = np.stack([R[b]["lf_all"] for b in range(8)], axis=1)
    mkv = np.stack([R[b]["mkv"] for b in range(8)], axis=1)
    c = np.ascontiguousarray
    y_prompt = c(y_all[:, :TP])
    y_sample = c(y_all[:, TP:])
    p_state = c(hst[:, 0])
    s_state = c(hst[:, 1])
    p_k = c(kv[:, :, :TP, 0:256]).reshape(DEPTH, 8, TP, 4, 64)
    p_v = c(kv[:, :, :TP, 256:512]).reshape(DEPTH, 8, TP, 4, 64)
    s_k = c(kv[:, :, TP:, 0:256]).reshape(DEPTH, 8, TS, 4, 64)
    s_v = c(kv[:, :, TP:, 256:512]).reshape(DEPTH, 8, TS, 4, 64)
    p_lf = c(lf[:, :, :TP])
    s_lf = c(lf[:, :, TP:])
    p_mk = c(mkv[:, :, :, 0:256]).reshape(DEPTH, 8, 256, 4, 64)
    p_mv = c(mkv[:, :, :, 256:512]).reshape(DEPTH, 8, 256, 4, 64)
    return (y_prompt, y_sample, p_state, p_k, p_v, p_lf, p_mk, p_mv, s_state, s_k, s_v, s_lf)
```

```python
import os
import numpy as np
from contextlib import ExitStack
import concourse.bass as bass
import concourse.mybir as mybir
from concourse.bass_utils import run_bass_kernel_spmd

F32 = mybir.dt.float32
BF16 = mybir.dt.bfloat16
AF = mybir.ActivationFunctionType
ALU = mybir.AluOpType
AX = mybir.AxisListType

D = 1024
TP = 2048
TS = 16
TT = TP + TS
NB = 17
PAST = 4096
NPB = PAST // 128
DEPTH = 2
INW = 3588
HQ, HF, HI, HG, FQ, FK, FV, FG, FL, MQ, MG = 0, 512, 1024, 1536, 2048, 2304, 2560, 2816, 3072, 3076, 3332
TTILES = [(0, 512), (512, 512), (1024, 512), (1536, 512), (2048, 16)]
DN_ALPHA = (2 * DEPTH) ** 0.25
LN_EPS = 1e-5
RMS_EPS = 1e-6
N_DMA_SEMS = 56


def blk_rows(tb):
    return 128 if tb < 16 else 16


class Buf:
    __slots__ = ("name", "last_w", "rd", "rd_dma")

    def __init__(self, name):
        self.name = name
        self.last_w = None
        self.rd = {}
        self.rd_dma = []


class Op:
    __slots__ = ("eng", "meth", "args", "kwargs", "dma", "deps", "needed", "sem", "val", "idx", "prev_dma")


def flat(x):
    out = []
    for e in x:
        if isinstance(e, (list, tuple)):
            out.extend(flat(e))
        elif e is not None:
            out.append(e)
    return out


class Prog:
    def __init__(self):
        self.ops = []

    def add(self, eng, meth, args=(), kwargs=None, reads=(), writes=(), dma=False):
        op = Op()
        op.eng, op.meth, op.args, op.kwargs, op.dma = eng, meth, args, (kwargs or {}), dma
        op.idx = len(self.ops)
        op.needed = dma
        op.sem = None
        op.val = None
        op.prev_dma = None
        deps = set()
        reads = flat(reads)
        writes = flat(writes)
        ops = self.ops

        def consider(d, raw):
            if d is None:
                return
            o = ops[d]
            if (not raw) and (not dma) and (not o.dma) and o.eng == eng:
                return
            deps.add(d)

        for b in reads:
            consider(b.last_w, True)
        for b in writes:
            consider(b.last_w, False)
            for r in b.rd.values():
                consider(r, False)
            for r in b.rd_dma:
                consider(r, False)
        op.deps = deps
        for b in reads:
            if dma:
                b.rd_dma.append(op.idx)
            else:
                b.rd[eng] = op.idx
        for b in writes:
            b.last_w = op.idx
            b.rd = {}
            b.rd_dma = []
        ops.append(op)
        return op

    def emit(self, nc, es):
        ops = self.ops
        for op in ops:
            for d in op.deps:
                ops[d].needed = True
        engs = {"pe": nc.tensor, "act": nc.scalar, "dve": nc.vector, "pool": nc.gpsimd, "sp": nc.sync}
        esem = {k: es.enter_context(nc.semaphore("sem_" + k)) for k in ("pe", "act", "dve", "pool")}
        dsem = [es.enter_context(nc.semaphore("dsem%d" % i)) for i in range(N_DMA_SEMS)]
        cnt = {k: 0 for k in esem}
        dcount = [0] * N_DMA_SEMS
        dlast = [None] * N_DMA_SEMS
        NSP = 24
        ndma = {"sp": 0, "pool": 0}
        for op in ops:
            if op.dma:
                if op.eng == "sp":
                    s = ndma["sp"] % NSP
                else:
                    s = NSP + ndma["pool"] % (N_DMA_SEMS - NSP)
                ndma[op.eng] += 1
                dcount[s] += 16
                op.sem, op.val = dsem[s], dcount[s]
                op.prev_dma = dlast[s]
                dlast[s] = op.idx
            elif op.needed:
                cnt[op.eng] += 1
                op.sem, op.val = esem[op.eng], cnt[op.eng]
        per_eng = {k: [] for k in engs}
        for op in ops:
            per_eng[op.eng].append(op)
        block = es.enter_context(nc.Block())

        def run(ename):
            def body(e):
                waited = {}
                nwait = 0
                for op in per_eng[ename]:
                    deps = set(op.deps)
                    if op.prev_dma is not None:
                        deps.add(op.prev_dma)
                    need = {}
                    for d in deps:
                        o = ops[d]
                        key = id(o.sem)
                        if waited.get(key, 0) >= o.val:
                            continue
                        if key not in need or need[key][1] < o.val:
                            need[key] = (o.sem, o.val)
                    for key, (s, v) in need.items():
                        e.wait_ge(s, v)
                        waited[key] = v
                        nwait += 1
                    ins = getattr(e, op.meth)(*op.args, **op.kwargs)
                    if op.sem is not None:
                        ins.then_inc(op.sem, 16 if op.dma else 1)
                if ename == "sp":
                    for i in range(N_DMA_SEMS):
                        if dcount[i] > 0:
                            e.wait_ge(dsem[i], dcount[i])
                print("engine %s: %d ops, %d waits" % (ename, len(per_eng[ename]), nwait))
            return body

        block.tensor(run("pe"))
        block.scalar(run("act"))
        block.vector(run("dve"))
        block.gpsimd(run("pool"))
        block.sync(run("sp"))


def build_nc():
    nc = bass.Bass("TRN2", target_bir_lowering=False)
    es = ExitStack()
    P = Prog()

    def din(name, shape):
        return nc.dram_tensor(name, list(shape), F32, kind="ExternalInput").ap()

    def dout(name, shape):
        return nc.dram_tensor(name, list(shape), F32, kind="ExternalOutput").ap()

    x_all = din("x_all", [TT, D])
    mem_in = din("mem", [256, D])
    st_in = din("st", [DEPTH, 4, 128, 128])
    ck_in = din("ck", [DEPTH, PAST, 256])
    cv_in = din("cv", [DEPTH, PAST, 256])
    clf_in = din("clf", [DEPTH, PAST, 4])
    cmk_in = din("cmk", [DEPTH, 256, 256])
    cmv_in = din("cmv", [DEPTH, 256, 256])
    w_in = din("w_in", [DEPTH, D, INW])
    bff_in = din("bff", [DEPTH, 4])
    hlb_in = din("hlb", [DEPTH, 512])
    hng_in = din("hng", [DEPTH, 128])
    wkv_in = din("wkv", [DEPTH, D, 512])
    wout_in = din("wout", [DEPTH, D, D])
    lng_in = din("lng", [DEPTH, D])
    lnb_in = din("lnb", [DEPTH, D])

    y_out = dout("y_all", [TT, D])
    hst_out = dout("hst", [DEPTH, 2, 4, 128, 128])
    kv_out = dout("kv_all", [DEPTH, TT, 512])
    lf_out = dout("lf_all", [DEPTH, TT, 4])
    mkv_out = dout("mkv", [DEPTH, 256, 512])
    xres = nc.dram_tensor("xres_scratch", [TT, D], F32).ap()

    def sb(name, shape, dt=F32):
        return es.enter_context(nc.sbuf_tensor(name, list(shape), dt))

    fence_t = sb("fence_t", [128, 1])
    fence_s = sb("fence_s", [128, 1])
    c_fence = Buf("fence_src")
    P.add("pool", "memset", (fence_s[:], 0.0), {}, [], [c_fence])
    closed_bufs = []
    open_scopes = []

    class Scope:
        def __init__(self, tag):
            self.tag = tag
            self.es = ExitStack()
            self.bufs = []
            self.fence = None
            open_scopes.append(self)
            if closed_bufs:
                old = list(closed_bufs)
                del closed_bufs[:]
                self.fence = P.add("sp", "dma_start", (), dict(out=fence_t[0:1, 0:1], in_=fence_s[0:1, 0:1]), old + [c_fence], old, dma=True)

        def sb(self, name, shape, dt=F32):
            return self.es.enter_context(nc.sbuf_tensor(self.tag + name, list(shape), dt))

        def buf(self, name):
            b = Buf(self.tag + name)
            if self.fence is not None:
                b.last_w = self.fence.idx
            self.bufs.append(b)
            return b

        def close(self):
            closed_bufs.extend(self.bufs)
            self.es.close()
            open_scopes.remove(self)

    def psum(name, shape, dt=F32):
        return es.enter_context(nc.psum_tensor(name, list(shape), dt))

    def mm(out, lhsT, rhs, start, stop, reads, writes, skip=False):
        kw = dict(lhsT=lhsT, rhs=rhs, start=start, stop=stop)
        if skip:
            kw["skip_group_check"] = True
        P.add("pe", "matmul", (out,), kw, reads, writes)

    def tr(out, in_, ident, reads, writes):
        P.add("pe", "transpose", (out, in_, ident), {}, reads, writes)

    def act(out, in_, func, reads, writes, bias=None, scale=None):
        kw = dict(out=out, in_=in_, func=func)
        if bias is not None:
            kw["bias"] = bias
        if scale is not None:
            kw["scale"] = scale
        P.add("act", "activation", (), kw, reads, writes)

    def vop(eng, meth, reads, writes, *args, **kwargs):
        P.add(eng, meth, args, kwargs, reads, writes)

    def dma(eng, out, in_, reads, writes, **kw):
        P.add(eng, "dma_start", (), dict(out=out, in_=in_, **kw), reads, writes, dma=True)

    NBANK = 7
    psb = [psum("psb%d" % i, [128, 512]) for i in range(NBANK)]
    psb_b = [[Buf("psb%d" % i)] * 8 for i in range(NBANK)]
    pst = psum("pst", [128, 1024], BF16)
    pst_b = Buf("pst")

    identb = sb("identb", [128, 128], BF16)
    identf = sb("identf", [128, 128])
    trif = sb("trif", [128, 128])
    trib = sb("trib", [128, 128], BF16)
    onesf = sb("onesf", [128, 128])
    scanm = sb("scanm", [128, 512])
    c_ident, c_tri, c_ones, c_scanm = Buf("ident"), Buf("tri"), Buf("ones"), Buf("scanm")

    vop("pool", "memset", [], [c_ident], identf[:], 0.0)
    vop("pool", "affine_select", [c_ident], [c_ident], out=identf[:], in_=identf[:], pattern=[[-1, 128]],
        compare_op=ALU.not_equal, fill=1.0, base=0, channel_multiplier=1)
    vop("pool", "tensor_copy", [c_ident], [c_ident], out=identb[:], in_=identf[:])
    vop("pool", "memset", [], [c_tri], trif[:], 1.0)
    vop("pool", "affine_select", [c_tri], [c_tri], out=trif[:], in_=trif[:], pattern=[[1, 128]],
        compare_op=ALU.is_ge, fill=0.0, base=0, channel_multiplier=-1)
    vop("pool", "tensor_copy", [c_tri], [c_tri], out=trib[:], in_=trif[:])
    vop("pool", "memset", [], [c_ones], onesf[:], 1.0)
    mneg = sb("mneg", [128, 128])
    c_mneg = Buf("mneg")
    vop("pool", "memset", [], [c_mneg], mneg[:], 0.0)
    vop("pool", "affine_select", [c_mneg], [c_mneg], out=mneg[:], in_=mneg[:], pattern=[[1, 128]],
        compare_op=ALU.is_ge, fill=-1.0e30, base=0, channel_multiplier=-1)
    onesb = sb("onesb", [128, 128], BF16)
    vop("pool", "tensor_copy", [c_ones], [c_ones], out=onesb[:], in_=onesf[:])
    sutf = sb("sutf", [128, 128])
    vop("pool", "tensor_tensor", [c_tri, c_ident], [c_tri], out=sutf[:], in0=trif[:], in1=identf[:], op=ALU.subtract)
    vop("pool", "memset", [], [c_scanm], scanm[:], 1.0)
    vop("pool", "memset", [c_scanm], [c_scanm], scanm[:].rearrange("p (c k) -> p c k", k=64)[:, :, 0:1], 0.0)

    lbt = sb("lbt", [128, 4, 2])
    lbe = sb("lbe", [128, 4, 2])
    lbs = sb("lbs", [128, 4, 1])
    low = sb("low", [128, 4, 2])
    lnoml = sb("lnoml", [128, 4, 2])
    c_lb = Buf("lb")
    with nc.allow_non_contiguous_dma(reason="tiny parameter vectors"):
        pass
    for l_ in range(DEPTH):
        for h_ in range(4):
            dma("sp", lbt[:, h_, l_:l_ + 1], hlb_in[l_:l_ + 1, h_ * 128:(h_ + 1) * 128].rearrange("o p -> p o"), [], [c_lb],
                allow_slow_non_contiguous=True)
    act(lbe[:], lbt[:], AF.Exp, [c_lb], [c_lb])
    vop("dve", "tensor_tensor", [c_lb], [c_lb], out=lbs[:], in0=lbe[:, :, 0:1], in1=lbe[:, :, 1:2], op=ALU.add)
    vop("dve", "reciprocal", [c_lb], [c_lb], out=lbs[:], in_=lbs[:])
    vop("dve", "tensor_tensor", [c_lb], [c_lb], out=lbe[:], in0=lbe[:], in1=lbs[:].to_broadcast([128, 4, 2]), op=ALU.mult)
    vop("dve", "tensor_tensor", [c_lb], [c_lb], out=low[:, :, 0:1], in0=lbe[:, :, 0:1], in1=lbe[:, :, 0:1], op=ALU.subtract)
    vop("dve", "tensor_tensor", [c_lb], [c_lb], out=low[:, :, 1:2], in0=lbe[:, :, 0:1], in1=lbe[:, :, 1:2], op=ALU.add)
    vop("dve", "tensor_tensor", [c_lb], [c_lb], out=low[:, :, 1:2], in0=low[:, :, 1:2], in1=lbe[:, :, 0:1], op=ALU.subtract)
    act(lnoml[:], low[:], AF.Ln, [c_lb], [c_lb], bias=1.0, scale=-1.0)

    negb = sb("negb", [4, 2])
    c_negb = Buf("negb")
    dma("sp", negb[:], bff_in.rearrange("l h -> h l"), [], [c_negb], allow_slow_non_contiguous=True)
    vop("dve", "tensor_scalar", [c_negb], [c_negb], out=negb[:], in0=negb[:], scalar1=-1.0, scalar2=None, op0=ALU.mult)
    gcol = sb("gcol", [128, 2])
    c_gcol = Buf("gcol")
    dma("sp", gcol[:], hng_in.rearrange("l p -> p l"), [], [c_gcol], allow_slow_non_contiguous=True)
    vop("dve", "tensor_scalar", [c_gcol], [c_gcol], out=gcol[:], in0=gcol[:], scalar1=float(np.sqrt(128.0)), scalar2=None, op0=ALU.mult)

    xT = sb("xT", [128, 8, TT], BF16)
    xT_b = [Buf("xT%d" % i) for i in range(NB)]
    mix = sb("mix", [128, 8, TT], BF16)
    mix_b = [[Buf("mix%d_%d" % (c, j)) for j in range(5)] for c in range(8)]

    def xT_bufs(t0, n):
        return [xT_b[i] for i in range(t0 // 128, (t0 + n - 1) // 128 + 1)]

    NSLAB = 2
    slab_big = sb("slab", [128, 8, 1024], BF16)
    slab = [slab_big[:, :, i * 512:(i + 1) * 512] for i in range(NSLAB)]
    slab_b = [Buf("slab%d" % i) for i in range(NSLAB)]
    slab_ctr = [0]

    def load_slab(src2d, c0, n):
        i = slab_ctr[0] % NSLAB
        slab_ctr[0] += 1
        dma("pool", slab[i][:, :, 0:n], src2d.rearrange("(k p) c -> p k c", p=128)[:, :, c0:c0 + n], [], [slab_b[i]])
        return slab[i], slab_b[i]

    acc_ctr = [0]

    def next_acc():
        i = acc_ctr[0] % 3
        acc_ctr[0] += 1
        return psb[i], psb_b[i]

    xbt = [sb("xbt%d" % i, [128, D], BF16) for i in range(2)]
    xbt_b = [Buf("xbt%d" % i) for i in range(2)]

    tr_ctr = [0]

    def transpose_block_to(dst, dst_bufs, src_tile, src_buf, rows, col0):
        w = tr_ctr[0] % 2
        tr_ctr[0] += 1
        if w == 0:
            pt, pt_b = pst[:], [pst_b]
        else:
            pt, pt_b = psb[6][:].bitcast(BF16), psb_b[6]
        for k in range(8):
            tr(pt[:, k * 128:k * 128 + rows], src_tile[0:rows, k * 128:(k + 1) * 128], identb[0:rows, 0:rows],
               [src_buf, c_ident], pt_b)
        src3 = pt.rearrange("p (k t) -> p k t", k=8)[:, :, 0:rows]
        if w == 0:
            vop("dve", "tensor_copy", pt_b, dst_bufs, out=dst[:, :, col0:col0 + rows], in_=src3)
        else:
            act(dst[:, :, col0:col0 + rows], src3, AF.Copy, pt_b, dst_bufs)

    def build_xT_block(tb):
        r = blk_rows(tb)
        i = (tb // 2) % 2
        dma("pool", xbt[i][0:r, :], x_all[tb * 128:tb * 128 + r, :], [], [xbt_b[i]])
        transpose_block_to(xT, [xT_b[tb]], xbt[i], xbt_b[i], r, tb * 128)

    memT = sb("memT", [128, 8, 256], BF16)
    c_memT = Buf("memT")

    def build_memT():
        for mb in range(2):
            i = mb % 2
            dma("pool", xbt[i][:, :], mem_in[mb * 128:(mb + 1) * 128, :], [], [xbt_b[i]])
            transpose_block_to(memT, [c_memT], xbt[i], xbt_b[i], 128, mb * 128)

    one_c = sb("one_c", [128, 1])
    c_onec = Buf("onec")
    vop("pool", "memset", [], [c_onec], one_c[:], 1.0)

    ATT = {}

    def alloc_attn_tiles(sc):
        ATT["PT"] = [sc.sb("PT%d" % i, [128, 512], BF16) for i in range(5)]
        ATT["PT_b"] = [sc.buf("PT%d" % i) for i in range(5)]
        ATT["stg"] = [sc.sb("stg%d" % i, [128, 512]) for i in range(3)]
        ATT["stg_b"] = [sc.buf("stg%d" % i) for i in range(3)]
        ATT["rs"] = [sc.sb("rs%d" % i, [128, 512]) for i in range(2)]
        ATT["rs_b"] = [sc.buf("rs%d" % i) for i in range(2)]

    pt_ctr = [0]
    stg_ctr = [0]

    def next_stg():
        i = stg_ctr[0] % 3
        stg_ctr[0] += 1
        return ATT["stg"][i], ATT["stg_b"][i]

    def next_pt():
        i = pt_ctr[0] % 5
        pt_ctr[0] += 1
        return ATT["PT"][i], ATT["PT_b"][i]

    sc_ctr = [0]

    score_banks = [[3, 4]]

    def next_score():
        bl = score_banks[0]
        i = bl[sc_ctr[0] % len(bl)]
        sc_ctr[0] += 1
        return psb[i], psb_b[i]

    oacc_ctr = [0]

    def next_oacc():
        i = (5, 6)[oacc_ctr[0] % 2]
        oacc_ctr[0] += 1
        return psb[i], psb_b[i]

    rs_ctr = [0]
    recip_on_dve = [False]

    def attn_finish(po, po_b, h, ci, t0, n, c0=0):
        pb = 64 * (h % 2)
        ob = 64 - pb
        i = rs_ctr[0] % 2
        rs_ctr[0] += 1
        j = t0 // 512
        rs_t, rs_b = ATT["rs"], ATT["rs_b"]
        if recip_on_dve[0]:
            vop("dve", "reciprocal", po_b, [rs_b[i]], out=rs_t[i][pb:pb + 64, 0:n], in_=po[ob:ob + 64, c0:c0 + n])
        else:
            act(rs_t[i][pb:pb + 64, 0:n], po[ob:ob + 64, c0:c0 + n], AF.Ln, po_b, [rs_b[i]])
            act(rs_t[i][pb:pb + 64, 0:n], rs_t[i][pb:pb + 64, 0:n], AF.Exp, [rs_b[i]], [rs_b[i]], scale=-1.0)
        vop("dve", "tensor_tensor", po_b + [rs_b[i]], [rs_b[i]], out=rs_t[i][pb:pb + 64, 0:n], in0=po[pb:pb + 64, c0:c0 + n],
            in1=rs_t[i][pb:pb + 64, 0:n], op=ALU.mult)
        vop("dve", "tensor_tensor", [rs_b[i], mix_b[ci][j]], [mix_b[ci][j]], out=mix[pb:pb + 64, ci, t0:t0 + n],
            in0=rs_t[i][pb:pb + 64, 0:n], in1=mix[pb:pb + 64, ci, t0:t0 + n], op=ALU.mult)

    def keep_warm(n, bank=1):
        for _ in range(n):
            P.add("pe", "matmul", (psb[bank][:, 0:128],), dict(lhsT=identb[:], rhs=identb[:], start=True, stop=True, skip_group_check=True),
                  [c_ident], psb_b[bank])

    NWARM = int(os.environ.get("KWARM", "0"))

    def run_pipeline(steps, lag=1):
        lag = int(os.environ.get("KLAG", lag))
        n = len(steps)
        for k in range(n + lag):
            if k < n and steps[k][0] is not None:
                steps[k][0]()
            if k >= lag and steps[k - lag][1] is not None:
                steps[k - lag][1]()

    xres_b = [Buf("xres%d" % tb) for tb in range(NB)]

    dbg_stop = os.environ.get("KDBG", "")

    class _Stop(Exception):
        pass

    def stage(name):
        if dbg_stop == name:
            raise _Stop()

    def layer_body(l):
        w_l = w_in[l]

        def proj_fm(sl, sl_b, c0, m, evac, tiles=None):
            for j, (t0, n) in enumerate(TTILES):
                if tiles is not None and j not in tiles:
                    continue
                ps, ps_bufs = next_acc()
                for k in range(8):
                    mm(ps[0:m, 0:n], sl[:, k, c0:c0 + m], xT[:, k, t0:t0 + n], k == 0, k == 7,
                       [sl_b] + xT_bufs(t0, n), ps_bufs)
                evac(ps, ps_bufs, j, t0, n)

        def gate_evac(ci):
            def f(ps, ps_bufs, j, t0, n):
                act(mix[:, ci, t0:t0 + n], ps[:, 0:n], AF.Silu, ps_bufs, [mix_b[ci][j]])
            return f

        sl, slb = load_slab(w_l, HG, 512)
        if l == 0:
            SX = Scope("X0_")
            xs = [SX.sb("xs%d" % i, [128, D]) for i in range(2)]
            xs_b = [SX.buf("xs%d" % i) for i in range(2)]
            xb2 = [SX.sb("xb2%d" % i, [128, D], BF16) for i in range(2)]
            xb2_b = [SX.buf("xb2%d" % i) for i in range(2)]
            for j in range(5):
                for tb in range(4 * j, min(4 * j + 4, NB)):
                    if tb % 2 == 0:
                        build_xT_block(tb)
                    else:
                        r = blk_rows(tb)
                        i = (tb // 2) % 2
                        dma("sp", xs[i][0:r, :], x_all[tb * 128:tb * 128 + r, :], [], [xs_b[i]])
                        vop("dve", "tensor_copy", [xs_b[i]], [xb2_b[i]], out=xb2[i][0:r, :], in_=xs[i][0:r, :])
                        transpose_block_to(xT, [xT_b[tb]], xb2[i], xb2_b[i], r, tb * 128)
                for c in range(4):
                    proj_fm(sl, slb, c * 128, 128, gate_evac(c), tiles=(j,))
            build_memT()
            SX.close()
        else:
            for c in range(4):
                proj_fm(sl, slb, c * 128, 128, gate_evac(c))
        sl, slb = load_slab(w_l, FG, 256)
        for c in range(2):
            proj_fm(sl, slb, c * 128, 128, gate_evac(4 + c))
        sl, slb = load_slab(w_l, MG, 256)
        for c in range(2):
            proj_fm(sl, slb, c * 128, 128, gate_evac(6 + c))

        sl_hq, slb_hq = load_slab(w_l, HQ, 512)
        SH = Scope("H%d_" % l)
        qt = [SH.sb("qt%d" % h, [128, TT], BF16) for h in range(4)]
        kt = [SH.sb("kt%d" % h, [128, TT], BF16) for h in range(4)]
        ktm = [SH.sb("ktm%d" % h, [128, NB, 128], BF16) for h in range(4)]
        qt_b = [SH.buf("qt%d" % h) for h in range(4)]
        kt_b = [SH.buf("kt%d" % h) for h in range(4)]
        ktm_b = [SH.buf("ktm%d" % h) for h in range(4)]
        eGl = SH.sb("eGl", [128, 4, 33])
        eGl_b = [SH.buf("eGl%d" % h) for h in range(4)]
        vhg = SH.sb("vhg", [128, NB, 512], BF16)
        vhg_b = [SH.buf("vhg%d" % tb) for tb in range(NB)]
        S32 = [SH.sb("S32_%d" % h, [128, 128]) for h in range(4)]
        Sbf = [SH.sb("Sbf_%d" % h, [128, 128], BF16) for h in range(4)]
        S32s = [SH.sb("S32s_%d" % h, [128, 128]) for h in range(4)]
        Sbfs = [SH.sb("Sbfs_%d" % h, [128, 128], BF16) for h in range(4)]
        S_b = [SH.buf("S%d" % h) for h in range(4)]
        Sbf_b = [SH.buf("Sbf%d" % h) for h in range(4)]
        Ss_b = [SH.buf("Ss%d" % h) for h in range(4)]
        Sbfs_b = [SH.buf("Sbfs%d" % h) for h in range(4)]
        AT = [SH.sb("AT%d" % h, [128, NB, 64], BF16) for h in range(4)]
        AT_b = [[SH.buf("AT%d_%d" % (h, j)) for j in range(5)] for h in range(4)]
        Tst = [[SH.sb("T%d_%d" % (h, i), [128, 128]) for i in range(2)] for h in range(4)]
        Tst_b = [[SH.buf("T%d_%d" % (h, i)) for i in range(2)] for h in range(4)]
        osq_ctr = [0]
        osb_ctr = [0]
        rms_pending = []
        rms_b_pending = []
        SHp = Scope("Hp%d_" % l)
        NHT = 3
        hu = [SHp.sb("hu%d" % i, [128, 512]) for i in range(NHT)]
        hl1 = [SHp.sb("hl1%d" % i, [128, 512]) for i in range(NHT)]
        hl2 = [SHp.sb("hl2%d" % i, [128, 512]) for i in range(NHT)]
        hG = [SHp.sb("hG%d" % i, [128, 512]) for i in range(NHT)]
        hu_b = [SHp.buf("hu%d" % i) for i in range(NHT)]
        hl1_b = [SHp.buf("hl1%d" % i) for i in range(NHT)]
        hl2_b = [SHp.buf("hl2%d" % i) for i in range(NHT)]
        hG_b = [SHp.buf("hG%d" % i) for i in range(NHT)]
        ht_ctr = [0]
        hf_pending = []

        sl, slb = sl_hq, slb_hq
        for h in range(4):
            def f(ps, ps_bufs, j, t0, n, h=h):
                act(qt[h][:, t0:t0 + n], ps[:, 0:n], AF.Silu, ps_bufs, [qt_b[h]])
            proj_fm(sl, slb, h * 128, 128, f)
        sl, slb = load_slab(w_l, HI, 512)
        for tb in range(NB):
            r = blk_rows(tb)
            ps, ps_bufs = next_acc()
            for k in range(8):
                mm(ps[0:r, :], xT[:, k, tb * 128:tb * 128 + r], sl[:, k, 0:512], k == 0, k == 7, [slb, xT_b[tb]], ps_bufs)
            vop("dve", "tensor_copy", ps_bufs, [vhg_b[tb]], out=vhg[0:r, tb, :], in_=ps[0:r, :])
        stage("Ha%d" % l)
        def ktm_transposes(h):
            for g0 in range(0, NB, 8):
                g1 = min(NB, g0 + 8)
                for tb in range(g0, g1):
                    r = blk_rows(tb)
                    tr(pst[0:r, (tb - g0) * 128:(tb - g0 + 1) * 128], kt[h][:, tb * 128:tb * 128 + r], identb[:], [kt_b[h], c_ident], [pst_b])
                if g1 - g0 == 8:
                    vop("dve", "tensor_copy", [pst_b], [ktm_b[h]], out=ktm[h][:, g0:g1, :], in_=pst[:].rearrange("p (b k) -> p b k", b=8))
                else:
                    vop("dve", "tensor_copy", [pst_b], [ktm_b[h]], out=ktm[h][0:16, g0, :], in_=pst[0:16, 0:128])

        sl, slb = load_slab(w_l, HF, 512)
        for h in range(4):
            def f(ps, ps_bufs, j, t0, n, h=h):
                i = ht_ctr[0] % NHT
                ht_ctr[0] += 1
                u_, l1_, l2_, G_ = hu[i][:, 0:n], hl1[i][:, 0:n], hl2[i][:, 0:n], hG[i][:, 0:n]

                def stage_b(i=i, n=n, l1_=l1_, l2_=l2_, G_=G_):
                    vop("pool", "tensor_tensor", [hl1_b[i], hl2_b[i]], [hl2_b[i]], out=l2_, in0=l2_, in1=l1_, op=ALU.subtract)
                    vop("dve", "tensor_tensor_scan", [hl2_b[i], c_scanm], [hG_b[i]], out=G_, data0=scanm[:, 0:n], data1=l2_,
                        initial=0.0, op0=ALU.mult, op1=ALU.add)
                    vop("pool", "tensor_tensor", [hl1_b[i], hG_b[i]], [hl1_b[i]], out=l1_, in0=l1_, in1=G_, op=ALU.add)

                def stage_c(i=i, h=h, j=j, t0=t0, n=n, u_=u_, l1_=l1_, l2_=l2_, G_=G_):
                    act(l2_, G_, AF.Exp, [hG_b[i]], [hl2_b[i]])
                    vop("dve", "tensor_tensor", [hl2_b[i], qt_b[h]], [qt_b[h]], out=qt[h][:, t0:t0 + n], in0=l2_, in1=qt[h][:, t0:t0 + n],
                        op=ALU.mult)
                    if n == 512:
                        vop("dve", "tensor_copy", [hl2_b[i]], [eGl_b[h]], out=eGl[:, h, j * 8:(j + 1) * 8],
                            in_=l2_.rearrange("p (c k) -> p c k", k=64)[:, :, 63])
                    else:
                        vop("dve", "tensor_copy", [hl2_b[i]], [eGl_b[h]], out=eGl[:, h, 32:33], in_=hl2[i][:, n - 1:n])
                    act(l1_, l1_, AF.Exp, [hl1_b[i], c_lb], [hl1_b[i]], bias=lnoml[:, h, l:l + 1], scale=-1.0)
                    vop("dve", "tensor_tensor", [hl1_b[i], hu_b[i]], [kt_b[h]], out=kt[h][:, t0:t0 + n], in0=l1_, in1=u_, op=ALU.mult)

                if len(hf_pending) >= 2:
                    hf_pending.pop(0)[1]()
                if hf_pending:
                    hf_pending[-1][0]()
                act(u_, ps[:, 0:n], AF.Exp, ps_bufs, [hu_b[i]], scale=-1.0)
                act(l1_, u_, AF.Ln, [hu_b[i]], [hl1_b[i]], bias=1.0, scale=1.0)
                act(l2_, u_, AF.Ln, [hu_b[i], c_lb], [hl2_b[i]], bias=1.0, scale=low[:, h, l:l + 1])
                hf_pending.append((stage_b, stage_c))
            proj_fm(sl, slb, h * 128, 128, f)
            if h >= 1:
                ktm_transposes(h - 1)
        if hf_pending:
            if len(hf_pending) >= 2:
                hf_pending[0][1]()
            hf_pending[-1][0]()
            hf_pending[-1][1]()
            del hf_pending[:]
        ktm_transposes(3)
        for h in range(4):
            dma("sp", S32s[h][:], st_in[l, h], [], [Ss_b[h]])
            dma("pool", Sbfs[h][:], st_in[l, h], [], [Sbfs_b[h]])
        SHp.close()
        SHl = Scope("Hl%d_" % l)
        osq = [SHl.sb("osq%d" % i, [128, 512]) for i in range(2)]
        osq_b = [SHl.buf("osq%d" % i) for i in range(2)]
        osqb = [SHl.sb("osqb%d" % i, [128, 512], BF16) for i in range(2)]
        osqb_b = [SHl.buf("osqb%d" % i) for i in range(2)]
        osb = [SHl.sb("osb%d" % i, [128, 512]) for i in range(4)]
        osb_b = [SHl.buf("osb%d" % i) for i in range(4)]
        stage("Hb%d" % l)
        for j, (t0, n) in enumerate(TTILES):
            C = 64 if j < 4 else 16
            nch = n // C
            for h in range(4):
                ps, ps_bufs = next_score()
                for cc in range(nch):
                    ta = t0 + cc * C
                    mm(ps[0:C, cc * 64:cc * 64 + C], kt[h][:, ta:ta + C], qt[h][:, ta:ta + C], True, True, [kt_b[h], qt_b[h]], ps_bufs)
                if j < 4:
                    for half in range(2):
                        vop("dve", "tensor_tensor", ps_bufs + [c_tri], [AT_b[h][j]], out=AT[h][half * 64:(half + 1) * 64, 4 * j:4 * j + 4, :],
                            in0=ps[0:64, :].rearrange("p (c k) -> p c k", k=64)[:, half:8:2, :],
                            in1=trif[0:64, 0:64].unsqueeze(1).to_broadcast([64, 4, 64]), op=ALU.mult)
                else:
                    vop("dve", "tensor_tensor", ps_bufs + [c_tri], [AT_b[h][j]], out=AT[h][0:16, 16, 0:16], in0=ps[0:16, 0:16],
                        in1=trif[0:16, 0:16], op=ALU.mult)
        stage("Hc%d" % l)
        obanks = [0, 1, 2, 5]
        lbanks = [6, 4, 3]

        def emit_L(c):
            if c > 32:
                return
            C_ = 64 if c < 32 else 16
            tb_ = c // 2 if c < 32 else 16
            pb_ = 64 * (c % 2) if c < 32 else 0
            lb_ = lbanks[c % 3]
            for h in range(4):
                mm(psb[lb_][:, h * 128:(h + 1) * 128], ktm[h][pb_:pb_ + C_, tb_, :], vhg[pb_:pb_ + C_, tb_, h * 128:(h + 1) * 128], True, True,
                   [ktm_b[h], vhg_b[tb_]], psb_b[lb_])

        emit_L(0)
        for j, (t0, n) in enumerate(TTILES):
            C = 64 if j < 4 else 16
            nch = n // C
            samp = (j == 4)
            for cc in range(nch):
                c = (t0 // 64) + cc if j < 4 else 32
                tb = c // 2 if j < 4 else 16
                pbk = 64 * (c % 2) if j < 4 else 0
                ta = t0 + cc * C
                first = (c == 0)
                for h in range(4):
                    po = psb[obanks[h]][:, cc * C:(cc + 1) * C]
                    mm(po, vhg[pbk:pbk + C, tb, h * 128:(h + 1) * 128], AT[h][pbk:pbk + C, tb, 0:C], True, first,
                       [vhg_b[tb], AT_b[h][j]], psb_b[obanks[h]])
                lbank = lbanks[c % 3]
                emit_L(c + 1)
                if not first:
                    for h in range(4):
                        po = psb[obanks[h]][:, cc * C:(cc + 1) * C]
                        if samp:
                            mm(po, Sbfs[h][:], qt[h][:, ta:ta + C], False, True, [Sbfs_b[h], qt_b[h]], psb_b[obanks[h]])
                        else:
                            mm(po, Sbf[h][:], qt[h][:, ta:ta + C], False, True, [Sbf_b[h], qt_b[h]], psb_b[obanks[h]])
                for h in range(4):
                    psS = psb[lbank][:, h * 128:(h + 1) * 128]
                    psS_b = psb_b[lbank]
                    eg = eGl[:, h, c:c + 1]
                    if samp:
                        vop("dve", "tensor_tensor", psS_b + [Ss_b[h]], [Tst_b[h][0]], out=Tst[h][0][:], in0=psS, in1=S32s[h][:], op=ALU.add)
                        vop("dve", "tensor_scalar", [Tst_b[h][0], eGl_b[h]], [Ss_b[h]], out=S32s[h][:], in0=Tst[h][0][:], scalar1=eg, scalar2=None,
                            op0=ALU.mult)
                        dma("sp", hst_out[l, 1, h], S32s[h][:], [Ss_b[h]], [])
                        continue
                    cur, prv = Tst[h][c % 2], Tst[h][(c + 1) % 2]
                    cur_b, prv_b = Tst_b[h][c % 2], Tst_b[h][(c + 1) % 2]
                    if first:
                        vop("dve", "tensor_copy", psS_b, [cur_b], out=cur[:], in_=psS)
                    else:
                        vop("dve", "scalar_tensor_tensor", [prv_b, eGl_b[h]] + psS_b, [cur_b], out=cur[:], in0=prv[:], scalar=eGl[:, h, c - 1:c],
                            in1=psS, op0=ALU.mult, op1=ALU.add)
                    if c == 31:
                        vop("dve", "tensor_scalar", [cur_b, eGl_b[h]], [S_b[h]], out=S32[h][:], in0=cur[:], scalar1=eg, scalar2=None, op0=ALU.mult)
                        dma("sp", hst_out[l, 0, h], S32[h][:], [S_b[h]], [])
                    else:
                        act(Sbf[h][:], cur[:], AF.Identity, [cur_b, eGl_b[h]], [Sbf_b[h]], scale=eg)
                if rms_b_pending:
                    rms_b_pending.pop(0)()
                if rms_pending and (cc % 2 == 1 or samp):
                    rms_b_pending.append(rms_pending.pop(0)())
            while rms_pending or rms_b_pending:
                if rms_b_pending:
                    rms_b_pending.pop(0)()
                if rms_pending:
                    rms_b_pending.append(rms_pending.pop(0)())
            for h in range(4):
                ob = obanks[h]
                oi = osb_ctr[0] % 4
                osb_ctr[0] += 1
                act(osb[oi][:, 0:n], psb[ob][:, 0:n], AF.Copy, psb_b[ob], [osb_b[oi]])

                def rms_a(h=h, oi=oi, j=j, t0=t0, n=n, stt={}):
                    qi_ = osq_ctr[0] % 2
                    osq_ctr[0] += 1
                    stt["qi"] = qi_
                    act(osqb[qi_][:, 0:n], osb[oi][:, 0:n], AF.Square, [osb_b[oi]], [osqb_b[qi_]])

                    def rms_b():
                        pss, pss_b = pst[:].bitcast(F32), [pst_b]
                        mm(pss[:, 0:n], onesb[:], osqb[qi_][:, 0:n], True, True, [c_ones, osqb_b[qi_]], pss_b)
                        act(osq[qi_][:, 0:n], pss[:, 0:n], AF.Ln, pss_b, [osq_b[qi_]], bias=float(128.0 * RMS_EPS), scale=1.0)
                        act(osq[qi_][:, 0:n], osq[qi_][:, 0:n], AF.Exp, [osq_b[qi_]], [osq_b[qi_]], scale=-0.5)
                        vop("dve", "tensor_tensor", [osb_b[oi], osq_b[qi_]], [osq_b[qi_]], out=osq[qi_][:, 0:n], in0=osb[oi][:, 0:n],
                            in1=osq[qi_][:, 0:n], op=ALU.mult)
                        vop("dve", "scalar_tensor_tensor", [osq_b[qi_], c_gcol, mix_b[h][j]], [mix_b[h][j]], out=mix[:, h, t0:t0 + n],
                            in0=osq[qi_][:, 0:n], scalar=gcol[:, l:l + 1], in1=mix[:, h, t0:t0 + n], op0=ALU.mult, op1=ALU.mult)
                    return rms_b
                rms = rms_a
                if os.environ.get("KRMS", "1") == "1":
                    rms_pending.append(rms)
                else:
                    rms()
        while rms_pending or rms_b_pending:
            if rms_b_pending:
                rms_b_pending.pop(0)()
            if rms_pending:
                rms_b_pending.append(rms_pending.pop(0)())
        SHl.close()
        SH.close()
        stage("H%d" % l)

        sl_mq, slb_mq = load_slab(w_l, MQ, 256)
        SM = Scope("M%d_" % l)
        alloc_attn_tiles(SM)
        MKT = SM.sb("MKT", [128, 2, 256], BF16)
        MKTs = SM.sb("MKTs", [128, 2, 256], BF16)
        MVa = SM.sb("MVa", [128, 2, 4, 128], BF16)
        MVas = SM.sb("MVas", [128, 2, 4, 128], BF16)
        c_MKT, c_MKTs, c_MVa, c_MVas = SM.buf("MKT"), SM.buf("MKTs"), SM.buf("MVa"), SM.buf("MVas")
        vop("dve", "memset", [], [c_MVa], MVa[:], 1.0)
        vop("dve", "memset", [], [c_MVas], MVas[:], 1.0)
        mqT = SM.sb("mqT", [128, 2, TT], BF16)
        mqT_b = [SM.buf("mqT%d" % c) for c in range(2)]

        sl, slb = sl_mq, slb_mq
        for c in range(2):
            def f(ps, ps_bufs, j, t0, n, c=c):
                act(mqT[:, c, t0:t0 + n], ps[:, 0:n], AF.Identity, ps_bufs, [mqT_b[c]], scale=0.125)
            proj_fm(sl, slb, c * 128, 128, f)
        sl, slb = load_slab(wkv_in[l], 0, 512)
        for mb in range(2):
            ps, ps_bufs = next_acc()
            for k in range(8):
                mm(ps[:, :], memT[:, k, mb * 128:(mb + 1) * 128], sl[:, k, 0:512], k == 0, k == 7, [slb, c_memT], ps_bufs)
            st_, st_bf = next_stg()
            vop("dve", "tensor_copy", ps_bufs, [st_bf], out=st_[:], in_=ps[:, :])
            dma("sp", mkv_out[l, mb * 128:(mb + 1) * 128, :], st_[:], [st_bf], [])
            v4 = st_[:, 256:512].rearrange("p (h d) -> p h d", h=4)
            vop("pool", "tensor_copy", [st_bf], [c_MVa], out=MVa[:, mb, 0:4:2, 0:64], in_=v4[:, 0:4:2, :])
            vop("pool", "tensor_copy", [st_bf], [c_MVa], out=MVa[:, mb, 1:4:2, 64:128], in_=v4[:, 1:4:2, :])
        for c in range(2):
            ps, ps_bufs = next_acc()
            for k in range(8):
                mm(ps[:, 0:256], sl[:, k, c * 128:(c + 1) * 128], memT[:, k, :], k == 0, k == 7, [slb, c_memT], ps_bufs)
            vop("dve", "tensor_copy", ps_bufs, [c_MKT], out=MKT[:, c, :], in_=ps[:, 0:256])
        for mb in range(2):
            i = mb % 2
            dma("pool", xbt[i][:, 0:256], cmk_in[l, mb * 128:(mb + 1) * 128, :], [], [xbt_b[i]])
            for c in range(2):
                tr(pst[:, c * 128:(c + 1) * 128], xbt[i][:, c * 128:(c + 1) * 128], identb[:], [xbt_b[i], c_ident], [pst_b])
            vop("dve", "tensor_copy", [pst_b], [c_MKTs], out=MKTs[:, :, mb * 128:(mb + 1) * 128],
                in_=pst[:, 0:256].rearrange("p (c t) -> p c t", c=2))
            cmv4 = cmv_in[l, mb * 128:(mb + 1) * 128, :].rearrange("p (h d) -> p h d", h=4)
            dma("pool", MVas[:, mb, 0:4:2, 0:64], cmv4[:, 0:4:2, :], [], [c_MVas])
            dma("pool", MVas[:, mb, 1:4:2, 64:128], cmv4[:, 1:4:2, :], [], [c_MVas])
        steps = []
        for h in range(4):
            c = h // 2
            pb = 64 * (h % 2)
            for j, (t0, n) in enumerate(TTILES):
                samp = (j == 4)
                kt_, kt_bf = (MKTs, c_MKTs) if samp else (MKT, c_MKT)
                mv_, mv_bf = (MVas, c_MVas) if samp else (MVa, c_MVa)
                st = {}
                for mb in range(2):
                    def front(st=st, mb=mb, kt_=kt_, kt_bf=kt_bf, c=c, pb=pb, t0=t0, n=n):
                        ps, ps_bufs = next_score()
                        mm(ps[:, 0:n], kt_[pb:pb + 64, c, mb * 128:(mb + 1) * 128], mqT[pb:pb + 64, c, t0:t0 + n], True, True,
                           [kt_bf, mqT_b[c]], ps_bufs)
                        pt_, pt_bf = next_pt()
                        act(pt_[:, 0:n], ps[:, 0:n], AF.Exp, ps_bufs, [pt_bf])
                        st[mb] = (pt_, pt_bf)

                    def back(st=st, mb=mb, mv_=mv_, mv_bf=mv_bf, h=h, c=c, t0=t0, n=n):
                        if mb == 0:
                            st["po"] = next_oacc()
                        po, po_b = st["po"]
                        pt_, pt_bf = st[mb]
                        mm(po[:, 0:n], mv_[:, mb, h, :], pt_[:, 0:n], mb == 0, mb == 1, [mv_bf, pt_bf], po_b)
                        if mb == 1:
                            attn_finish(po, po_b, h, 6 + c, t0, n)
                    steps.append((front, back))
        score_banks[0] = [3, 4, 1]
        run_pipeline(steps, lag=2)
        score_banks[0] = [3, 4]
        SM.close()
        stage("M%d" % l)

        sl_fq, slb_fq = load_slab(w_l, FQ, 512)
        SF = Scope("F%d_" % l)
        alloc_attn_tiles(SF)
        Qa = [SF.sb("Qa%d" % h, [128, TT], BF16) for h in range(4)]
        Ka = [SF.sb("Ka%d" % h, [128, TT], BF16) for h in range(4)]
        Qa_b = [SF.buf("Qa%d" % h) for h in range(4)]
        Qaug_b = [[SF.buf("Qaug%d_%d" % (h, r_)) for r_ in range(3)] for h in range(4)]
        Ka_b = [SF.buf("Ka%d" % h) for h in range(4)]
        Kaug_b = [SF.buf("Kaug%d" % h) for h in range(4)]
        Va = SF.sb("Va", [128, NB, 4, 128], BF16)
        Va_b, Va1_b = SF.buf("Va"), SF.buf("Va1")
        negD = SF.sb("negD", [128, 4, NB])
        negDp = SF.sb("negDp", [128, 4, NPB])
        c_negD, c_negDp = SF.buf("negD"), SF.buf("negDp")
        lf = SF.sb("lf", [4, TT])
        Dsp = SF.sb("Dsp", [4, 3, TT], BF16)
        clf_t = SF.sb("clf_t", [128, 32, 4])
        clf_c = SF.sb("clf_c", [128, 32, 4])
        Dtot = SF.sb("Dtot", [4, 1])
        c_lf, c_Dsp, c_Dpast, c_Dtot = SF.buf("lf"), SF.buf("Dsp"), SF.buf("Dpast"), SF.buf("Dtot")
        KTq = SF.sb("KTq", [128, 4, 1024], BF16)
        KTq_b, KTq1_b = SF.buf("KTq"), SF.buf("KTq1")
        Ve = SF.sb("Ve", [128, 16, 128], BF16)
        Vo = SF.sb("Vo", [128, 16, 128], BF16)
        Vq_b, Vq1_b = SF.buf("Vq"), SF.buf("Vq1")
        Kc = SF.sb("Kc", [128, 8, 256], BF16)
        Kc_b = SF.buf("Kc")
        for h in range(4):
            vop("dve", "memset", [], [Kaug_b[h]], Ka[h][64:67, :], 1.0)
        vop("dve", "memset", [], [KTq1_b], KTq[64:67, :, :], 1.0)
        vop("dve", "memset", [], [Vq1_b], Ve[:, :, 64:128], 1.0)
        vop("dve", "memset", [], [Vq1_b], Vo[:, :, 0:64], 1.0)
        vop("dve", "memset", [], [Va1_b], Va[:, :, 0:4:2, 64:128], 1.0)
        vop("dve", "memset", [], [Va1_b], Va[:, :, 1:4:2, 0:64], 1.0)
        ck3 = ck_in[l].rearrange("(p b) c -> p b c", b=32)
        cvv = cv_in[l].rearrange("(p b) (m two d) -> p (b m) two d", b=32, two=2, d=64)
        pos, pos_b = psb[0], psb_b[0]

        def load_quarter_dma(qi):
            dma("pool", Kc[:], ck3[:, qi * 8:(qi + 1) * 8, :], [], [Kc_b])
            dma("pool", Ve[:, :, 0:64], cvv[:, qi * 16:(qi + 1) * 16, 0, :], [], [Vq_b])
            dma("pool", Vo[:, :, 64:128], cvv[:, qi * 16:(qi + 1) * 16, 1, :], [], [Vq_b])

        def load_quarter_tr(qi):
            for c in range(2):
                for b8 in range(8):
                    tr(pst[:, b8 * 128:(b8 + 1) * 128], Kc[:, b8, c * 128:(c + 1) * 128], identb[:], [Kc_b, c_ident], [pst_b])
                vop("dve", "tensor_copy", [pst_b], [KTq_b], out=KTq[0:64, 2 * c, :], in_=pst[0:64, :])
                act(KTq[0:64, 2 * c + 1, :], pst[64:128, :], AF.Copy, [pst_b], [KTq_b])

        sq_state = {}

        def sample_front(qi):
            ps, ps_bufs = next_score()
            for h in range(4):
                for b8 in range(8):
                    cc = (h * 8 + b8) * 16
                    mm(ps[:, cc:cc + 16], KTq[0:67, h, b8 * 128:(b8 + 1) * 128], Qa[h][0:67, TP:TT], True, True,
                       [KTq_b, KTq1_b, Qa_b[h]] + Qaug_b[h], ps_bufs)
            st_, st_bf = next_stg()
            vop("dve", "tensor_tensor", ps_bufs + [c_negDp], [st_bf], out=st_[:].rearrange("p (h b q) -> p h b q", h=4, b=8),
                in0=ps[:, :].rearrange("p (h b q) -> p h b q", h=4, b=8),
                in1=negDp[:, :, qi * 8:(qi + 1) * 8].unsqueeze(3).to_broadcast([128, 4, 8, 16]), op=ALU.add)
            pt_, pt_bf = next_pt()
            act(pt_[:], st_[:], AF.Exp, [st_bf], [pt_bf])
            sq_state[qi] = (pt_, pt_bf)

        def sample_back(qi):
            pt_, pt_bf = sq_state[qi]
            for h in range(4):
                vt = Ve if h % 2 == 0 else Vo
                for b8 in range(8):
                    cc = (h * 8 + b8) * 16
                    mm(pos[:, h * 16:(h + 1) * 16], vt[:, b8 * 2 + h // 2, :], pt_[:, cc:cc + 16], qi == 0 and b8 == 0 and h == 0, False,
                       [Vq_b, Vq1_b, pt_bf], pos_b, skip=True)
            if qi + 1 < 4:
                load_quarter_dma(qi + 1)

        load_quarter_dma(0)
        dma("sp", clf_t[:], clf_in[l].rearrange("(p b) h -> p b h", b=32), [], [c_Dpast])

        sl, slb = sl_fq, slb_fq
        for c in range(2):
            def f(ps, ps_bufs, j, t0, n, c=c):
                act(Qa[2 * c][0:64, t0:t0 + n], ps[0:64, 0:n], AF.Identity, ps_bufs, [Qa_b[2 * c]], scale=0.125)
                act(Qa[2 * c + 1][0:64, t0:t0 + n], ps[64:128, 0:n], AF.Identity, ps_bufs, [Qa_b[2 * c + 1]], scale=0.125)
            proj_fm(sl, slb, c * 128, 128, f)
        for c in range(2):
            def f(ps, ps_bufs, j, t0, n, c=c):
                vop("dve", "tensor_copy", ps_bufs, [Ka_b[2 * c]], out=Ka[2 * c][0:64, t0:t0 + n], in_=ps[0:64, 0:n])
                vop("dve", "tensor_copy", ps_bufs, [Ka_b[2 * c + 1]], out=Ka[2 * c + 1][0:64, t0:t0 + n], in_=ps[64:128, 0:n])
            proj_fm(sl, slb, 256 + c * 128, 128, f)
        sl, slb = load_slab(w_l, FL, 4)

        def f(ps, ps_bufs, j, t0, n):
            act(lf[:, t0:t0 + n], ps[0:4, 0:n], AF.Exp, ps_bufs + [c_negb], [c_lf], bias=negb[:, l:l + 1], scale=-1.0)
        proj_fm(sl, slb, 0, 4, f)

        onebc = one_c[0:4, 0:1]
        stat_stages = []

        def s1():
            act(lf[:], lf[:], AF.Ln, [c_lf], [c_lf], bias=1.0, scale=1.0)
            vop("dve", "tensor_scalar", [c_lf], [c_lf], out=lf[:], in0=lf[:], scalar1=-1.0, scalar2=None, op0=ALU.mult)
            for h in range(4):
                vop("dve", "tensor_tensor_scan", [c_Dpast, c_onec], [c_Dpast], out=clf_c[:, :, h], data0=one_c[:, 0:1].to_broadcast([128, 32]),
                    data1=clf_t[:, :, h], initial=0.0, op0=ALU.mult, op1=ALU.add)
        stat_stages.append(s1)

        def s2():
            ps, ps_bufs = next_score()
            for tb in range(NB):
                r = blk_rows(tb)
                tr(ps[0:r, tb * 4:tb * 4 + 4], lf[0:4, tb * 128:tb * 128 + r], identf[0:4, 0:4], [c_lf, c_ident], ps_bufs)
            st_, st_bf = next_stg()
            vop("dve", "tensor_copy", ps_bufs, [st_bf], out=st_[:, 0:NB * 4], in_=ps[:, 0:NB * 4])
            dma("sp", lf_out[l, 0:TP, :].rearrange("(b p) h -> p b h", p=128), st_[:, 0:64].rearrange("p (b h) -> p b h", h=4), [st_bf], [])
            dma("sp", lf_out[l, TP:TT, :], st_[0:16, 64:68], [st_bf], [])
            ps, ps_bufs = next_score()
            mm(ps[:, 0:4], sutf[:], clf_c[:, 31, :], True, True, [c_tri, c_Dpast], ps_bufs)
            mm(ps[0:4, 8:9], clf_c[:, 31, :], onesf[:, 0:1], True, True, [c_ones, c_Dpast], ps_bufs)
            vop("dve", "tensor_copy", ps_bufs, [c_Dtot], out=Dtot[:], in_=ps[0:4, 8:9])
            vop("dve", "scalar_tensor_tensor", ps_bufs + [c_Dpast], [c_negDp], out=negDp[:].rearrange("p h b -> p b h"), in0=clf_c[:], scalar=-1.0,
                in1=ps[:, 0:4].unsqueeze(1).to_broadcast([128, 32, 4]), op0=ALU.mult, op1=ALU.subtract)
        stat_stages.append(s2)

        def s3():
            vop("dve", "tensor_tensor_scan", [c_lf, c_onec], [c_lf], out=lf[:, 0:TP], data0=onebc.to_broadcast([4, TP]),
                data1=lf[:, 0:TP], initial=0.0, op0=ALU.mult, op1=ALU.add)
            vop("dve", "tensor_tensor_scan", [c_lf, c_onec, c_Dtot], [c_lf], out=lf[:, TP:TT], data0=onebc.to_broadcast([4, TS]),
                data1=lf[:, TP:TT], initial=Dtot[:, 0:1], op0=ALU.mult, op1=ALU.add)
        stat_stages.append(s3)

        def s4():
            ps, ps_bufs = next_score()
            for tb in range(NB):
                r = blk_rows(tb)
                tr(ps[0:r, tb * 4:tb * 4 + 4], lf[0:4, tb * 128:tb * 128 + r], identf[0:4, 0:4], [c_lf, c_ident], ps_bufs)
            vop("dve", "tensor_scalar", ps_bufs, [c_negD], out=negD[:].rearrange("p h b -> p b h"),
                in0=ps[:, 0:NB * 4].rearrange("p (b h) -> p b h", h=4), scalar1=-1.0, scalar2=None, op0=ALU.mult)
        stat_stages.append(s4)

        def s5():
            vop("dve", "tensor_copy", [c_lf], [c_Dsp], out=Dsp[:, 0, :], in_=lf[:])
            vop("dve", "tensor_tensor", [c_lf, c_Dsp], [c_lf], out=lf[:], in0=lf[:], in1=Dsp[:, 0, :], op=ALU.subtract)
            vop("dve", "tensor_copy", [c_lf], [c_Dsp], out=Dsp[:, 1, :], in_=lf[:])
            vop("dve", "tensor_tensor", [c_lf, c_Dsp], [c_lf], out=lf[:], in0=lf[:], in1=Dsp[:, 1, :], op=ALU.subtract)
            vop("dve", "tensor_copy", [c_lf], [c_Dsp], out=Dsp[:, 2, :], in_=lf[:])
            for h in range(4):
                for r_ in range(3):
                    dma("sp", Qa[h][64 + r_:65 + r_, :], Dsp[h:h + 1, r_, :], [c_Dsp], [Qaug_b[h][r_]])
        stat_stages.append(s5)

        sl, slb = load_slab(w_l, FK, 512)
        for tb in range(NB):
            r = blk_rows(tb)
            ps, ps_bufs = next_acc()
            for k in range(8):
                mm(ps[0:r, :], xT[:, k, tb * 128:tb * 128 + r], sl[:, k, 0:512], k == 0, k == 7, [slb, xT_b[tb]], ps_bufs)
            st_, st_bf = next_stg()
            act(st_[0:r, :], ps[0:r, :], AF.Copy, ps_bufs, [st_bf])
            dma("sp", kv_out[l, tb * 128:tb * 128 + r, :], st_[0:r, :], [st_bf], [])
            v4 = st_[0:r, 256:512].rearrange("p (h d) -> p h d", h=4)
            vop("pool", "tensor_copy", [st_bf], [Va_b], out=Va[0:r, tb, 0:4:2, 0:64], in_=v4[:, 0:4:2, :])
            vop("pool", "tensor_copy", [st_bf], [Va_b], out=Va[0:r, tb, 1:4:2, 64:128], in_=v4[:, 1:4:2, :])
            if tb % 3 == 1 and stat_stages:
                stat_stages.pop(0)()
        while stat_stages:
            stat_stages.pop(0)()
        load_quarter_tr(0)
        dma("pool", slab_big[:], wout_in[l].rearrange("(k p) c -> p k c", p=128), [], [slab_b[0], slab_b[1]])
        stage("F1%d" % l)

        nk_state = {}

        def newkeys_front():
            ps, ps_bufs = next_score()
            for h in range(4):
                mm(ps[0:16, h * 16:(h + 1) * 16], Ka[h][0:67, TP:TT], Qa[h][0:67, TP:TT], True, True,
                   [Ka_b[h], Kaug_b[h], Qa_b[h]] + Qaug_b[h], ps_bufs)
            vop("dve", "tensor_tensor", ps_bufs + [c_mneg], ps_bufs, out=ps[0:16, 0:64].rearrange("p (h q) -> p h q", h=4),
                in0=ps[0:16, 0:64].rearrange("p (h q) -> p h q", h=4), in1=mneg[0:16, 0:16].unsqueeze(1).to_broadcast([16, 4, 16]), op=ALU.add)
            pt_, pt_bf = next_pt()
            for h in range(4):
                act(pt_[0:16, h * 16:(h + 1) * 16], ps[0:16, h * 16:(h + 1) * 16], AF.Exp, ps_bufs + [c_negD], [pt_bf],
                    bias=negD[0:16, h, 16:17], scale=1.0)
            nk_state["pt"] = (pt_, pt_bf)

        def newkeys_back():
            pt_, pt_bf = nk_state["pt"]
            for h in range(4):
                mm(pos[:, h * 16:(h + 1) * 16], Va[0:16, 16, h, :], pt_[0:16, h * 16:(h + 1) * 16], False, True, [Va_b, Va1_b, pt_bf], pos_b, skip=True)
            for h in range(4):
                attn_finish(pos, pos_b, h, 4 + h // 2, TP, TS, c0=h * 16)

        steps = []
        for h in range(4):
            for j in range(4):
                nkb = 4 * j + 4
                st = {}
                if j == 0 and h >= 1:
                    steps.append((lambda h=h: load_quarter_tr(h), None))
                if j == 1:
                    steps.append((lambda h=h: sample_front(h), lambda h=h: sample_back(h)))
                if j == 3 and h == 3:
                    steps.append((newkeys_front, newkeys_back))
                for kb in range(nkb):
                    q0 = max(512 * j, 128 * kb)
                    n = 512 * (j + 1) - q0

                    def front(st=st, h=h, j=j, kb=kb, q0=q0, n=n):
                        ps, ps_bufs = next_score()
                        mm(ps[:, 0:n], Ka[h][0:67, kb * 128:(kb + 1) * 128], Qa[h][0:67, q0:q0 + n], True, True,
                           [Ka_b[h], Kaug_b[h], Qa_b[h]] + Qaug_b[h], ps_bufs)
                        if kb >= 4 * j:
                            vop("dve", "tensor_tensor", ps_bufs + [c_mneg], ps_bufs, out=ps[:, 0:128], in0=ps[:, 0:128], in1=mneg[:], op=ALU.add)
                        pt_, pt_bf = next_pt()
                        act(pt_[:, 0:n], ps[:, 0:n], AF.Exp, ps_bufs + [c_negD], [pt_bf], bias=negD[:, h, kb:kb + 1], scale=1.0)
                        st[kb] = (pt_, pt_bf)

                    def back(st=st, h=h, j=j, kb=kb, q0=q0, n=n, nkb=nkb):
                        if kb == 0:
                            st["po"] = next_oacc()
                        po, po_b = st["po"]
                        pt_, pt_bf = st[kb]
                        mm(po[:, q0 - 512 * j:512], Va[:, kb, h, :], pt_[:, 0:n], kb == 0, kb == nkb - 1, [Va_b, Va1_b, pt_bf], po_b)
                        if kb == nkb - 1:
                            attn_finish(po, po_b, h, 4 + h // 2, 512 * j, 512)
                        keep_warm(NWARM)
                    steps.append((front, back))

        score_banks[0] = [3, 4, 1, 2]
        recip_on_dve[0] = True
        run_pipeline(steps, lag=int(os.environ.get("KLAGF", "4")))
        recip_on_dve[0] = False
        score_banks[0] = [3, 4]
        SF.close()
        stage("F%d" % l)

        SL = Scope("L%d_" % l)
        lng_t = SL.sb("lng_t", [128, D])
        lnb_t = SL.sb("lnb_t", [128, D])
        c_lng, c_lnb = SL.buf("lng"), SL.buf("lnb")
        NR = 3
        xr_t = [SL.sb("xr%d" % i, [128, D]) for i in range(NR)]
        xr_b = [SL.buf("xr%d" % i) for i in range(NR)]
        r_t = [SL.sb("r%d" % i, [128, D]) for i in range(NR)]
        r_b = [SL.buf("r%d" % i) for i in range(NR)]
        y_t = [SL.sb("y%d" % i, [128, D]) for i in range(NR)]
        y_b = [SL.buf("y%d" % i) for i in range(NR)]
        yb_t = [SL.sb("yb%d" % i, [128, D], BF16) for i in range(NB)]
        yb_b = [SL.buf("yb%d" % i) for i in range(NB)]
        bst = [SL.sb("bst%d" % i, [128, 2, 6]) for i in range(NR)]
        mv_t = [SL.sb("mv%d" % i, [128, 4]) for i in range(NR)]
        st_b = [SL.buf("st%d" % i) for i in range(NR)]
        wo = slab_big
        dma("sp", lng_t[:], lng_in[l:l + 1, :].partition_broadcast(128), [], [c_lng])
        dma("sp", lnb_t[:], lnb_in[l:l + 1, :].partition_broadcast(128), [], [c_lnb])
        last = (l == DEPTH - 1)
        bank_pairs = [(0, 1), (2, 3), (4, 5)]

        def stage_a(tb):
            r = blk_rows(tb)
            i = tb % NR
            j = tb // 4
            banks = bank_pairs[i]
            for hf in range(2):
                ps, ps_bufs = psb[banks[hf]], psb_b[banks[hf]]
                for k in range(8):
                    mm(ps[0:r, :], mix[:, k, tb * 128:tb * 128 + r], wo[:, k, hf * 512:(hf + 1) * 512], k == 0, k == 7,
                       [mix_b[k][j], slab_b[0], slab_b[1]], ps_bufs)

        def load_xr(tb):
            r = blk_rows(tb)
            i = tb % NR
            if l == 0:
                dma("pool", xr_t[i][0:r, :], x_all[tb * 128:tb * 128 + r, :], [], [xr_b[i]])
            else:
                dma("pool", xr_t[i][0:r, :], xres[tb * 128:tb * 128 + r, :], [xres_b[tb]], [xr_b[i]])

        def stage_a2(tb):
            r = blk_rows(tb)
            i = tb % NR
            banks = bank_pairs[i]
            for hf in range(2):
                ps, ps_bufs = psb[banks[hf]], psb_b[banks[hf]]
                vop("dve", "scalar_tensor_tensor", ps_bufs + [xr_b[i]], [r_b[i]], out=r_t[i][0:r, hf * 512:(hf + 1) * 512],
                    in0=xr_t[i][0:r, hf * 512:(hf + 1) * 512], scalar=float(DN_ALPHA), in1=ps[0:r, :], op0=ALU.mult, op1=ALU.add)
                vop("dve", "bn_stats", [r_b[i]], [st_b[i]], out=bst[i][0:r, hf, :], in_=r_t[i][0:r, hf * 512:(hf + 1) * 512])
            vop("dve", "bn_aggr", [st_b[i]], [st_b[i]], out=mv_t[i][0:r, 0:2], in_=bst[i][0:r, :, :])
            act(mv_t[i][0:r, 2:3], mv_t[i][0:r, 1:2], AF.Ln, [st_b[i]], [st_b[i]], bias=float(LN_EPS), scale=1.0)
            act(mv_t[i][0:r, 2:3], mv_t[i][0:r, 2:3], AF.Exp, [st_b[i]], [st_b[i]], scale=-0.5)
            vop("dve", "scalar_tensor_tensor", [st_b[i]], [st_b[i]], out=mv_t[i][0:r, 3:4], in0=mv_t[i][0:r, 0:1], scalar=-1.0,
                in1=mv_t[i][0:r, 2:3], op0=ALU.mult, op1=ALU.mult)
            act(y_t[i][0:r, :], r_t[i][0:r, :], AF.Identity, [r_b[i], st_b[i]], [y_b[i]], bias=mv_t[i][0:r, 3:4], scale=mv_t[i][0:r, 2:3])

        def stage_b(tb):
            r = blk_rows(tb)
            i = tb % NR
            vop("pool", "tensor_tensor", [y_b[i], c_lng], [y_b[i]], out=y_t[i][0:r, :], in0=y_t[i][0:r, :], in1=lng_t[0:r, :], op=ALU.mult)
            vop("dve", "tensor_tensor", [y_b[i], c_lnb], [y_b[i]], out=y_t[i][0:r, 0:512], in0=y_t[i][0:r, 0:512], in1=lnb_t[0:r, 0:512], op=ALU.add)
            vop("pool", "tensor_tensor", [y_b[i], c_lnb], [y_b[i]], out=y_t[i][0:r, 512:D], in0=y_t[i][0:r, 512:D], in1=lnb_t[0:r, 512:D], op=ALU.add)
            if last:
                dma("sp", y_out[tb * 128:tb * 128 + r, :], y_t[i][0:r, :], [y_b[i]], [])
            else:
                dma("sp", xres[tb * 128:tb * 128 + r, :], y_t[i][0:r, :], [y_b[i]], [xres_b[tb]])

        def stage_c1(tb):
            if last:
                return
            r = blk_rows(tb)
            i = tb % NR
            act(yb_t[tb][0:r, :], y_t[i][0:r, :], AF.Copy, [y_b[i]], [yb_b[tb]])

        def stage_c2(tb):
            if last:
                return
            transpose_block_to(xT, [xT_b[tb]], yb_t[tb], yb_b[tb], blk_rows(tb), tb * 128)

        for t in range(NB + 2):
            if t == 0:
                load_xr(0)
                load_xr(1)
            if t + 2 < NB:
                load_xr(t + 2)
            if t < NB:
                stage_a(t)
                stage_a2(t)
            if 0 <= t - 1 < NB:
                stage_b(t - 1)
            if 0 <= t - 2 < NB:
                stage_c1(t - 2)
        for t in range(NB):
            stage_c2(t)
        SL.close()
        stage("L%d" % l)

    try:
        for l in range(DEPTH):
            layer_body(l)
    except _Stop:
        for sc in reversed(list(open_scopes)):
            sc.close()
    P.emit(nc, es)
    es.close()
    return nc


_NC_CACHE = {}


def kernel(x_prompt, x_sample, mem_prompt, state_hgrn, cache_fox_k, cache_fox_v, cache_fox_logf,
           cache_mem_k, cache_mem_v, w_in, b_fox_forget, hgrn_lower_bounds, hgrn_norm_g,
           w_mem_kv, w_out, ln_g, ln_b):
    f = lambda a: np.ascontiguousarray(np.asarray(a, dtype=np.float32))
    x_prompt, x_sample, mem_prompt = f(x_prompt), f(x_sample), f(mem_prompt)
    state_hgrn, cache_fox_k, cache_fox_v, cache_fox_logf = f(state_hgrn), f(cache_fox_k), f(cache_fox_v), f(cache_fox_logf)
    cache_mem_k, cache_mem_v = f(cache_mem_k), f(cache_mem_v)
    shared = dict(w_in=f(w_in), bff=f(b_fox_forget), hlb=f(hgrn_lower_bounds), hng=f(hgrn_norm_g), wkv=f(w_mem_kv),
                  wout=f(w_out), lng=f(ln_g), lnb=f(ln_b))
    if "nc" not in _NC_CACHE:
        _NC_CACHE["nc"] = build_nc()
    nc = _NC_CACHE["nc"]
    in_maps = []
    for b in range(8):
        m = dict(shared)
        m["x_all"] = np.ascontiguousarray(np.concatenate([x_prompt[b], x_sample[b]], axis=0))
        m["mem"] = mem_prompt[b]
        m["st"] = np.ascontiguousarray(state_hgrn[:, b])
        m["ck"] = np.ascontiguousarray(cache_fox_k[:, b].reshape(DEPTH, PAST, 256))
        m["cv"] = np.ascontiguousarray(cache_fox_v[:, b].reshape(DEPTH, PAST, 256))
        m["clf"] = np.ascontiguousarray(cache_fox_logf[:, b])
        m["cmk"] = np.ascontiguousarray(cache_mem_k[:, b].reshape(DEPTH, 256, 256))
        m["cmv"] = np.ascontiguousarray(cache_mem_v[:, b].reshape(DEPTH, 256, 256))
        in_maps.append(m)
    res = run_bass_kernel_spmd(nc, in_maps, core_ids=list(range(8)))
    R = res.results
    y_all = np.stack([R[b]["y_all"] for b in range(8)])
    hst = np.stack([R[b]["hst"] for b in range(8)], axis=2)
    kv = np.stack([R[b]["kv_all"] for b in range(8)], axis=1)
    lf = np.stack([R[b]["lf_all"] for b in range(8)], axis=1)
    mkv = np.stack([R[b]["mkv"] for b in range(8)], axis=1)
    c = np.ascontiguousarray
    y_prompt = c(y_all[:, :TP])
    y_sample = c(y_all[:, TP:])
    p_state = c(hst[:, 0])
    s_state = c(hst[:, 1])
    p_k = c(kv[:, :, :TP, 0:256]).reshape(DEPTH, 8, TP, 4, 64)
    p_v = c(kv[:, :, :TP, 256:512]).reshape(DEPTH, 8, TP, 4, 64)
    s_k = c(kv[:, :, TP:, 0:256]).reshape(DEPTH, 8, TS, 4, 64)
    s_v = c(kv[:, :, TP:, 256:512]).reshape(DEPTH, 8, TS, 4, 64)
    p_lf = c(lf[:, :, :TP])
    s_lf = c(lf[:, :, TP:])
    p_mk = c(mkv[:, :, :, 0:256]).reshape(DEPTH, 8, 256, 4, 64)
    p_mv = c(mkv[:, :, :, 256:512]).reshape(DEPTH, 8, 256, 4, 64)
    return (y_prompt, y_sample, p_state, p_k, p_v, p_lf, p_mk, p_mv, s_state, s_k, s_v, s_lf)
```
